# Optimizing a Trainium2 kernel written in Bass

```python
import math
import jax, jax.numpy as jnp
from jax import lax
import numpy as np

D_MODEL = 2048
BATCH = 8
SEQ = 2048
DEPTH = 1

MLA_HEADS = D_MODEL // 256
MLA_Q_RANK = D_MODEL // 4
MLA_KV_RANK = D_MODEL // 8
MLA_NOPE_DIM = 128
MLA_ROPE_DIM = 64
MLA_V_DIM = 128
MLA_SCALE = (MLA_NOPE_DIM + MLA_ROPE_DIM) ** -0.5
DIFF_HEADS = D_MODEL // 256
DIFF_QK_DIM = 64
DIFF_V_DIM = 2 * DIFF_QK_DIM
DIFF_SCALE = DIFF_QK_DIM ** -0.5
MIX_WIDTH = MLA_HEADS * MLA_V_DIM + DIFF_HEADS * DIFF_V_DIM
IN_Q_LAT = MLA_Q_RANK
IN_KV_LAT = MLA_KV_RANK
IN_K_ROPE = MLA_ROPE_DIM
IN_DIFF_Q = DIFF_HEADS * 2 * DIFF_QK_DIM
IN_DIFF_K = DIFF_HEADS * 2 * DIFF_QK_DIM
IN_DIFF_V = DIFF_HEADS * DIFF_V_DIM
IN_WIDTH = IN_Q_LAT + IN_KV_LAT + IN_K_ROPE + IN_DIFF_Q + IN_DIFF_K + IN_DIFF_V
SPLIT_POINTS = (IN_Q_LAT,
                IN_Q_LAT + IN_KV_LAT,
                IN_Q_LAT + IN_KV_LAT + IN_K_ROPE,
                IN_Q_LAT + IN_KV_LAT + IN_K_ROPE + IN_DIFF_Q,
                IN_Q_LAT + IN_KV_LAT + IN_K_ROPE + IN_DIFF_Q + IN_DIFF_K)
D_FF = 5632
CONV_WIDTH = 3
REL_BUCKETS = 32
REL_MAX_DIST = 128
ROPE_THETA = 10000.0
Q_BLOCK = 128
NORM_EPS = 1e-6
N_MOD = 6

kernel_name = "hybrid_mla_diffattn_convffn_encoder"


def rms_norm(x, w):
    xf = x.astype(jnp.float32)
    y = xf * lax.rsqrt(jnp.mean(xf * xf, axis=-1, keepdims=True) + NORM_EPS)
    return (y * w.astype(jnp.float32)).astype(x.dtype)


def modulate(h, shift, scale):
    return h * (1 + scale[:, None, :]) + shift[:, None, :]


def rope_tables(positions):
    inv = 1.0 / (ROPE_THETA ** (jnp.arange(0, MLA_ROPE_DIM, 2, dtype=jnp.float32) / MLA_ROPE_DIM))
    ang = positions.astype(jnp.float32)[..., None] * inv
    return jnp.cos(ang), jnp.sin(ang)


def apply_rope(t, cos, sin):
    half = t.shape[-1] // 2
    t1, t2 = t[..., :half], t[..., half:]
    cos = cos.astype(t.dtype)
    sin = sin.astype(t.dtype)
    return jnp.concatenate([t1 * cos - t2 * sin, t1 * sin + t2 * cos], axis=-1)


def t5_bucket(rel):
    half = REL_BUCKETS // 2
    max_exact = half // 2
    ret = jnp.where(rel > 0, half, 0)
    n = jnp.abs(rel)
    nf = jnp.maximum(n, 1).astype(jnp.float32)
    large = max_exact + (jnp.log(nf / max_exact) / math.log(REL_MAX_DIST / max_exact)
                         * (half - max_exact)).astype(jnp.int32)
    large = jnp.minimum(large, half - 1)
    return ret + jnp.where(n < max_exact, n, large)


def to_blocks(t):
    b, s = t.shape[0], t.shape[1]
    return jnp.moveaxis(t.reshape(b, s // Q_BLOCK, Q_BLOCK, *t.shape[2:]), 1, 0)


def from_blocks(t):
    nb, b = t.shape[0], t.shape[1]
    t = jnp.moveaxis(t, 0, 1)
    return t.reshape(b, nb * Q_BLOCK, *t.shape[3:])


def hybrid_mixer(h, positions, cos, sin, rel_bias, w_in, q_norm, w_uq, kv_norm, w_ukv,
                 lq1, lk1, lq2, lk2, subln, w_o, lambda_init):
    B, S, _ = h.shape
    proj = h @ w_in
    q_lat, kv_lat, k_rope, dq, dk, dv = jnp.split(proj, SPLIT_POINTS, axis=-1)

    q = (rms_norm(q_lat, q_norm) @ w_uq).reshape(B, S, MLA_HEADS, MLA_NOPE_DIM + MLA_ROPE_DIM)
    q_nope = q[..., :MLA_NOPE_DIM]
    q_rope = apply_rope(q[..., MLA_NOPE_DIM:], cos[:, :, None, :], sin[:, :, None, :])
    kv = (rms_norm(kv_lat, kv_norm) @ w_ukv).reshape(B, S, MLA_HEADS, MLA_NOPE_DIM + MLA_V_DIM)
    k_nope = kv[..., :MLA_NOPE_DIM]
    v_mla = kv[..., MLA_NOPE_DIM:]
    k_rope = apply_rope(k_rope, cos, sin)

    dq = dq.reshape(B, S, DIFF_HEADS, 2, DIFF_QK_DIM)
    dk = dk.reshape(B, S, DIFF_HEADS, 2, DIFF_QK_DIM)
    dv = dv.reshape(B, S, DIFF_HEADS, DIFF_V_DIM)
    lam = (jnp.exp(jnp.sum(lq1.astype(jnp.float32) * lk1.astype(jnp.float32)))
           - jnp.exp(jnp.sum(lq2.astype(jnp.float32) * lk2.astype(jnp.float32)))
           + lambda_init)
    bias_tab = rel_bias.astype(jnp.float32)

    def block(args):
        qn, qr, dqb, pq = args
        s = (jnp.einsum('bqhd,bkhd->bhqk', qn, k_nope)
             + jnp.einsum('bqhr,bkr->bhqk', qr, k_rope)) * MLA_SCALE
        p = jax.nn.softmax(s.astype(jnp.float32), axis=-1).astype(v_mla.dtype)
        o_mla = jnp.einsum('bhqk,bkhd->bqhd', p, v_mla)

        rel = positions[:, None, :] - pq[:, :, None]
        bias = jnp.take(bias_tab, t5_bucket(rel), axis=0)
        bias = jnp.transpose(bias, (0, 3, 1, 2))
        s1 = jnp.einsum('bqhd,bkhd->bhqk', dqb[..., 0, :], dk[..., 0, :]).astype(jnp.float32) * DIFF_SCALE + bias
        s2 = jnp.einsum('bqhd,bkhd->bhqk', dqb[..., 1, :], dk[..., 1, :]).astype(jnp.float32) * DIFF_SCALE + bias
        a = jax.nn.softmax(s1, axis=-1) - lam * jax.nn.softmax(s2, axis=-1)
        o_diff = jnp.einsum('bhqk,bkhd->bqhd', a.astype(dv.dtype), dv)
        return o_mla, o_diff

    o_mla, o_diff = lax.map(block, (to_blocks(q_nope), to_blocks(q_rope),
                                    to_blocks(dq), to_blocks(positions)))
    o_mla = from_blocks(o_mla)
    o_diff = from_blocks(o_diff)
    o_diff = rms_norm(o_diff, subln) * (1.0 - lambda_init)
    out = jnp.concatenate([o_mla.reshape(B, S, MLA_HEADS * MLA_V_DIM),
                           o_diff.reshape(B, S, DIFF_HEADS * DIFF_V_DIM)], axis=-1)
    return out @ w_o


def conv_geglu_ffn(h, w_up, conv_w, conv_b, w_down):
    u = h @ w_up
    pad = CONV_WIDTH // 2
    up = jnp.pad(u, ((0, 0), (pad, pad), (0, 0)))
    S = u.shape[1]
    u = conv_b + sum(up[:, t:t + S] * conv_w[t] for t in range(CONV_WIDTH))
    gate, val = jnp.split(u, 2, axis=-1)
    return (jax.nn.gelu(gate, approximate=True) * val) @ w_down


def setup_inputs(seed: int = 0) -> dict:
    key = jax.random.key(seed)
    ks = jax.random.split(key, 32)
    f32 = jnp.float32
    nrm = lambda k, shape, s: jax.random.normal(k, shape, f32) * s
    L = DEPTH
    x = jax.random.normal(ks[0], (BATCH, SEQ, D_MODEL), f32)
    c = jax.random.normal(ks[1], (BATCH, D_MODEL), f32)
    positions = (jnp.arange(SEQ, dtype=jnp.int32)[None, :]
                 + jax.random.randint(ks[2], (BATCH, 1), 0, 1024, dtype=jnp.int32))
    return {
        "x": x,
        "c": c,
        "positions": positions,
        "rel_bias": nrm(ks[3], (REL_BUCKETS, DIFF_HEADS), 0.5),
        "ada_w": nrm(ks[4], (L, D_MODEL, N_MOD * D_MODEL), 0.5 * D_MODEL ** -0.5),
        "ada_b": nrm(ks[5], (L, N_MOD * D_MODEL), 0.02),
        "attn_pre_norm": 1.0 + nrm(ks[6], (L, D_MODEL), 0.05),
        "attn_post_norm": 1.0 + nrm(ks[7], (L, D_MODEL), 0.05),
        "w_in": nrm(ks[8], (L, D_MODEL, IN_WIDTH), D_MODEL ** -0.5),
        "q_norm": 1.0 + nrm(ks[9], (L, MLA_Q_RANK), 0.05),
        "w_uq": nrm(ks[10], (L, MLA_Q_RANK, MLA_HEADS * (MLA_NOPE_DIM + MLA_ROPE_DIM)), MLA_Q_RANK ** -0.5),
        "kv_norm": 1.0 + nrm(ks[11], (L, MLA_KV_RANK), 0.05),
        "w_ukv": nrm(ks[12], (L, MLA_KV_RANK, MLA_HEADS * (MLA_NOPE_DIM + MLA_V_DIM)), MLA_KV_RANK ** -0.5),
        "lambda_q1": nrm(ks[13], (L, DIFF_QK_DIM), 0.1),
        "lambda_k1": nrm(ks[14], (L, DIFF_QK_DIM), 0.1),
        "lambda_q2": nrm(ks[15], (L, DIFF_QK_DIM), 0.1),
        "lambda_k2": nrm(ks[16], (L, DIFF_QK_DIM), 0.1),
        "diff_subln": 1.0 + nrm(ks[17], (L, DIFF_V_DIM), 0.05),
        "w_o": nrm(ks[18], (L, MIX_WIDTH, D_MODEL), MIX_WIDTH ** -0.5),
        "ffn_pre_norm": 1.0 + nrm(ks[19], (L, D_MODEL), 0.05),
        "ffn_post_norm": 1.0 + nrm(ks[20], (L, D_MODEL), 0.05),
        "w_up": nrm(ks[21], (L, D_MODEL, 2 * D_FF), D_MODEL ** -0.5),
        "conv_w": nrm(ks[22], (L, CONV_WIDTH, 2 * D_FF), CONV_WIDTH ** -0.5),
        "conv_b": nrm(ks[23], (L, 2 * D_FF), 0.02),
        "w_down": nrm(ks[24], (L, D_FF, D_MODEL), D_FF ** -0.5),
    }


def reference(x, c, positions, rel_bias, ada_w, ada_b, attn_pre_norm, attn_post_norm,
              w_in, q_norm, w_uq, kv_norm, w_ukv, lambda_q1, lambda_k1, lambda_q2, lambda_k2,
              diff_subln, w_o, ffn_pre_norm, ffn_post_norm, w_up, conv_w, conv_b, w_down):
    cos, sin = rope_tables(positions)
    c_act = jax.nn.silu(c)
    for l in range(DEPTH):
        lambda_init = 0.8 - 0.6 * math.exp(-0.3 * l)
        mod = c_act @ ada_w[l] + ada_b[l]
        sh_a, sc_a, g_a, sh_f, sc_f, g_f = jnp.split(mod, N_MOD, axis=-1)
        h = modulate(rms_norm(x, attn_pre_norm[l]), sh_a, sc_a)
        y = hybrid_mixer(h, positions, cos, sin, rel_bias, w_in[l], q_norm[l], w_uq[l],
                         kv_norm[l], w_ukv[l], lambda_q1[l], lambda_k1[l], lambda_q2[l],
                         lambda_k2[l], diff_subln[l], w_o[l], lambda_init)
        x = x + g_a[:, None, :] * rms_norm(y, attn_post_norm[l])
        h = modulate(rms_norm(x, ffn_pre_norm[l]), sh_f, sc_f)
        y = conv_geglu_ffn(h, w_up[l], conv_w[l], conv_b[l], w_down[l])
        x = x + g_f[:, None, :] * rms_norm(y, ffn_post_norm[l])
    return x
```

```python
import math
import os
from contextlib import ExitStack

import numpy as np
import concourse.bass as bass
import concourse.mybir as mybir
from concourse.bass_utils import run_bass_kernel_spmd

F32 = mybir.dt.float32
BF16 = mybir.dt.bfloat16
I32 = mybir.dt.int32
U8 = mybir.dt.uint8
AF = mybir.ActivationFunctionType
ALU = mybir.AluOpType
AX = mybir.AxisListType

D = 2048
S = 2048
NT = 16
DFF = 5632
NFB = 44
EPS = 1e-6
MLA_SCALE = 192 ** -0.5
DIFF_SCALE = 64 ** -0.5
LAMBDA_INIT = 0.8 - 0.6 * math.exp(0.0)
ROPE_THETA = 10000.0

BUCKETS = {0: (0, 0), 8: (-11, -8), 9: (-15, -12), 10: (-22, -16), 11: (-31, -23), 12: (-45, -32),
           13: (-63, -46), 14: (-90, -64), 15: (-100000, -91), 24: (8, 11), 25: (12, 15), 26: (16, 22),
           27: (23, 31), 28: (32, 45), 29: (46, 63), 30: (64, 90), 31: (91, 100000)}
for _b in range(1, 8):
    BUCKETS[_b] = (-_b, -_b)
    BUCKETS[16 + _b] = (_b, _b)


SHARED = [("relb", (32, 8)), ("relbT", (8, 32)), ("ident", (128, 128)), ("nvals", (8, 512)), ("selh", (8, 1024)), ("invf", (128, 32)),
          ("ada_r", (24, 128, 16 * 512)), ("ada_bT", (128, 96)), ("ada_b", (1, 6 * D)), ("apnT", (128, 16)), ("fpnT", (128, 16)),
          ("apost", (1, D)), ("fpost", (1, D)), ("w_fm", (22, 128, 2048)), ("w_kr", (128, 1024)), ("w_dv", (8, 128, 2048)),
          ("qnT", (128, 4)), ("kvnT", (128, 2)), ("w_uq", (128, 4 * 1536)), ("w_ukv", (128, 2 * 2048)), ("lamv", (1, 256)),
          ("sublnT", (128, 1)), ("w_o", (128, 16 * 2048)), ("w_up", (NFB, 128, 16 * 256)), ("cwT", (128, 88 * 3)), ("cbT", (128, 88)),
          ("w_dn", (16, 128, NFB * 128))]
CW = 2048


def _layout():
    off = [0, 0]
    offs = {}
    g = 0
    for name, shp in SHARED:
        if name == "lamv":
            g = 1
        offs[name] = (g, off[g])
        n = int(np.prod(shp))
        off[g] += (n + 63) // 64 * 64
    rows = [(o + 8 * CW - 1) // (8 * CW) for o in off]
    return offs, rows


class Sched:
    COMPUTE = ("pe", "act", "dve")
    QUEUES = ("sp", "pool")

    def __init__(self, nc, es):
        self.nc = nc
        self.es = es
        self.streams = {e: [] for e in self.COMPUTE + self.QUEUES}
        self.semobj = {e: es.enter_context(nc.semaphore("s_" + e)) for e in self.COMPUTE}
        self.cnt = {e: 0 for e in self.COMPUTE}
        self.dcnt = {}
        self.waited = {e: {} for e in self.COMPUTE + self.QUEUES}
        self.lastw = {}
        self.readers = {}

    def _dsem(self, name):
        k = "d:" + name
        if k not in self.semobj:
            self.semobj[k] = self.es.enter_context(self.nc.semaphore("q_" + name))
            self.dcnt[k] = 0
        return k

    def _wait(self, eng, tok):
        k, v = tok
        if self.waited[eng].get(k, 0) < v:
            self.waited[eng][k] = v
            self.streams[eng].append(("w", k, v))

    def op(self, eng, fn, reads=(), writes=(), signal=True, dma=None):
        deps = []
        for r in reads:
            if r in self.lastw:
                deps.append(self.lastw[r])
        for w in writes:
            if w in self.lastw:
                deps.append(self.lastw[w])
            deps.extend(self.readers.get(w, ()))
        for t in deps:
            if eng == "pe" and t[0] == "pe":
                continue
            self._wait(eng, t)
        if dma is not None:
            k = self._dsem(dma)
            self.dcnt[k] += 16
            tok = (k, self.dcnt[k])
            self.streams[eng].append(("i", fn, k, 16))
        else:
            if signal:
                self.cnt[eng] += 1
                tok = (eng, self.cnt[eng])
                self.streams[eng].append(("i", fn, eng, 1))
            else:
                tok = (eng, self.cnt[eng] + 1)
                self.streams[eng].append(("i", fn, None, 0))
        for w in writes:
            self.lastw[w] = tok
            self.readers[w] = []
        for r in reads:
            self.readers.setdefault(r, []).append(tok)
        return tok

    def barrier(self):
        toks = [(e, self.cnt[e]) for e in self.COMPUTE if self.cnt[e] > 0]
        toks += [(k, v) for k, v in self.dcnt.items() if v > 0]
        for e in self.COMPUTE + self.QUEUES:
            for t in toks:
                if t[0] != e:
                    self._wait(e, t)
        self.lastw.clear()
        self.readers.clear()

    def finish(self):
        for k, v in self.dcnt.items():
            if v > 0:
                self._wait("sp", (k, v))

    def simulate(self):
        val = {k: 0 for k in self.semobj}
        pc = {e: 0 for e in self.streams}
        progress = True
        while progress:
            progress = False
            for e, st in self.streams.items():
                while pc[e] < len(st):
                    a = st[pc[e]]
                    if a[0] == "w":
                        if val[a[1]] < a[2]:
                            break
                    elif a[2] is not None:
                        val[a[2]] += a[3]
                    pc[e] += 1
                    progress = True
        stuck = {e: (pc[e], len(st), st[pc[e]][:3] if pc[e] < len(st) else None) for e, st in self.streams.items() if pc[e] < len(st)}
        return stuck, val

    def replay(self, block):
        def mk(stream):
            def body(e):
                for a in stream:
                    if a[0] == "w":
                        e.wait_ge(self.semobj[a[1]], a[2])
                    else:
                        ins = a[1](e)
                        if a[2] is not None:
                            ins.then_inc(self.semobj[a[2]], a[3])
            return body
        block.tensor(mk(self.streams["pe"]))
        block.scalar(mk(self.streams["act"]))
        block.vector(mk(self.streams["dve"]))
        block.sync(mk(self.streams["sp"]))
        block.gpsimd(mk(self.streams["pool"]))


def build(nb=1, stop_after=99, dbg=False):
    nc = bass.Bass("TRN2", target_bir_lowering=False)

    skip = set()
    if stop_after in (12, 13, 14, 15, 16):
        skip |= {"w_up", "w_dn", "fpost", "w_o", "apost", "ada_b"}
    elif stop_after < 4:
        skip |= {"w_up", "w_dn", "fpost"}
    if stop_after < 3:
        skip |= {"w_o", "apost", "ada_b"}
    if stop_after < 2:
        skip |= {"w_uq", "w_ukv"}

    def din(name, shape, dt=F32):
        if name in skip:
            return None
        return nc.dram_tensor(name, list(shape), dt, kind="ExternalInput").ap()

    x = din("x", [nb * S, D])
    cT = din("cT", [128, nb * 16])
    pos = din("pos", [128, nb * 16], I32)
    shp = dict(SHARED)
    relb = din("relb", shp["relb"]); relbTd = din("relbT", shp["relbT"]); identd = din("ident", shp["ident"]); nvals = din("nvals", shp["nvals"])
    selh = din("selh", shp["selh"]); invf = din("invf", shp["invf"]); ada_r = din("ada_r", shp["ada_r"]); ada_bT = din("ada_bT", shp["ada_bT"])
    ada_b = din("ada_b", shp["ada_b"]); apnT = din("apnT", shp["apnT"]); fpnT = din("fpnT", shp["fpnT"]); apost = din("apost", shp["apost"])
    fpost = din("fpost", shp["fpost"]); w_fm = din("w_fm", shp["w_fm"]); w_kr = din("w_kr", shp["w_kr"]); w_dv = din("w_dv", shp["w_dv"])
    qnT = din("qnT", shp["qnT"]); kvnT = din("kvnT", shp["kvnT"]); w_uq = din("w_uq", shp["w_uq"]); w_ukv = din("w_ukv", shp["w_ukv"])
    lamv = din("lamv", shp["lamv"]); sublnT = din("sublnT", shp["sublnT"]); w_o = din("w_o", shp["w_o"]); w_up = din("w_up", shp["w_up"])
    cwT = din("cwT", shp["cwT"]); cbT = din("cbT", shp["cbT"]); w_dn = din("w_dn", shp["w_dn"])
    out = nc.dram_tensor("out", [nb * S, D], F32, kind="ExternalOutput").ap()
    x1s = nc.dram_tensor("x1s", [S, D], F32, kind="ExternalOutput" if dbg else "Internal").ap()
    bsc = nc.dram_tensor("bsc", [8, 128 * 512], F32, kind="Internal").ap()
    dbg_out = nc.dram_tensor("dbg", [128, 4096], F32, kind="ExternalOutput").ap() if dbg else None

    with ExitStack() as es:
        E = es.enter_context
        ARENA = 202752
        arena = E(nc.sbuf_tensor("arena", [128, ARENA], U8))

        def view(off, dt, n, pat=None, **kw):
            sz = {F32: 4, BF16: 2, I32: 4}[dt]
            v = arena[:, off:off + n * sz].bitcast(dt)
            if pat:
                v = v.rearrange(pat, **kw)
            return v

        def sb(name, shape, dt=F32):
            return E(nc.sbuf_tensor(name, list(shape), dt))

        ident = sb("identb", [128, 128], BF16)
        onesb = sb("onesb", [128, 128], BF16)
        sel1 = sb("sel1", [128, 128], BF16)
        sel2 = sb("sel2", [128, 128], BF16)
        cact = sb("cact", [128, 16], BF16)
        msc = sb("msc", [128, 64])
        adab = sb("adab", [128, 96])
        apn = sb("apn", [128, 16]); fpn = sb("fpn", [128, 16])
        a_pre = sb("a_pre", [128, 16]); a_ffn = sb("a_ffn", [128, 16])
        qnw = sb("qnw", [128, 4]); kvnw = sb("kvnw", [128, 2])
        posi = sb("posi", [128, 16], I32)
        posf = sb("posf", [128, 16])
        invt = sb("invt", [128, 32])
        cost = sb("cost", [128, 16 * 32]); sint = sb("sint", [128, 16 * 32])
        tabbc = sb("tabbc", [128, 16])
        lamb = sb("lamb", [128, 256])
        lamt = sb("lamt", [128, 8])
        subw = sb("subw", [128, 1])
        cw = sb("cw", [128, 88 * 3]); cb = sb("cb", [128, 88])
        stat = sb("stat", [128, 64])
        halo = sb("halo", [128, 16 * 2], BF16)
        st2 = sb("st2", [128, 64])
        psb = [E(nc.psum_tensor("pp%d" % i, [128, 1024], F32)) for i in range(4)]

        def bank(b, n=512, off=0):
            return psb[b // 2][:, (b % 2) * 512 + off:(b % 2) * 512 + off + n]

        def bankbf(b):
            return psb[b // 2][:, (b % 2) * 512:(b % 2) * 512 + 512].bitcast(BF16)

        sc = Sched(nc, es)
        rr = [0]

        def evac_eng():
            rr[0] ^= 1
            return "act" if rr[0] else "dve"

        def copy(eng, out_, in_, reads, writes):
            if eng == "act":
                sc.op("act", lambda e: e.activation(out=out_, in_=in_, func=AF.Copy), reads, writes)
            else:
                sc.op("dve", lambda e: e.tensor_copy(out=out_, in_=in_), reads, writes)

        def dma(q, out_, in_, name, reads=(), writes=()):
            sc.op(q, lambda e: e.dma_start(out=out_, in_=in_), reads, writes, dma=name)

        OFF_A = 0
        OFF_P = 65536
        hT = view(OFF_A, BF16, 16 * 2048, "p (k t) -> p k t", k=16)
        dqT = view(OFF_P, BF16, 8 * 2048, "p (h t) -> p h t", h=8)
        dkT = view(OFF_P + 32768, BF16, 8 * 2048, "p (h t) -> p h t", h=8)
        dvaug = view(OFF_P + 65536, BF16, 16 * 8 * 132, "p (t h c) -> p t h c", t=16, h=8)
        OFF_L = OFF_P + 65536 + 33792
        qlatT = view(OFF_L, BF16, 4 * 2048, "p (k t) -> p k t", k=4)
        kvlatT = view(OFF_L + 16384, BF16, 2 * 2048, "p (k t) -> p k t", k=2)
        kropeT = view(OFF_L + 24576, BF16, 2048)
        OFF_S = OFF_L + 28672
        wst = [view(OFF_S + i * 4096, BF16, 16 * 128, "p (k c) -> p k c", k=16) for i in range(2)]
        xt = [view(OFF_P + i * 8192, F32, 2048) for i in range(2)]
        xn = [view(OFF_P + 16384 + i * 4096, BF16, 2048) for i in range(2)]
        adast = [view(OFF_P + 24576 + i * 16384, BF16, 16 * 512, "p (k c) -> p k c", k=16) for i in range(2)]
        zt = view(OFF_P + 57344, F32, 512)
        ztm = view(OFF_P + 59392, F32, 512)
        ztm2 = view(OFF_P + 61440, F32, 512)
        nv = view(OFF_P + 63488, F32, 512)
        selhs = view(OFF_P + 65536, F32, 1024)
        zrep = view(OFF_P + 69632, F32, 512)
        ang = view(OFF_P + 71680, F32, 512)
        angq = view(OFF_P + 73728, F32, 512)
        angi = view(OFF_P + 75776, I32, 512)
        idf = view(OFF_P + 77824, F32, 128)

        for lb in range(nb):
            if lb == 0:
                dma("sp", idf, identd, "c0", writes=["idf"])
                sc.op("dve", lambda e: e.tensor_copy(out=ident[:], in_=idf), ["idf"], ["ident"])
                sc.op("dve", lambda e: e.memset(onesb[:], 1.0), [], ["onesb"])
                sc.op("dve", lambda e: e.memset(sel1[:], 0.0), [], ["sel1"])
                sc.op("dve", lambda e: e.memset(sel1[0:64, :], 1.0), [], ["sel1"])
                sc.op("dve", lambda e: e.memset(sel2[:], 0.0), [], ["sel2"])
                sc.op("dve", lambda e: e.memset(sel2[64:128, :], 1.0), [], ["sel2"])
                sc.op("dve", lambda e: e.memset(halo[:], 0.0), [], ["halo"])
                small = [(adab, ada_bT), (apn, apnT), (fpn, fpnT), (qnw, qnT), (kvnw, kvnT), (invt, invf),
                         (subw, sublnT), (cw, cwT), (cb, cbT)]
                for i, (d_, s_) in enumerate(small):
                    dma("sp", d_[:], s_, "c%d" % (1 + i), writes=[d_.name])
            dma("sp", posi[:], pos[:, lb * 16:(lb + 1) * 16], "c10", writes=["posi"])
            if lb == 0:
                dma("sp", tabbc[:, 0:8], relb[15:16, :].partition_broadcast(128), "c11", writes=["tabbc0"])
                dma("sp", tabbc[:, 8:16], relb[31:32, :].partition_broadcast(128), "c12", writes=["tabbc1"])
                dma("sp", lamb[:], lamv.partition_broadcast(128), "c13", writes=["lamb"])
            ctf = view(OFF_P + 78848, F32, 16)
            dma("sp", ctf, cT[:, lb * 16:(lb + 1) * 16], "c14", writes=["ctf"])
            sc.op("act", lambda e: e.activation(out=cact[:], in_=ctf, func=AF.Silu), ["ctf"], ["cact"])

            mps = bank(0, 64)
            k_ada = [0]

            def ada_piece(ci, ch, piece):
                k_ = k_ada[0]
                st = adast[k_ % 2]
                dma("pool", st, ada_r[ch * 4 + piece].rearrange("p (k c) -> p k c", k=16), "ada%d" % (k_ % 2), writes=["adast%d" % (k_ % 2)])
                for fcl in range(4):
                    col = ci * 16 + piece * 4 + fcl
                    for kc in range(16):
                        sc.op("pe", lambda e, st=st, fcl=fcl, kc=kc, col=col: e.matmul(
                            mps[:, col:col + 1], lhsT=st[:, kc, fcl * 128:(fcl + 1) * 128], rhs=cact[:, kc:kc + 1],
                            start=(kc == 0), stop=(kc == 15)),
                            ["adast%d" % (k_ % 2), "cact"], ["mps"], signal=(kc == 15))
                k_ada[0] += 1

            SQD = math.sqrt(D)

            def ada_finish(pairs, dst, wn, ci_scale):
                for ci, ch in pairs:
                    sc.op("dve", lambda e, ci=ci, ch=ch: e.tensor_tensor(out=msc[:, ci * 16:(ci + 1) * 16], in0=mps[:, ci * 16:(ci + 1) * 16],
                                                                        in1=adab[:, ch * 16:(ch + 1) * 16], op=ALU.add),
                          ["mps", "adab"], [("msc", ci)])
                sc.op("dve", lambda e: e.scalar_tensor_tensor(
                    out=dst[:], in0=msc[:, ci_scale * 16:(ci_scale + 1) * 16], scalar=1.0, in1=wn[:], op0=ALU.add, op1=ALU.mult),
                    [("msc", ci_scale), wn.name], [dst.name])
                sc.op("dve", lambda e: e.tensor_scalar(out=dst[:], in0=dst[:], scalar1=SQD, scalar2=None, op0=ALU.mult),
                      [dst.name], [dst.name])

            for ci, ch in ((0, 0), (1, 1)):
                for piece in range(4):
                    ada_piece(ci, ch, piece)
            ada_finish(((0, 0), (1, 1)), a_pre, apn, 1)
            ada_later = [(ci, ch, piece) for ci, ch in ((2, 3), (3, 4)) for piece in range(4)]

            if lb == 0:
                sc.op("dve", lambda e: e.tensor_tensor(out=lamb[:, 0:64], in0=lamb[:, 0:64], in1=lamb[:, 64:128], op=ALU.mult), ["lamb"], ["lamb"])
                sc.op("dve", lambda e: e.tensor_tensor(out=lamb[:, 128:192], in0=lamb[:, 128:192], in1=lamb[:, 192:256], op=ALU.mult), ["lamb"], ["lamb"])
                sc.op("dve", lambda e: e.tensor_reduce(out=lamt[:, 0:1], in_=lamb[:, 0:64], axis=AX.X, op=ALU.add), ["lamb"], ["lamt"])
                sc.op("dve", lambda e: e.tensor_reduce(out=lamt[:, 1:2], in_=lamb[:, 128:192], axis=AX.X, op=ALU.add), ["lamb"], ["lamt"])
                sc.op("act", lambda e: e.activation(out=lamt[:, 2:4], in_=lamt[:, 0:2], func=AF.Exp), ["lamt"], ["lamt"])
                sc.op("dve", lambda e: e.scalar_tensor_tensor(out=lamt[:, 4:5], in0=lamt[:, 3:4], scalar=-LAMBDA_INIT, in1=lamt[:, 2:3],
                                                              op0=ALU.add, op1=ALU.subtract), ["lamt"], ["lamt"])
                sc.op("dve", lambda e: e.tensor_scalar(out=subw[:], in0=subw[:], scalar1=1.0 - LAMBDA_INIT, scalar2=None, op0=ALU.mult),
                      ["subw"], ["subw"])

            sc.op("dve", lambda e: e.tensor_copy(out=posf[:], in_=posi[:]), ["posi"], ["posf"])
            for tt in range(16):
                sc.op("dve", lambda e, tt=tt: e.tensor_scalar(out=ang[:, tt * 32:(tt + 1) * 32], in0=invt[:], scalar1=posf[:, tt:tt + 1],
                                                             scalar2=None, op0=ALU.mult), ["posf", "invt"], ["ang"])
            C1 = 6.28125
            C2 = 2 * math.pi - C1
            for (dst, shift) in ((sint, 0.0), (cost, math.pi / 2)):
                sc.op("dve", lambda e, shift=shift: e.tensor_scalar(out=angq, in0=ang, scalar1=shift, scalar2=1.0 / (2 * math.pi),
                                                                   op0=ALU.add, op1=ALU.mult), ["ang"], ["angq"])
                sc.op("dve", lambda e: e.tensor_copy(out=angi, in_=angq), ["angq"], ["angi"])
                sc.op("dve", lambda e: e.tensor_copy(out=angq, in_=angi), ["angi"], ["angq"])
                sc.op("dve", lambda e, shift=shift, dst=dst: e.tensor_scalar(out=dst[:], in0=ang, scalar1=shift, scalar2=None, op0=ALU.add),
                      ["ang"], [dst.name])
                sc.op("dve", lambda e, dst=dst: e.scalar_tensor_tensor(out=dst[:], in0=angq, scalar=-C1, in1=dst[:], op0=ALU.mult, op1=ALU.add),
                      ["angq", dst.name], [dst.name])
                sc.op("dve", lambda e, dst=dst: e.scalar_tensor_tensor(out=dst[:], in0=angq, scalar=-C2, in1=dst[:], op0=ALU.mult, op1=ALU.add),
                      ["angq", dst.name], [dst.name])
                sc.op("dve", lambda e, dst=dst: e.tensor_scalar(out=angq, in0=dst[:], scalar1=math.pi, scalar2=-2 * math.pi, op0=ALU.is_gt, op1=ALU.mult),
                      [dst.name], ["angq"])
                sc.op("dve", lambda e, dst=dst: e.tensor_tensor(out=dst[:], in0=dst[:], in1=angq, op=ALU.add), ["angq", dst.name], [dst.name])
                sc.op("dve", lambda e, dst=dst: e.tensor_scalar(out=angq, in0=dst[:], scalar1=-math.pi, scalar2=2 * math.pi, op0=ALU.is_lt, op1=ALU.mult),
                      [dst.name], ["angq"])
                sc.op("dve", lambda e, dst=dst: e.tensor_tensor(out=dst[:], in0=dst[:], in1=angq, op=ALU.add), ["angq", dst.name], [dst.name])
                sc.op("dve", lambda e, dst=dst: e.tensor_scalar(out=dst[:], in0=dst[:], scalar1=math.pi, scalar2=-math.pi, op0=ALU.min, op1=ALU.max),
                      [dst.name], [dst.name])
                sc.op("act", lambda e, dst=dst: e.activation(out=dst[:], in_=dst[:], func=AF.Sin), [dst.name], [dst.name])

            if lb == 0:
                dma("sp", nv[0:8, :], nvals, "c15", writes=["nv"])
                dma("sp", selhs[0:8, :], selh, "c16", writes=["selhs"])
                relbT = view(OFF_P + 79872, F32, 32)
                dma("sp", relbT[0:8, :], relbTd, "c17", writes=["relbT"])
                sc.op("dve", lambda e: e.memset(zt[0:8, :], 0.0), [], ["zt"])
                for b_, (lo, hi) in sorted(BUCKETS.items()):
                    sc.op("dve", lambda e, lo=lo: e.tensor_scalar(out=ztm[0:8, :], in0=nv[0:8, :], scalar1=float(lo) - 0.5, scalar2=None, op0=ALU.is_gt),
                          ["nv"], ["ztm"])
                    sc.op("dve", lambda e, hi=hi: e.scalar_tensor_tensor(out=ztm2[0:8, :], in0=nv[0:8, :], scalar=float(hi) + 0.5, in1=ztm[0:8, :],
                                                                        op0=ALU.is_lt, op1=ALU.mult), ["nv", "ztm"], ["ztm2"])
                    sc.op("dve", lambda e, b_=b_: e.scalar_tensor_tensor(out=zt[0:8, :], in0=ztm2[0:8, :], scalar=relbT[0:8, b_:b_ + 1], in1=zt[0:8, :],
                                                                        op0=ALU.mult, op1=ALU.add), ["ztm2", "relbT", "zt"], ["zt"])
                for h in range(8):
                    sc.op("pe", lambda e, h=h: e.matmul(bank(1), lhsT=selhs[0:8, h * 128:(h + 1) * 128], rhs=zt[0:8, :], start=True, stop=True),
                          ["selhs", "zt"], ["b1"])
                    sc.op("dve", lambda e: e.tensor_copy(out=zrep, in_=bank(1)), ["b1"], ["zrep"])
                    dma("sp", bsc[h].rearrange("(r c) -> r c", c=512), zrep, "bsc", reads=["zrep"], writes=["bsc%d" % h])

            xv = x[lb * S:(lb + 1) * S, :].rearrange("(t p) d -> t p d", p=128)
            for tt in range(NT):
                b_ = tt % 2
                dma("sp", xt[b_], xv[tt], "xt%d" % b_, writes=["xt%d" % b_])
                sc.op("act", lambda e, b_=b_, tt=tt: e.activation(out=xn[b_], in_=xt[b_], func=AF.Square, accum_out=stat[:, tt:tt + 1]),
                      ["xt%d" % b_], ["xn%d" % b_, "stat_a%d" % tt])
                sc.op("dve", lambda e, tt=tt: e.tensor_scalar(out=stat[:, 16 + tt:17 + tt], in0=stat[:, tt:tt + 1], scalar1=D * EPS, scalar2=None, op0=ALU.add),
                      ["stat_a%d" % tt], ["stat_b%d" % tt])
                sc.op("act", lambda e, tt=tt: e.activation(out=stat[:, 16 + tt:17 + tt], in_=stat[:, 16 + tt:17 + tt], func=AF.Ln),
                      ["stat_b%d" % tt], ["stat_b%d" % tt])
                sc.op("act", lambda e, tt=tt: e.activation(out=stat[:, 16 + tt:17 + tt], in_=stat[:, 16 + tt:17 + tt], func=AF.Exp, scale=-0.5),
                      ["stat_b%d" % tt], ["stat_b%d" % tt])
                sc.op("dve", lambda e, b_=b_, tt=tt: e.tensor_scalar(out=xn[b_], in0=xt[b_], scalar1=stat[:, 16 + tt:17 + tt], scalar2=None, op0=ALU.mult),
                      ["xt%d" % b_, "stat_b%d" % tt], ["xn%d" % b_])
                for g in range(2):
                    pb = 2 + g
                    pv = bankbf(pb)
                    for j in range(8):
                        kc = g * 8 + j
                        sc.op("pe", lambda e, pv=pv, j=j, kc=kc, b_=b_: e.transpose(pv[:, j * 128:(j + 1) * 128], xn[b_][:, kc * 128:(kc + 1) * 128], ident[:]),
                              ["xn%d" % b_, "ident"], ["b%d" % pb], signal=(j == 7))
                    for j in range(8):
                        kc = g * 8 + j
                        if evac_eng() == "act":
                            sc.op("act", lambda e, pv=pv, j=j, kc=kc, tt=tt: e.activation(
                                out=hT[:, kc, tt * 128:(tt + 1) * 128], in_=pv[:, j * 128:(j + 1) * 128], func=AF.Identity,
                                scale=a_pre[:, kc:kc + 1], bias=msc[:, kc:kc + 1]), ["b%d" % pb, "a_pre", ("msc", 0)], [("hT", tt)])
                        else:
                            sc.op("dve", lambda e, pv=pv, j=j, kc=kc, tt=tt: e.tensor_scalar(
                                out=hT[:, kc, tt * 128:(tt + 1) * 128], in0=pv[:, j * 128:(j + 1) * 128],
                                scalar1=a_pre[:, kc:kc + 1], scalar2=msc[:, kc:kc + 1], op0=ALU.mult, op1=ALU.add),
                                ["b%d" % pb, "a_pre", ("msc", 0)], [("hT", tt)])
                if tt % 2 == 1 and ada_later:
                    ada_piece(*ada_later.pop(0))
            while ada_later:
                ada_piece(*ada_later.pop(0))
            ada_finish(((2, 3), (3, 4)), a_ffn, fpn, 3)
            sc.barrier()

            def dbg_dump(ap_f32_src, ncols, reads):
                t_ = view(OFF_A, F32, 4096) if False else None
                dma("sp", dbg_out[:, 0:ncols], ap_f32_src, "dbg", reads=reads)

            if stop_after == 0:
                tmpf = view(OFF_P, F32, 4096)
                sc.op("dve", lambda e: e.tensor_copy(out=tmpf[:, 0:2048], in_=hT[:, 0, :]), [], ["tmpf"])
                sc.op("dve", lambda e: e.tensor_copy(out=tmpf[:, 2048:4096], in_=hT[:, 15, :]), [], ["tmpf"])
                dbg_dump(tmpf, 4096, ["tmpf"])
                sc.finish()
                sc.replay(E(nc.Block()))
                return nc

            sc.op("dve", lambda e: e.memset(dvaug[:, :, :, 128:132], 1.0), [], ["dvaug"])
            hkeys = lambda tc: [("hT", 4 * tc + i) for i in range(4)]
            nw = 0
            for j in range(22):
                wb = wst[nw % 2]; wk = "wst%d" % (nw % 2); nw += 1
                dma("pool", wb, w_fm[j].rearrange("p (k c) -> p k c", k=16), wk, writes=[wk])
                for tc in range(4):
                    pb = 2 * (tc % 2) + 0 if False else (tc % 4)
                    pb = 4 + (tc % 4)
                    for kc in range(16):
                        sc.op("pe", lambda e, wb=wb, kc=kc, tc=tc, pb=pb: e.matmul(bank(pb), lhsT=wb[:, kc, :], rhs=hT[:, kc, tc * 512:(tc + 1) * 512],
                                                                                start=(kc == 0), stop=(kc == 15)),
                              [wk] + hkeys(tc), ["b%d" % pb], signal=(kc == 15))
                    if j < 4:
                        dst, key = qlatT[:, j, tc * 512:(tc + 1) * 512], "qlatT"
                    elif j < 6:
                        dst, key = kvlatT[:, j - 4, tc * 512:(tc + 1) * 512], "kvlatT"
                    elif j < 14:
                        dst, key = dqT[:, j - 6, tc * 512:(tc + 1) * 512], ("dqT", j - 6, tc)
                    else:
                        dst, key = dkT[:, j - 14, tc * 512:(tc + 1) * 512], ("dkT", j - 14)
                    copy(evac_eng(), dst, bank(pb), ["b%d" % pb], [key])
            for h in range(8):
                wb = wst[nw % 2]; wk = "wst%d" % (nw % 2); nw += 1
                dma("pool", wb, w_dv[h].rearrange("p (k c) -> p k c", k=16), wk, writes=[wk])
                for tg in range(4):
                    pb = 4 + (tg % 4)
                    for i in range(4):
                        tt = tg * 4 + i
                        for kc in range(16):
                            sc.op("pe", lambda e, wb=wb, kc=kc, tt=tt, pb=pb, i=i: e.matmul(
                                bank(pb, 128, i * 128), lhsT=hT[:, kc, tt * 128:(tt + 1) * 128], rhs=wb[:, kc, :],
                                start=(kc == 0), stop=(kc == 15)), [wk, ("hT", tt)], ["b%d" % pb], signal=(kc == 15))
                    copy(evac_eng(), dvaug[:, tg * 4:(tg + 1) * 4, h, 0:128], bank(pb).rearrange("p (t c) -> p t c", t=4),
                         ["b%d" % pb], ["dvaug"])
            wkr = view(OFF_S, BF16, 16 * 64, "p (k c) -> p k c", k=16)
            dma("pool", wkr, w_kr.rearrange("p (k c) -> p k c", k=16), "wst0", writes=["wst0"])
            krt = view(OFF_S + 4096, BF16, 128)
            rtmp = view(OFF_S + 4096 + 256, F32, 128)
            for tt in range(NT):
                pb = 4 + (tt % 2)
                kp = bank(pb, 64)
                for kc in range(16):
                    sc.op("pe", lambda e, kc=kc, tt=tt, kp=kp: e.matmul(kp, lhsT=hT[:, kc, tt * 128:(tt + 1) * 128], rhs=wkr[:, kc, :],
                                                                      start=(kc == 0), stop=(kc == 15)),
                          ["wst0", ("hT", tt)], ["b%d" % pb], signal=(kc == 15))
                cs = cost[:, tt * 32:(tt + 1) * 32]; sn = sint[:, tt * 32:(tt + 1) * 32]
                sc.op("dve", lambda e, kp=kp, cs=cs: e.tensor_tensor(out=rtmp[:, 0:32], in0=kp[:, 0:32], in1=cs, op=ALU.mult), ["b%d" % pb, "cost"], ["wst1"])
                sc.op("dve", lambda e, kp=kp, sn=sn: e.tensor_tensor(out=rtmp[:, 32:64], in0=kp[:, 32:64], in1=sn, op=ALU.mult), ["b%d" % pb, "sint"], ["wst1"])
                sc.op("dve", lambda e, kp=kp, sn=sn: e.tensor_tensor(out=rtmp[:, 64:96], in0=kp[:, 0:32], in1=sn, op=ALU.mult), ["b%d" % pb, "sint"], ["wst1"])
                sc.op("dve", lambda e, kp=kp, cs=cs: e.tensor_tensor(out=rtmp[:, 96:128], in0=kp[:, 32:64], in1=cs, op=ALU.mult), ["b%d" % pb, "cost"], ["wst1"])
                sc.op("dve", lambda e: e.tensor_tensor(out=krt[:, 0:32], in0=rtmp[:, 0:32], in1=rtmp[:, 32:64], op=ALU.subtract), ["wst1"], ["wst1"])
                sc.op("dve", lambda e: e.tensor_tensor(out=krt[:, 32:64], in0=rtmp[:, 64:96], in1=rtmp[:, 96:128], op=ALU.add), ["wst1"], ["wst1"])
                sc.op("dve", lambda e: e.tensor_copy(out=krt[:, 64:128], in_=krt[:, 0:64]), ["wst1"], ["wst1"])
                tp = bankbf(6 + (tt % 2))
                sc.op("pe", lambda e, tp=tp: e.transpose(tp[:, 0:128], krt, ident[:]), ["wst1", "ident"], ["b%d" % (6 + tt % 2)])
                copy(evac_eng(), kropeT[:, tt * 128:(tt + 1) * 128], tp[:, 0:128], ["b%d" % (6 + tt % 2)], ["kropeT"])
            sc.barrier()
            if stop_after == 1:
                break

            o = OFF_A
            PT = [view(o + i * 1024, BF16, 512) for i in range(8)]; o += 8192
            wuq = view(o, BF16, 4 * 1536, "p (k c) -> p k c", k=4); o += 12288
            wukv = view(o, BF16, 2 * 2048, "p (k c) -> p k c", k=2); o += 8192
            qropeT = view(o, BF16, 4 * 2048, "p (r t) -> p r t", r=4); o += 16384
            rchunk = view(o, F32, 512)
            sq4 = view(o, F32, 512); o += 2048
            sqtmp = view(OFF_S, BF16, 2048)
            bth = [view(OFF_S + 4096 + i * 1536, F32, 384, "p (d q) -> p d q", d=3) for i in range(2)]
            qnopeT = view(o, BF16, 2048); o += 4096
            knopeT = view(o, BF16, 2048); o += 4096
            vaug = view(o, BF16, 16 * 132, "p (t c) -> p t c", t=16); o += 4224
            qrtok = view(o, BF16, 512)
            odb4 = view(o, BF16, 512, "p (j c) -> p j c", j=4); o += 1024
            rt4 = view(o, F32, 128); o += 512
            odt4 = view(o, F32, 512, "p (j c) -> p j c", j=4); o += 2048
            bhi = view(o, BF16, 384, "p (d q) -> p d q", d=3); o += 768
            blo = view(o, BF16, 384, "p (d q) -> p d q", d=3); o += 768
            assert o <= 65536, o

            dma("pool", wuq, w_uq.rearrange("p (k c) -> p k c", k=4), "wuq", writes=["wuq"])
            dma("pool", wukv, w_ukv.rearrange("p (k c) -> p k c", k=2), "wukv", writes=["wukv"])
            for kc in range(4):
                sc.op("dve", lambda e, kc=kc: e.tensor_scalar(out=wuq[:, kc, :], in0=wuq[:, kc, :], scalar1=qnw[:, kc:kc + 1], scalar2=None, op0=ALU.mult),
                      ["wuq", "qnw"], ["wuq"])
            for kc in range(2):
                sc.op("dve", lambda e, kc=kc: e.tensor_scalar(out=wukv[:, kc, :], in0=wukv[:, kc, :], scalar1=kvnw[:, kc:kc + 1], scalar2=None, op0=ALU.mult),
                      ["wukv", "kvnw"], ["wukv"])

            for (lat, nk, key) in ((qlatT, 4, "qlatT"), (kvlatT, 2, "kvlatT")):
                n_lat = nk * 128
                for tc in range(4):
                    for kc in range(nk):
                        sc.op("act", lambda e, lat=lat, kc=kc, tc=tc: e.activation(out=sqtmp[:, 0:512], in_=lat[:, kc, tc * 512:(tc + 1) * 512], func=AF.Square),
                              [key], ["sqtmp"])
                        sc.op("pe", lambda e, kc=kc, nk=nk: e.matmul(bank(0), lhsT=onesb[:], rhs=sqtmp[:, 0:512], start=(kc == 0), stop=(kc == nk - 1)),
                              ["sqtmp", "onesb"], ["b0"], signal=True)
                    sc.op("dve", lambda e, n_lat=n_lat: e.tensor_scalar(out=rchunk, in0=bank(0), scalar1=1.0 / n_lat, scalar2=EPS, op0=ALU.mult, op1=ALU.add),
                          ["b0"], ["rchunk"])
                    sc.op("act", lambda e: e.activation(out=rchunk, in_=rchunk, func=AF.Ln), ["rchunk"], ["rchunk"])
                    sc.op("act", lambda e: e.activation(out=rchunk, in_=rchunk, func=AF.Exp, scale=-0.5), ["rchunk"], ["rchunk"])
                    for kc in range(nk):
                        sc.op("dve", lambda e, lat=lat, kc=kc, tc=tc: e.tensor_tensor(out=lat[:, kc, tc * 512:(tc + 1) * 512], in0=lat[:, kc, tc * 512:(tc + 1) * 512],
                                                                                   in1=rchunk, op=ALU.mult), [key, "rchunk"], [key])
            sc.barrier()

            for tt in range(NT):
                qp = bank(4 + tt % 2)
                for kc in range(4):
                    sc.op("pe", lambda e, kc=kc, tt=tt, qp=qp: e.matmul(qp, lhsT=qlatT[:, kc, tt * 128:(tt + 1) * 128], rhs=wuq[:, kc, 1024:1536],
                                                                      start=(kc == 0), stop=(kc == 3)), ["qlatT", "wuq"], ["b%d" % (4 + tt % 2)], signal=(kc == 3))
                cs = cost[:, tt * 32:(tt + 1) * 32]; sn = sint[:, tt * 32:(tt + 1) * 32]
                bk = "b%d" % (4 + tt % 2)
                for h in range(8):
                    t1 = qp[:, h * 64:h * 64 + 32]; t2 = qp[:, h * 64 + 32:h * 64 + 64]
                    sc.op("dve", lambda e, t1=t1, cs=cs: e.tensor_tensor(out=rt4[:, 0:32], in0=t1, in1=cs, op=ALU.mult), [bk, "cost"], ["rt4"])
                    sc.op("dve", lambda e, t2=t2, sn=sn: e.tensor_tensor(out=rt4[:, 32:64], in0=t2, in1=sn, op=ALU.mult), [bk, "sint"], ["rt4"])
                    sc.op("dve", lambda e, t1=t1, sn=sn: e.tensor_tensor(out=rt4[:, 64:96], in0=t1, in1=sn, op=ALU.mult), [bk, "sint"], ["rt4"])
                    sc.op("dve", lambda e, t2=t2, cs=cs: e.tensor_tensor(out=rt4[:, 96:128], in0=t2, in1=cs, op=ALU.mult), [bk, "cost"], ["rt4"])
                    sc.op("dve", lambda e, h=h: e.tensor_tensor(out=qrtok[:, h * 64:h * 64 + 32], in0=rt4[:, 0:32], in1=rt4[:, 32:64], op=ALU.subtract), ["rt4"], ["qrtok"])
                    sc.op("dve", lambda e, h=h: e.tensor_tensor(out=qrtok[:, h * 64 + 32:h * 64 + 64], in0=rt4[:, 64:96], in1=rt4[:, 96:128], op=ALU.add), ["rt4"], ["qrtok"])
                tp = bankbf(6 + tt % 2)
                for pr in range(4):
                    sc.op("pe", lambda e, tp=tp, pr=pr: e.transpose(tp[:, pr * 128:(pr + 1) * 128], qrtok[:, pr * 128:(pr + 1) * 128], ident[:]),
                          ["qrtok", "ident"], ["b%d" % (6 + tt % 2)], signal=(pr == 3))
                copy(evac_eng(), qropeT[:, :, tt * 128:(tt + 1) * 128], tp[:, 0:512].rearrange("p (r t) -> p r t", r=4),
                     ["b%d" % (6 + tt % 2)], ["qropeT"])

            sc.barrier()
            if stop_after == 12:
                break
            OT_diff = dqT
            OT_mla = dkT
            SC_B = (0, 1)

            def pvacc(j, n=132):
                return bank(2 + j, n)

            def sumsq_max(srcs, dst_col):
                for tc in range(4):
                    for i, (ap_, sel_, key_, rows) in enumerate(srcs):
                        sc.op("pe", lambda e, ap_=ap_, sel_=sel_, tc=tc, i=i, n=len(srcs): e.matmul(
                            bank(6), lhsT=sel_, rhs=ap_[:, tc * 512:(tc + 1) * 512], start=(i == 0), stop=(i == n - 1)),
                            [key_, "sel1", "sel2", "onesb"], ["b6"], signal=(i == len(srcs) - 1))
                    sc.op("dve", lambda e, tc=tc: e.tensor_reduce(out=st2[:, 32 + tc:33 + tc], in_=bank(6), axis=AX.X, op=ALU.max), ["b6"], ["st2x"])
                sc.op("dve", lambda e: e.tensor_reduce(out=st2[:, dst_col:dst_col + 1], in_=st2[:, 32:36], axis=AX.X, op=ALU.max), ["st2x"], ["st2x", "st2"])


            def attention_core(nmaps, kparts, qparts, vsrc, negm_cols, hbias, h, is_diff):
                for qc in range(4):
                    for mp in range(nmaps):
                        parts = kparts[mp]
                        negm = st2[:, negm_cols[mp]:negm_cols[mp] + 1]

                        def emit_score(kt, qc=qc, mp=mp, parts=parts):
                            sb_ = SC_B[kt % 2]
                            for pi, (kf, qf, keys) in enumerate(zip(parts[0], parts[1], parts[2])):
                                if len(parts) > 3:
                                    keys = keys + [parts[3] + (qc,)]
                                near = [j for j in range(4) if abs(kt - (4 * qc + j)) <= 1] if is_diff else []
                                last_score = (pi == len(parts[0]) - 1)
                                sc.op("pe", lambda e, kf=kf, qf=qf, kt=kt, qc=qc, sb_=sb_, pi=pi, last=(last_score and not near): e.matmul(
                                    bank(sb_), lhsT=kf(kt), rhs=qf(qc), start=(pi == 0), stop=last),
                                    keys, ["b%d" % sb_], signal=(last_score and not near))
                                if last_score:
                                    for ni, j in enumerate(near):
                                        dl = kt - (4 * qc + j)
                                        for bi, bsrc_ in enumerate((bhi, blo)):
                                            fin = (ni == len(near) - 1 and bi == 1)
                                            sc.op("pe", lambda e, sb_=sb_, j=j, dl=dl, bsrc_=bsrc_, fin=fin: e.matmul(
                                                bank(sb_, 128, j * 128), lhsT=ident[:], rhs=bsrc_[:, dl + 1, :], start=False, stop=fin),
                                                ["bhi", "blo", "ident"], ["b%d" % sb_], signal=fin)

                        def emit_exp(kt, qc=qc, mp=mp, negm=negm):
                            sb_ = SC_B[kt % 2]
                            slot = kt % 8
                            pt = PT[slot]
                            if not is_diff:
                                sc.op("act", lambda e, pt=pt, sb_=sb_, negm=negm: e.activation(out=pt, in_=bank(sb_), func=AF.Exp, scale=MLA_SCALE, bias=negm),
                                      ["b%d" % sb_, "st2"], [("PT", slot)])
                                return
                            cls = []
                            for j in range(4):
                                dl = kt - (4 * qc + j)
                                cls.append(16 + mp if dl > 1 else (8 + mp if dl < -1 else negm_cols[mp]))
                            j0 = 0
                            while j0 < 4:
                                j1 = j0
                                while j1 + 1 < 4 and cls[j1 + 1] == cls[j0]:
                                    j1 += 1
                                w_ = (j1 - j0 + 1) * 128
                                ps_ = bank(sb_, w_, j0 * 128)
                                po = pt[:, j0 * 128:j0 * 128 + w_]
                                bia = st2[:, cls[j0]:cls[j0] + 1]
                                sc.op("act", lambda e, po=po, ps_=ps_, bia=bia: e.activation(out=po, in_=ps_, func=AF.Exp, scale=DIFF_SCALE, bias=bia),
                                      ["b%d" % sb_, "st2"], [("PT", slot)])
                                j0 = j1 + 1

                        def emit_pv(kt):
                            slot = kt % 8
                            pt = PT[slot]
                            for j in range(4):
                                sc.op("pe", lambda e, pt=pt, j=j, kt=kt: e.matmul(
                                    pvacc(j), lhsT=pt[:, j * 128:(j + 1) * 128], rhs=vsrc(kt), start=(kt == 0), stop=(kt == 15)),
                                    [("PT", slot), "vsrc"], ["b%d" % (2 + j)], signal=(kt == 15))

                        emit_score(0)
                        for kt in range(16):
                            if kt + 1 < 16:
                                emit_score(kt + 1)
                            emit_exp(kt)
                            emit_pv(kt)
                        sc.barrier()
                        yield qc, mp

            for h in range(1 if stop_after in (14, 15) else 8):
                for di, delta in enumerate((-1, 0, 1)):
                    src = bass.AP(bsc.tensor, h * 65536 + 255 - delta * 128, [[511, 128], [1, 128]])
                    dma("sp", bth[h % 2][:, di, :], src, "bT%d" % (h % 2), writes=[("bT", h % 2)])
                bcur = bth[h % 2]
                sc.op("dve", lambda e, bcur=bcur: e.tensor_scalar(out=bcur, in0=bcur, scalar1=1.0 / DIFF_SCALE, scalar2=None, op0=ALU.mult), [("bT", h % 2)], [("bT", h % 2)])
                sc.op("dve", lambda e, bcur=bcur: e.tensor_copy(out=bhi, in_=bcur), [("bT", h % 2)], ["bhi"])
                sc.op("dve", lambda e, bcur=bcur: e.tensor_tensor(out=bcur, in0=bcur, in1=bhi, op=ALU.subtract), [("bT", h % 2), "bhi"], [("bT", h % 2)])
                sc.op("dve", lambda e, bcur=bcur: e.tensor_copy(out=blo, in_=bcur), [("bT", h % 2)], ["blo"])
                sc.op("act", lambda e, h=h: e.activation(out=sqtmp, in_=dqT[:, h, :], func=AF.Square), [("dqT", h, q_) for q_ in range(4)], ["sqtmp"])
                sumsq_max([(sqtmp, sel1[:], "sqtmp", 128)], 0)
                sumsq_max([(sqtmp, sel2[:], "sqtmp", 128)], 1)
                sc.op("act", lambda e, h=h: e.activation(out=sqtmp, in_=dkT[:, h, :], func=AF.Square), [("dkT", h)], ["sqtmp"])
                sumsq_max([(sqtmp, sel1[:], "sqtmp", 128)], 2)
                sumsq_max([(sqtmp, sel2[:], "sqtmp", 128)], 3)
                sc.op("dve", lambda e: e.tensor_tensor(out=st2[:, 4:6], in0=st2[:, 0:2], in1=st2[:, 2:4], op=ALU.add), ["st2"], ["st2"])
                sc.op("dve", lambda e: e.tensor_scalar(out=st2[:, 4:6], in0=st2[:, 4:6], scalar1=-0.5 * DIFF_SCALE * 1.02, scalar2=None, op0=ALU.mult), ["st2"], ["st2"])
                sc.op("dve", lambda e, h=h: e.tensor_scalar(out=st2[:, 8:10], in0=st2[:, 4:6], scalar1=tabbc[:, h:h + 1], scalar2=None, op0=ALU.add), ["st2", "tabbc0"], ["st2"])
                sc.op("dve", lambda e, h=h: e.tensor_scalar(out=st2[:, 16:18], in0=st2[:, 4:6], scalar1=tabbc[:, 8 + h:9 + h], scalar2=None, op0=ALU.add), ["st2", "tabbc1"], ["st2"])
                if stop_after == 13:
                    break
                kparts = []
                for mp in range(2):
                    r0 = mp * 64
                    kparts.append(([lambda kt, r0=r0, h=h: dkT[r0:r0 + 64, h, kt * 128:(kt + 1) * 128]],
                                   [lambda qc, r0=r0, h=h: dqT[r0:r0 + 64, h, qc * 512:(qc + 1) * 512]],
                                   [[("dkT", h)]], ("dqT", h)))
                vsrc = lambda kt, h=h: dvaug[:, kt, h, :]
                for qc, mp in attention_core(2, kparts, None, vsrc, (4, 5), None, h, True):
                    if stop_after == 14:
                        continue
                    for j in range(4):
                        oj = pvacc(j)
                        sc.op("dve", lambda e, oj=oj, j=j, mp=mp: e.reciprocal(out=st2[:, 44 + 4 * mp + j:45 + 4 * mp + j], in_=oj[:, 128:129]), ["b%d" % (2 + j)], ["st2y"])
                    if mp == 1:
                        sc.op("dve", lambda e: e.tensor_scalar(out=st2[:, 48:52], in0=st2[:, 48:52], scalar1=lamt[:, 4:5], scalar2=None, op0=ALU.mult), ["st2y", "lamt"], ["st2y"])
                    for j in range(4):
                        oj = pvacc(j)
                        odt = odt4[:, j, :]
                        if mp == 0:
                            sc.op("dve", lambda e, oj=oj, odt=odt, j=j: e.tensor_scalar(out=odt, in0=oj[:, 0:128], scalar1=st2[:, 44 + j:45 + j], scalar2=None, op0=ALU.mult),
                                  ["b%d" % (2 + j), "st2y"], ["odt"])
                        else:
                            sc.op("dve", lambda e, oj=oj, odt=odt, j=j: e.scalar_tensor_tensor(out=odt, in0=oj[:, 0:128], scalar=st2[:, 48 + j:49 + j], in1=odt, op0=ALU.mult, op1=ALU.add),
                                  ["b%d" % (2 + j), "st2y", "odt"], ["odt"])
                    if mp == 0:
                        continue
                    odf = odt4[:].rearrange("p j c -> p (j c)")
                    sc.op("dve", lambda e, odf=odf: e.tensor_tensor(out=sq4, in0=odf, in1=odf, op=ALU.mult), ["odt"], ["sq4"])
                    sc.op("dve", lambda e: e.tensor_reduce(out=st2[:, 52:56], in_=sq4.rearrange("p (j c) -> p j c", j=4), axis=AX.X, op=ALU.add), ["sq4"], ["st2z"])
                    sc.op("dve", lambda e: e.tensor_scalar(out=st2[:, 52:56], in0=st2[:, 52:56], scalar1=1.0 / 128, scalar2=EPS, op0=ALU.mult, op1=ALU.add), ["st2z"], ["st2z"])
                    sc.op("act", lambda e: e.activation(out=st2[:, 52:56], in_=st2[:, 52:56], func=AF.Ln), ["st2z"], ["st2z"])
                    sc.op("act", lambda e: e.activation(out=st2[:, 52:56], in_=st2[:, 52:56], func=AF.Exp, scale=-0.5), ["st2z"], ["st2z"])
                    tp = bankbf(7)
                    for j in range(4):
                        sc.op("dve", lambda e, j=j: e.tensor_scalar(out=odb4[:, j, :], in0=odt4[:, j, :], scalar1=st2[:, 52 + j:53 + j], scalar2=None, op0=ALU.mult),
                              ["odt", "st2z"], [("odb", j)])
                        sc.op("pe", lambda e, tp=tp, j=j: e.transpose(tp[:, j * 128:(j + 1) * 128], odb4[:, j, :], ident[:]), [("odb", j), "ident"], ["b7"], signal=(j == 3))
                    sc.op("act", lambda e, tp=tp, h=h, qc=qc: e.activation(out=OT_diff[:, h, qc * 512:(qc + 1) * 512], in_=tp[:, 0:512], func=AF.Identity, scale=subw[:, 0:1]),
                          ["b7", "subw"], [("dqT", h, qc)])

            if stop_after in (13, 14, 15, 16):
                break
            for h in range(8):
                half = (h % 2) * 64; pr = h // 2
                for tc in range(4):
                    pb = 6 + tc % 2
                    for kc in range(4):
                        sc.op("pe", lambda e, kc=kc, tc=tc, pb=pb, h=h: e.matmul(bank(pb), lhsT=wuq[:, kc, h * 128:(h + 1) * 128], rhs=qlatT[:, kc, tc * 512:(tc + 1) * 512],
                                                                              start=(kc == 0), stop=(kc == 3)), ["wuq", "qlatT"], ["b%d" % pb], signal=(kc == 3))
                    copy(evac_eng(), qnopeT[:, tc * 512:(tc + 1) * 512], bank(pb), ["b%d" % pb], ["qnopeT"])
                for tc in range(4):
                    pb = 6 + tc % 2
                    for kc in range(2):
                        sc.op("pe", lambda e, kc=kc, tc=tc, pb=pb, h=h: e.matmul(bank(pb), lhsT=wukv[:, kc, h * 128:(h + 1) * 128], rhs=kvlatT[:, kc, tc * 512:(tc + 1) * 512],
                                                                              start=(kc == 0), stop=(kc == 1)), ["wukv", "kvlatT"], ["b%d" % pb], signal=(kc == 1))
                    copy(evac_eng(), knopeT[:, tc * 512:(tc + 1) * 512], bank(pb), ["b%d" % pb], ["knopeT"])
                sc.op("dve", lambda e: e.memset(vaug[:, :, 128:132], 1.0), [], ["vsrc"])
                for tg in range(4):
                    pb = 6 + tg % 2
                    for i in range(4):
                        tt = tg * 4 + i
                        for kc in range(2):
                            sc.op("pe", lambda e, kc=kc, tt=tt, pb=pb, i=i, h=h: e.matmul(
                                bank(pb, 128, i * 128), lhsT=kvlatT[:, kc, tt * 128:(tt + 1) * 128], rhs=wukv[:, kc, 1024 + h * 128:1024 + (h + 1) * 128],
                                start=(kc == 0), stop=(kc == 1)), ["wukv", "kvlatT"], ["b%d" % pb], signal=(kc == 1))
                    copy(evac_eng(), vaug[:, tg * 4:(tg + 1) * 4, 0:128], bank(pb).rearrange("p (t c) -> p t c", t=4), ["b%d" % pb], ["vsrc"])
                qr = qropeT[half:half + 64, pr, :]
                kr = kropeT[half:half + 64, :]
                sel_h = onesb[half:half + 64, :]
                sc.op("act", lambda e: e.activation(out=sqtmp, in_=qnopeT, func=AF.Square), ["qnopeT"], ["sqtmp"])
                sumsq_max([(sqtmp, onesb[:], "sqtmp", 128)], 0)
                sc.op("act", lambda e: e.activation(out=sqtmp, in_=knopeT, func=AF.Square), ["knopeT"], ["sqtmp"])
                sumsq_max([(sqtmp, onesb[:], "sqtmp", 128)], 2)
                sc.op("act", lambda e, qr=qr, half=half: e.activation(out=sqtmp[half:half + 64, :], in_=qr, func=AF.Square), ["qropeT"], ["sqtmp"])
                sumsq_max([(sqtmp[half:half + 64, :], sel_h, "sqtmp", 64)], 1)
                sc.op("act", lambda e, kr=kr, half=half: e.activation(out=sqtmp[half:half + 64, :], in_=kr, func=AF.Square), ["kropeT"], ["sqtmp"])
                sumsq_max([(sqtmp[half:half + 64, :], sel_h, "sqtmp", 64)], 3)
                sc.op("dve", lambda e: e.tensor_tensor(out=st2[:, 4:6], in0=st2[:, 0:2], in1=st2[:, 2:4], op=ALU.add), ["st2"], ["st2"])
                sc.op("dve", lambda e: e.tensor_tensor(out=st2[:, 6:7], in0=st2[:, 4:5], in1=st2[:, 5:6], op=ALU.add), ["st2"], ["st2"])
                sc.op("dve", lambda e: e.tensor_scalar(out=st2[:, 6:7], in0=st2[:, 6:7], scalar1=-0.5 * MLA_SCALE * 1.02, scalar2=None, op0=ALU.mult), ["st2"], ["st2"])
                kparts = [([lambda kt: knopeT[:, kt * 128:(kt + 1) * 128], lambda kt, half=half: kropeT[half:half + 64, kt * 128:(kt + 1) * 128]],
                           [lambda qc: qnopeT[:, qc * 512:(qc + 1) * 512], lambda qc, half=half, pr=pr: qropeT[half:half + 64, pr, qc * 512:(qc + 1) * 512]],
                           [["knopeT", "qnopeT"], ["kropeT", "qropeT"]])]
                vsrc = lambda kt: vaug[:, kt, :]
                for qc, mp in attention_core(1, kparts, None, vsrc, (6,), None, h, False):
                    tp = bankbf(7)
                    for j in range(4):
                        o1 = pvacc(j)
                        sc.op("dve", lambda e, o1=o1, j=j: e.reciprocal(out=st2[:, 44 + j:45 + j], in_=o1[:, 128:129]), ["b%d" % (2 + j)], ["st2y"])
                        sc.op("dve", lambda e, o1=o1, j=j: e.tensor_scalar(out=odb4[:, j, :], in0=o1[:, 0:128], scalar1=st2[:, 44 + j:45 + j], scalar2=None, op0=ALU.mult),
                              ["b%d" % (2 + j), "st2y"], [("odb", j)])
                        sc.op("pe", lambda e, tp=tp, j=j: e.transpose(tp[:, j * 128:(j + 1) * 128], odb4[:, j, :], ident[:]), [("odb", j), "ident"], ["b7"], signal=(j == 3))
                    copy(evac_eng(), OT_mla[:, h, qc * 512:(qc + 1) * 512], tp[:, 0:512], ["b7"], [("dkT", h)])
            sc.barrier()
            if stop_after == 2:
                break

            wo = view(OFF_A, BF16, 16 * 2048, "p (k c) -> p k c", k=16)
            wov = w_o.rearrange("p (k c) -> p k c", k=16)
            for mc in range(16):
                dma("pool", wo[:, mc, :], wov[:, mc, :], "wo%d" % (mc % 4), writes=["wo"])
            OFF_W = OFF_P + 65536
            gw = view(OFF_W, F32, 2048)
            gtmp = view(OFF_W + 8192, F32, 2048)
            ast = [view(OFF_W + 16384 + i * 16384, BF16, 16 * 512, "p (k c) -> p k c", k=16) for i in range(2)]
            xt3 = [view(OFF_W + 49152 + i * 8192, F32, 2048) for i in range(2)]
            yt3 = view(OFF_W + 65536, F32, 2048) if OFF_W + 65536 + 8192 <= ARENA else None
            junk = view(OFF_W + 16384, BF16, 2048)
            cbk = view(OFF_W + 65536, BF16, 16 * 128, "p (k c) -> p k c", k=16)

            def build_gw(ch, postw, gw_, gtmp_, ast_, cbk_):
                gw, gtmp, ast, cbk = gw_, gtmp_, ast_, cbk_
                for kc in range(16):
                    sc.op("dve", lambda e, kc=kc: e.tensor_scalar(out=cbk[:, kc, :], in0=onesb[:], scalar1=cact[:, kc:kc + 1],
                                                                 scalar2=None, op0=ALU.mult), ["onesb", "cact"], ["cactbc"])
                dma("sp", gtmp, ada_b[:, ch * D:(ch + 1) * D].partition_broadcast(128), "gtmp", writes=["gtmp"])
                for piece in range(4):
                    st = ast[piece % 2]; sk = "ast%d" % (piece % 2)
                    dma("pool", st, ada_r[ch * 4 + piece].rearrange("p (k c) -> p k c", k=16), sk, writes=[sk])
                    for kc in range(16):
                        sc.op("pe", lambda e, st=st, kc=kc, piece=piece: e.matmul(bank(piece), lhsT=cbk[:, kc, :], rhs=st[:, kc, :], start=(kc == 0), stop=(kc == 15)),
                              [sk, "cactbc"], ["b%d" % piece], signal=(kc == 15))
                    sc.op("dve", lambda e, piece=piece: e.tensor_tensor(out=gw[:, piece * 512:(piece + 1) * 512], in0=bank(piece), in1=gtmp[:, piece * 512:(piece + 1) * 512], op=ALU.add),
                          ["b%d" % piece, "gtmp"], ["gw"])
                dma("sp", gtmp, postw.partition_broadcast(128), "gtmp", reads=["gw"], writes=["gtmp"])
                sc.op("dve", lambda e: e.tensor_tensor(out=gw, in0=gw, in1=gtmp, op=ALU.mult), ["gw", "gtmp"], ["gw"])
                sc.barrier()

            build_gw(2, apost, gw, gtmp, ast, cbk)
            x1v = x1s.rearrange("(t p) d -> t p d", p=128)
            outv = out[lb * S:(lb + 1) * S, :].rearrange("(t p) d -> t p d", p=128)

            def post_norm_residual(tt, ysrc, ykeys, xin, xkey, dst_dram, dkey, bufs):
                gw, gtmp, junk = bufs
                for dch in range(4):
                    sc.op("act", lambda e, dch=dch: e.activation(out=junk[:, dch * 512:(dch + 1) * 512], in_=ysrc[dch], func=AF.Square, accum_out=stat[:, 32 + dch:33 + dch]),
                          [ykeys[dch]], ["junk", "stat_y"])
                sc.op("dve", lambda e: e.tensor_reduce(out=stat[:, 36:37], in_=stat[:, 32:36], axis=AX.X, op=ALU.add), ["stat_y"], ["stat_y2"])
                sc.op("dve", lambda e: e.tensor_scalar(out=stat[:, 36:37], in0=stat[:, 36:37], scalar1=1.0 / D, scalar2=EPS, op0=ALU.mult, op1=ALU.add), ["stat_y2"], ["stat_y2"])
                sc.op("act", lambda e: e.activation(out=stat[:, 36:37], in_=stat[:, 36:37], func=AF.Ln), ["stat_y2"], ["stat_y2"])
                sc.op("act", lambda e: e.activation(out=stat[:, 36:37], in_=stat[:, 36:37], func=AF.Exp, scale=-0.5), ["stat_y2"], ["stat_y2"])
                for dch in range(4):
                    sl = slice(dch * 512, (dch + 1) * 512)
                    sc.op("dve", lambda e, dch=dch, sl=sl: e.scalar_tensor_tensor(out=gtmp[:, sl], in0=ysrc[dch], scalar=stat[:, 36:37], in1=gw[:, sl], op0=ALU.mult, op1=ALU.mult),
                          [ykeys[dch], "stat_y2", "gw"], ["gtmp"])
                    sc.op("dve", lambda e, sl=sl: e.tensor_tensor(out=xin[:, sl], in0=xin[:, sl], in1=gtmp[:, sl], op=ALU.add), ["gtmp", xkey], [xkey])
                dma("sp", dst_dram, xin, "st_" + xkey, reads=[xkey], writes=[dkey])

            for tt in range(NT):
                b_ = tt % 2
                dma("sp", xt3[b_], xv[tt], "xt3%d" % b_, writes=["xt3%d" % b_])
                for dch in range(4):
                    for mc in range(16):
                        src = OT_mla[:, mc, tt * 128:(tt + 1) * 128] if mc < 8 else OT_diff[:, mc - 8, tt * 128:(tt + 1) * 128]
                        sc.op("pe", lambda e, src=src, mc=mc, dch=dch: e.matmul(bank(dch), lhsT=src, rhs=wo[:, mc, dch * 512:(dch + 1) * 512], start=(mc == 0), stop=(mc == 15)),
                              ["wo"], ["b%d" % dch], signal=(mc == 15))
                post_norm_residual(tt, [bank(d_) for d_ in range(4)], ["b%d" % d_ for d_ in range(4)], xt3[b_], "xt3%d" % b_, x1v[tt], ("x1s", tt), (gw, gtmp, junk))
            sc.barrier()
            if stop_after == 3:
                break

            gT = view(0, BF16, NFB * 1024, "p (f t) -> p f t", f=NFB)
            h2T = view(90112, BF16, 16 * 1024, "p (k t) -> p k t", k=16)
            wupb = [view(122880 + i * 8192, BF16, 16 * 256, "p (k c) -> p k c", k=16) for i in range(2)]
            tcv = [view(139264 + i * 4096, F32, 1024) for i in range(2)]
            wdnb = [view(147456 + i * 11264, BF16, NFB * 128, "p (f c) -> p f c", f=NFB) for i in range(2)]
            xt4 = [view(169984 + i * 8192, F32, 2048) for i in range(2)]
            xn4 = view(186368, BF16, 2048)
            gw = view(190464, F32, 2048)
            gtmp = view(139264, F32, 2048)
            ast = [view(0 + i * 16384, BF16, 16 * 512, "p (k c) -> p k c", k=16) for i in range(2)]
            junk = xn4
            assert 202752 <= ARENA
            cbk = view(198656, BF16, 16 * 128, "p (k c) -> p k c", k=16)
            build_gw(5, fpost, gw, gtmp, ast, cbk)
            cwv = cw[:].rearrange("p (b t) -> p b t", t=3)
            sc.op("dve", lambda e: e.memset(halo[:], 0.0), [], ["halo"])

            for H in range(2):
                tiles = list(range(8 * H, 8 * H + 8))
                halo_tiles = ([8 * H - 1] if H == 1 else []) + ([8 * H + 8] if H == 0 else [])
                for tt in halo_tiles + tiles:
                    b_ = tt % 2
                    dma("sp", xt4[b_], x1v[tt], "xt4%d" % b_, reads=[("x1s", tt)], writes=["xt4%d" % b_])
                    sc.op("act", lambda e, b_=b_: e.activation(out=xn4, in_=xt4[b_], func=AF.Square, accum_out=stat[:, 40:41]), ["xt4%d" % b_], ["xn4", "stat_f"])
                    sc.op("dve", lambda e: e.tensor_scalar(out=stat[:, 40:41], in0=stat[:, 40:41], scalar1=D * EPS, scalar2=None, op0=ALU.add), ["stat_f"], ["stat_f"])
                    sc.op("act", lambda e: e.activation(out=stat[:, 40:41], in_=stat[:, 40:41], func=AF.Ln), ["stat_f"], ["stat_f"])
                    sc.op("act", lambda e: e.activation(out=stat[:, 40:41], in_=stat[:, 40:41], func=AF.Exp, scale=-0.5), ["stat_f"], ["stat_f"])
                    sc.op("dve", lambda e, b_=b_: e.tensor_scalar(out=xn4, in0=xt4[b_], scalar1=stat[:, 40:41], scalar2=None, op0=ALU.mult), ["xt4%d" % b_, "stat_f"], ["xn4"])
                    for g in range(2):
                        pb = 6 + g
                        pv = bankbf(pb)
                        for j in range(8):
                            kc = g * 8 + j
                            sc.op("pe", lambda e, pv=pv, j=j, kc=kc: e.transpose(pv[:, j * 128:(j + 1) * 128], xn4[:, kc * 128:(kc + 1) * 128], ident[:]),
                                  ["xn4", "ident"], ["b%d" % pb], signal=(j == 7))
                        for j in range(8):
                            kc = g * 8 + j
                            if tt in tiles:
                                lt = tt - 8 * H
                                dst = h2T[:, kc, lt * 128:(lt + 1) * 128]; src = pv[:, j * 128:(j + 1) * 128]; wk = "h2T"
                            elif tt < 8 * H:
                                dst = halo[:, kc * 2:kc * 2 + 1]; src = pv[:, j * 128 + 127:j * 128 + 128]; wk = "halo"
                            else:
                                dst = halo[:, kc * 2 + 1:kc * 2 + 2]; src = pv[:, j * 128:j * 128 + 1]; wk = "halo"
                            sc.op("dve", lambda e, dst=dst, src=src, kc=kc: e.tensor_scalar(out=dst, in0=src, scalar1=a_ffn[:, kc:kc + 1], scalar2=msc[:, 32 + kc:33 + kc],
                                                                                          op0=ALU.mult, op1=ALU.add), ["b%d" % pb, "a_ffn", "msc"], [wk])
                if H == 1:
                    sc.op("dve", lambda e: e.memset(halo[:].rearrange("p (k c) -> p k c", c=2)[:, :, 1:2], 0.0), [], ["halo"])
                hv = halo[:].rearrange("p (k c) -> p k c", c=2)
                for fb in range(NFB):
                    wb = wupb[fb % 2]; wk = "wup%d" % (fb % 2)
                    dma("pool", wb, w_up[fb].rearrange("p (k c) -> p k c", k=16), wk, writes=[wk])
                    for which in range(2):
                        blk = which * NFB + fb
                        up = psb[which]
                        uks = ["b%d" % (2 * which), "b%d" % (2 * which + 1)]
                        hk = "b%d" % (4 + which)
                        for tc in range(2):
                            for kc in range(16):
                                sc.op("pe", lambda e, wb=wb, kc=kc, tc=tc, up=up, which=which: e.matmul(
                                    up[:, tc * 512:(tc + 1) * 512], lhsT=wb[:, kc, which * 128:(which + 1) * 128], rhs=h2T[:, kc, tc * 512:(tc + 1) * 512],
                                    start=(kc == 0), stop=(kc == 15)), [wk, "h2T"], [uks[tc]], signal=(kc == 15))
                        hp = bank(4 + which, 2)
                        for kc in range(16):
                            sc.op("pe", lambda e, wb=wb, kc=kc, hp=hp, which=which: e.matmul(hp, lhsT=wb[:, kc, which * 128:(which + 1) * 128], rhs=hv[:, kc, :],
                                                                                          start=(kc == 0), stop=(kc == 15)), [wk, "halo"], [hk], signal=(kc == 15))
                        t_ = tcv[which]; tk = "tcv%d" % which
                        sc.op("act", lambda e, t_=t_, up=up, blk=blk: e.activation(out=t_, in_=up[:, :], func=AF.Identity, scale=cwv[:, blk, 1:2], bias=cb[:, blk:blk + 1]),
                              uks + ["cw", "cb"], [tk])
                        sc.op("dve", lambda e, t_=t_, up=up, blk=blk: e.scalar_tensor_tensor(out=t_[:, 1:1024], in0=up[:, 0:1023], scalar=cwv[:, blk, 0:1], in1=t_[:, 1:1024],
                                                                                           op0=ALU.mult, op1=ALU.add), uks + ["cw", tk], [tk])
                        sc.op("dve", lambda e, t_=t_, up=up, blk=blk: e.scalar_tensor_tensor(out=t_[:, 0:1023], in0=up[:, 1:1024], scalar=cwv[:, blk, 2:3], in1=t_[:, 0:1023],
                                                                                           op0=ALU.mult, op1=ALU.add), uks + ["cw", tk], [tk])
                        sc.op("dve", lambda e, t_=t_, hp=hp, blk=blk: e.scalar_tensor_tensor(out=t_[:, 0:1], in0=hp[:, 0:1], scalar=cwv[:, blk, 0:1], in1=t_[:, 0:1],
                                                                                           op0=ALU.mult, op1=ALU.add), [hk, "cw", tk], [tk])
                        sc.op("dve", lambda e, t_=t_, hp=hp, blk=blk: e.scalar_tensor_tensor(out=t_[:, 1023:1024], in0=hp[:, 1:2], scalar=cwv[:, blk, 2:3], in1=t_[:, 1023:1024],
                                                                                           op0=ALU.mult, op1=ALU.add), [hk, "cw", tk], [tk])
                    sc.op("act", lambda e: e.activation(out=tcv[0], in_=tcv[0], func=AF.Gelu_apprx_tanh), ["tcv0"], ["tcv0"])
                    sc.op("dve", lambda e, fb=fb: e.tensor_tensor(out=gT[:, fb, :], in0=tcv[0], in1=tcv[1], op=ALU.mult), ["tcv0", "tcv1"], ["gT"])
                yT = h2T
                for db in range(16):
                    wb = wdnb[db % 2]; wk = "wdn%d" % (db % 2)
                    dma("pool", wb, w_dn[db].rearrange("p (f c) -> p f c", f=NFB), wk, writes=[wk])
                    for tc in range(2):
                        pb = (db * 2 + tc) % 4
                        for fb in range(NFB):
                            sc.op("pe", lambda e, wb=wb, fb=fb, tc=tc, pb=pb: e.matmul(bank(pb), lhsT=wb[:, fb, :], rhs=gT[:, fb, tc * 512:(tc + 1) * 512],
                                                                                    start=(fb == 0), stop=(fb == NFB - 1)), [wk, "gT"], ["b%d" % pb], signal=(fb == NFB - 1))
                        copy(evac_eng(), yT[:, db, tc * 512:(tc + 1) * 512], bank(pb), ["b%d" % pb], ["h2T"])
                for lt in range(8):
                    tt = 8 * H + lt
                    b_ = tt % 2
                    dma("sp", xt4[b_], x1v[tt], "xt4%d" % b_, reads=[("x1s", tt)], writes=["xt4%d" % b_])
                    ysrc = []; ykeys = []
                    for g in range(2):
                        pb = 4 + g * 2
                        pb = 4 + g
                        pv = bankbf(pb)
                        for j in range(8):
                            db = g * 8 + j
                            sc.op("pe", lambda e, pv=pv, j=j, db=db, lt=lt: e.transpose(pv[:, j * 128:(j + 1) * 128], yT[:, db, lt * 128:(lt + 1) * 128], ident[:]),
                                  ["h2T", "ident"], ["b%d" % pb], signal=(j == 7))
                        ysrc += [pv[:, 0:512], pv[:, 512:1024]]
                        ykeys += ["b%d" % pb, "b%d" % pb]
                    post_norm_residual(tt, ysrc, ykeys, xt4[b_], "xt4%d" % b_, outv[tt], ("out", tt), (gw, gtmp, junk))
                sc.barrier()

        sc.finish()
        stuck, val = sc.simulate()
        if stuck:
            raise RuntimeError("semaphore dead-lock in recorded program: %r  sems=%r" % (stuck, {k: v for k, v in val.items() if not k.startswith("d:")}))
        build.sem_counts = {k: v for k, v in val.items()}
        sc.replay(E(nc.Block()))
    return nc


_CACHE = {}


def _prep_shared(inp):
    f = lambda a: np.ascontiguousarray(a, dtype=np.float32)
    w_in = inp["w_in"][0]
    fm_cols = list(range(0, 768)) + list(range(832, 2880))
    w_fm = w_in[:, fm_cols].reshape(16, 128, 22, 128).transpose(2, 1, 0, 3).reshape(22, 128, 16 * 128)
    w_kr = w_in[:, 768:832].reshape(16, 128, 64).transpose(1, 0, 2).reshape(128, 16 * 64)
    w_dv = w_in[:, 2880:3904].reshape(16, 128, 8, 128).transpose(2, 1, 0, 3).reshape(8, 128, 16 * 128)
    wuq = inp["w_uq"][0].reshape(512, 8, 192)
    wuq_r = np.concatenate([wuq[:, :, :128].reshape(512, 1024), wuq[:, :, 128:].reshape(512, 512)], axis=1)
    wuq_r = wuq_r.reshape(4, 128, 1536).transpose(1, 0, 2).reshape(128, 4 * 1536)
    wukv = inp["w_ukv"][0].reshape(256, 8, 256)
    wukv_r = np.concatenate([wukv[:, :, :128].reshape(256, 1024), wukv[:, :, 128:].reshape(256, 1024)], axis=1)
    wukv_r = wukv_r.reshape(2, 128, 2048).transpose(1, 0, 2).reshape(128, 2 * 2048)
    w_o = inp["w_o"][0].reshape(16, 128, 2048).transpose(1, 0, 2).reshape(128, 16 * 2048)
    wup = inp["w_up"][0].reshape(16, 128, 2, NFB, 128).transpose(3, 1, 0, 2, 4).reshape(NFB, 128, 16 * 256)
    cw = inp["conv_w"][0].reshape(3, 88, 128).transpose(2, 1, 0).reshape(128, 88 * 3)
    cb = inp["conv_b"][0].reshape(88, 128).T
    wdn = inp["w_down"][0].reshape(NFB, 128, 16, 128).transpose(2, 1, 0, 3).reshape(16, 128, NFB * 128)
    inv = (1.0 / (np.float32(ROPE_THETA) ** (np.arange(0, 64, 2, dtype=np.float32) / np.float32(64)))).astype(np.float32)
    selh = np.zeros((8, 8, 128), np.float32)
    for h in range(8):
        selh[h, h, :] = 1.0
    d = {
        "relb": f(inp["rel_bias"]), "relbT": f(inp["rel_bias"].T), "ident": np.eye(128, dtype=np.float32),
        "nvals": f(np.broadcast_to(255.0 - np.arange(512, dtype=np.float32), (8, 512))),
        "selh": f(selh.reshape(8, 1024)), "invf": f(np.broadcast_to(inv, (128, 32))),
        "ada_r": f(inp["ada_w"][0].reshape(16, 128, 24, 512).transpose(2, 1, 0, 3).reshape(24, 128, 16 * 512)), "ada_bT": f(inp["ada_b"][0].reshape(96, 128).T), "ada_b": f(inp["ada_b"]),
        "apnT": f(inp["attn_pre_norm"][0].reshape(16, 128).T), "fpnT": f(inp["ffn_pre_norm"][0].reshape(16, 128).T),
        "apost": f(inp["attn_post_norm"]), "fpost": f(inp["ffn_post_norm"]),
        "w_fm": f(w_fm), "w_kr": f(w_kr), "w_dv": f(w_dv),
        "qnT": f(inp["q_norm"][0].reshape(4, 128).T), "kvnT": f(inp["kv_norm"][0].reshape(2, 128).T),
        "w_uq": f(wuq_r), "w_ukv": f(wukv_r),
        "lamv": f(np.concatenate([inp["lambda_q1"][0], inp["lambda_k1"][0], inp["lambda_q2"][0], inp["lambda_k2"][0]])[None, :]),
        "sublnT": f(inp["diff_subln"][0][:, None]),
        "w_o": f(w_o), "w_up": f(wup), "cwT": f(cw), "cbT": f(cb), "w_dn": f(wdn),
    }
    return d


N_CORES = 8
NB = 8 // N_CORES


def _core_inputs(inp, shared, batches):
    m = dict(shared)
    m["x"] = np.ascontiguousarray(np.concatenate([inp["x"][b] for b in batches], axis=0), dtype=np.float32)
    m["cT"] = np.ascontiguousarray(np.concatenate([inp["c"][b].reshape(16, 128).T for b in batches], axis=1), dtype=np.float32)
    m["pos"] = np.ascontiguousarray(np.concatenate([inp["positions"][b].reshape(16, 128).T for b in batches], axis=1), dtype=np.int32)
    return m


def kernel(**inp):
    inp = {k: np.asarray(v) for k, v in inp.items()}
    if "nc" not in _CACHE:
        _CACHE["nc"] = build(nb=NB)
    nc = _CACHE["nc"]
    shared = _prep_shared(inp)
    in_maps = [_core_inputs(inp, shared, list(range(i * NB, (i + 1) * NB))) for i in range(N_CORES)]
    res = run_bass_kernel_spmd(nc, in_maps, core_ids=list(range(N_CORES)))
    outs = [np.asarray(r["out"], dtype=np.float32).reshape(NB, S, D) for r in res.results]
    return np.concatenate(outs, axis=0)
```

```python
import math
import os
from contextlib import ExitStack

import numpy as np
import concourse.bass as bass
import concourse.mybir as mybir
from concourse.bass_utils import run_bass_kernel_spmd

F32 = mybir.dt.float32
BF16 = mybir.dt.bfloat16
I32 = mybir.dt.int32
U8 = mybir.dt.uint8
AF = mybir.ActivationFunctionType
ALU = mybir.AluOpType
AX = mybir.AxisListType

D = 2048
S = 2048
NT = 16
DFF = 5632
NFB = 44
EPS = 1e-6
MLA_SCALE = 192 ** -0.5
DIFF_SCALE = 64 ** -0.5
LAMBDA_INIT = 0.8 - 0.6 * math.exp(0.0)
ROPE_THETA = 10000.0

BUCKETS = {0: (0, 0), 8: (-11, -8), 9: (-15, -12), 10: (-22, -16), 11: (-31, -23), 12: (-45, -32),
           13: (-63, -46), 14: (-90, -64), 15: (-100000, -91), 24: (8, 11), 25: (12, 15), 26: (16, 22),
           27: (23, 31), 28: (32, 45), 29: (46, 63), 30: (64, 90), 31: (91, 100000)}
for _b in range(1, 8):
    BUCKETS[_b] = (-_b, -_b)
    BUCKETS[16 + _b] = (_b, _b)


SHARED = [("relb", (32, 8)), ("relbT", (8, 32)), ("ident", (128, 128)), ("nvals", (8, 512)), ("selh", (8, 1024)), ("invf", (128, 32)),
          ("ada_r", (24, 128, 16 * 512)), ("ada_bT", (128, 96)), ("ada_b", (1, 6 * D)), ("apnT", (128, 16)), ("fpnT", (128, 16)),
          ("apost", (1, D)), ("fpost", (1, D)), ("w_fm", (22, 128, 2048)), ("w_kr", (128, 1024)), ("w_dv", (8, 128, 2048)),
          ("qnT", (128, 4)), ("kvnT", (128, 2)), ("w_uq", (128, 4 * 1536)), ("w_ukv", (128, 2 * 2048)), ("lamv", (1, 256)),
          ("sublnT", (128, 1)), ("w_o", (128, 16 * 2048)), ("w_up", (NFB, 128, 16 * 256)), ("cwT", (128, 88 * 3)), ("cbT", (128, 88)),
          ("w_dn", (16, 128, NFB * 128))]
CW = 2048


def _layout():
    off = [0, 0]
    offs = {}
    g = 0
    for name, shp in SHARED:
        if name == "lamv":
            g = 1
        offs[name] = (g, off[g])
        n = int(np.prod(shp))
        off[g] += (n + 63) // 64 * 64
    rows = [(o + 8 * CW - 1) // (8 * CW) for o in off]
    return offs, rows


class Sched:
    COMPUTE = ("pe", "act", "dve")
    QUEUES = ("sp", "pool")

    def __init__(self, nc, es):
        self.nc = nc
        self.es = es
        self.streams = {e: [] for e in self.COMPUTE + self.QUEUES}
        self.semobj = {e: es.enter_context(nc.semaphore("s_" + e)) for e in self.COMPUTE}
        self.cnt = {e: 0 for e in self.COMPUTE}
        self.dcnt = {}
        self.waited = {e: {} for e in self.COMPUTE + self.QUEUES}
        self.lastw = {}
        self.readers = {}

    def _dsem(self, name):
        k = "d:" + name
        if k not in self.semobj:
            self.semobj[k] = self.es.enter_context(self.nc.semaphore("q_" + name))
            self.dcnt[k] = 0
        return k

    def _wait(self, eng, tok):
        k, v = tok
        if self.waited[eng].get(k, 0) < v:
            self.waited[eng][k] = v
            self.streams[eng].append(("w", k, v))

    def op(self, eng, fn, reads=(), writes=(), signal=True, dma=None):
        deps = []
        for r in reads:
            if r in self.lastw:
                deps.append(self.lastw[r])
        for w in writes:
            if w in self.lastw:
                deps.append(self.lastw[w])
            deps.extend(self.readers.get(w, ()))
        for t in deps:
            if eng == "pe" and t[0] == "pe":
                continue
            self._wait(eng, t)
        if dma is not None:
            k = self._dsem(dma)
            self.dcnt[k] += 16
            tok = (k, self.dcnt[k])
            self.streams[eng].append(("i", fn, k, 16))
        else:
            if signal:
                self.cnt[eng] += 1
                tok = (eng, self.cnt[eng])
                self.streams[eng].append(("i", fn, eng, 1))
            else:
                tok = (eng, self.cnt[eng] + 1)
                self.streams[eng].append(("i", fn, None, 0))
        for w in writes:
            self.lastw[w] = tok
            self.readers[w] = []
        for r in reads:
            self.readers.setdefault(r, []).append(tok)
        return tok

    def barrier(self):
        toks = [(e, self.cnt[e]) for e in self.COMPUTE if self.cnt[e] > 0]
        toks += [(k, v) for k, v in self.dcnt.items() if v > 0]
        for e in self.COMPUTE + self.QUEUES:
            for t in toks:
                if t[0] != e:
                    self._wait(e, t)
        self.lastw.clear()
        self.readers.clear()

    def finish(self):
        for k, v in self.dcnt.items():
            if v > 0:
                self._wait("sp", (k, v))

    def simulate(self):
        val = {k: 0 for k in self.semobj}
        pc = {e: 0 for e in self.streams}
        progress = True
        while progress:
            progress = False
            for e, st in self.streams.items():
                while pc[e] < len(st):
                    a = st[pc[e]]
                    if a[0] == "w":
                        if val[a[1]] < a[2]:
                            break
                    elif a[2] is not None:
                        val[a[2]] += a[3]
                    pc[e] += 1
                    progress = True
        stuck = {e: (pc[e], len(st), st[pc[e]][:3] if pc[e] < len(st) else None) for e, st in self.streams.items() if pc[e] < len(st)}
        return stuck, val

    def replay(self, block):
        def mk(stream):
            def body(e):
                for a in stream:
                    if a[0] == "w":
                        e.wait_ge(self.semobj[a[1]], a[2])
                    else:
                        ins = a[1](e)
                        if a[2] is not None:
                            ins.then_inc(self.semobj[a[2]], a[3])
            return body
        block.tensor(mk(self.streams["pe"]))
        block.scalar(mk(self.streams["act"]))
        block.vector(mk(self.streams["dve"]))
        block.sync(mk(self.streams["sp"]))
        block.gpsimd(mk(self.streams["pool"]))


def build(nb=1, stop_after=99, dbg=False):
    nc = bass.Bass("TRN2", target_bir_lowering=False)

    skip = set()
    if stop_after in (12, 13, 14, 15, 16):
        skip |= {"w_up", "w_dn", "fpost", "w_o", "apost", "ada_b"}
    elif stop_after < 4:
        skip |= {"w_up", "w_dn", "fpost"}
    if stop_after < 3:
        skip |= {"w_o", "apost", "ada_b"}
    if stop_after < 2:
        skip |= {"w_uq", "w_ukv"}

    def din(name, shape, dt=F32):
        if name in skip:
            return None
        return nc.dram_tensor(name, list(shape), dt, kind="ExternalInput").ap()

    x = din("x", [nb * S, D])
    cT = din("cT", [128, nb * 16])
    pos = din("pos", [128, nb * 16], I32)
    shp = dict(SHARED)
    relb = din("relb", shp["relb"]); relbTd = din("relbT", shp["relbT"]); identd = din("ident", shp["ident"]); nvals = din("nvals", shp["nvals"])
    selh = din("selh", shp["selh"]); invf = din("invf", shp["invf"]); ada_r = din("ada_r", shp["ada_r"]); ada_bT = din("ada_bT", shp["ada_bT"])
    ada_b = din("ada_b", shp["ada_b"]); apnT = din("apnT", shp["apnT"]); fpnT = din("fpnT", shp["fpnT"]); apost = din("apost", shp["apost"])
    fpost = din("fpost", shp["fpost"]); w_fm = din("w_fm", shp["w_fm"]); w_kr = din("w_kr", shp["w_kr"]); w_dv = din("w_dv", shp["w_dv"])
    qnT = din("qnT", shp["qnT"]); kvnT = din("kvnT", shp["kvnT"]); w_uq = din("w_uq", shp["w_uq"]); w_ukv = din("w_ukv", shp["w_ukv"])
    lamv = din("lamv", shp["lamv"]); sublnT = din("sublnT", shp["sublnT"]); w_o = din("w_o", shp["w_o"]); w_up = din("w_up", shp["w_up"])
    cwT = din("cwT", shp["cwT"]); cbT = din("cbT", shp["cbT"]); w_dn = din("w_dn", shp["w_dn"])
    out = nc.dram_tensor("out", [nb * S, D], F32, kind="ExternalOutput").ap()
    x1s = nc.dram_tensor("x1s", [S, D], F32, kind="ExternalOutput" if dbg else "Internal").ap()
    bsc = nc.dram_tensor("bsc", [8, 128 * 512], F32, kind="Internal").ap()
    dbg_out = nc.dram_tensor("dbg", [128, 4096], F32, kind="ExternalOutput").ap() if dbg else None

    with ExitStack() as es:
        E = es.enter_context
        ARENA = 202752
        arena = E(nc.sbuf_tensor("arena", [128, ARENA], U8))

        def view(off, dt, n, pat=None, **kw):
            sz = {F32: 4, BF16: 2, I32: 4}[dt]
            v = arena[:, off:off + n * sz].bitcast(dt)
            if pat:
                v = v.rearrange(pat, **kw)
            return v

        def sb(name, shape, dt=F32):
            return E(nc.sbuf_tensor(name, list(shape), dt))

        ident = sb("identb", [128, 128], BF16)
        onesb = sb("onesb", [128, 128], BF16)
        sel1 = sb("sel1", [128, 128], BF16)
        sel2 = sb("sel2", [128, 128], BF16)
        cact = sb("cact", [128, 16], BF16)
        msc = sb("msc", [128, 64])
        adab = sb("adab", [128, 96])
        apn = sb("apn", [128, 16]); fpn = sb("fpn", [128, 16])
        a_pre = sb("a_pre", [128, 16]); a_ffn = sb("a_ffn", [128, 16])
        qnw = sb("qnw", [128, 4]); kvnw = sb("kvnw", [128, 2])
        posi = sb("posi", [128, 16], I32)
        posf = sb("posf", [128, 16])
        invt = sb("invt", [128, 32])
        cost = sb("cost", [128, 16 * 32]); sint = sb("sint", [128, 16 * 32])
        tabbc = sb("tabbc", [128, 16])
        lamb = sb("lamb", [128, 256])
        lamt = sb("lamt", [128, 8])
        subw = sb("subw", [128, 1])
        cw = sb("cw", [128, 88 * 3]); cb = sb("cb", [128, 88])
        stat = sb("stat", [128, 64])
        halo = sb("halo", [128, 16 * 2], BF16)
        st2 = sb("st2", [128, 64])
        psb = [E(nc.psum_tensor("pp%d" % i, [128, 1024], F32)) for i in range(4)]

        def bank(b, n=512, off=0):
            return psb[b // 2][:, (b % 2) * 512 + off:(b % 2) * 512 + off + n]

        def bankbf(b):
            return psb[b // 2][:, (b % 2) * 512:(b % 2) * 512 + 512].bitcast(BF16)

        sc = Sched(nc, es)
        rr = [0]

        def evac_eng():
            rr[0] ^= 1
            return "act" if rr[0] else "dve"

        def copy(eng, out_, in_, reads, writes):
            if eng == "act":
                sc.op("act", lambda e: e.activation(out=out_, in_=in_, func=AF.Copy), reads, writes)
            else:
                sc.op("dve", lambda e: e.tensor_copy(out=out_, in_=in_), reads, writes)

        def dma(q, out_, in_, name, reads=(), writes=()):
            sc.op(q, lambda e: e.dma_start(out=out_, in_=in_), reads, writes, dma=name)

        OFF_A = 0
        OFF_P = 65536
        hT = view(OFF_A, BF16, 16 * 2048, "p (k t) -> p k t", k=16)
        dqT = view(OFF_P, BF16, 8 * 2048, "p (h t) -> p h t", h=8)
        dkT = view(OFF_P + 32768, BF16, 8 * 2048, "p (h t) -> p h t", h=8)
        dvaug = view(OFF_P + 65536, BF16, 16 * 8 * 132, "p (t h c) -> p t h c", t=16, h=8)
        OFF_L = OFF_P + 65536 + 33792
        qlatT = view(OFF_L, BF16, 4 * 2048, "p (k t) -> p k t", k=4)
        kvlatT = view(OFF_L + 16384, BF16, 2 * 2048, "p (k t) -> p k t", k=2)
        kropeT = view(OFF_L + 24576, BF16, 2048)
        OFF_S = OFF_L + 28672
        wst = [view(OFF_S + i * 4096, BF16, 16 * 128, "p (k c) -> p k c", k=16) for i in range(2)]
        xt = [view(OFF_P + i * 8192, F32, 2048) for i in range(2)]
        xn = [view(OFF_P + 16384 + i * 4096, BF16, 2048) for i in range(2)]
        adast = [view(OFF_P + 24576 + i * 16384, BF16, 16 * 512, "p (k c) -> p k c", k=16) for i in range(2)]
        zt = view(OFF_P + 57344, F32, 512)
        ztm = view(OFF_P + 59392, F32, 512)
        ztm2 = view(OFF_P + 61440, F32, 512)
        nv = view(OFF_P + 63488, F32, 512)
        selhs = view(OFF_P + 65536, F32, 1024)
        zrep = view(OFF_P + 69632, F32, 512)
        ang = view(OFF_P + 71680, F32, 512)
        angq = view(OFF_P + 73728, F32, 512)
        angi = view(OFF_P + 75776, I32, 512)
        idf = view(OFF_P + 77824, F32, 128)

        for lb in range(nb):
            if lb == 0:
                dma("sp", idf, identd, "c0", writes=["idf"])
                sc.op("dve", lambda e: e.tensor_copy(out=ident[:], in_=idf), ["idf"], ["ident"])
                sc.op("dve", lambda e: e.memset(onesb[:], 1.0), [], ["onesb"])
                sc.op("dve", lambda e: e.memset(sel1[:], 0.0), [], ["sel1"])
                sc.op("dve", lambda e: e.memset(sel1[0:64, :], 1.0), [], ["sel1"])
                sc.op("dve", lambda e: e.memset(sel2[:], 0.0), [], ["sel2"])
                sc.op("dve", lambda e: e.memset(sel2[64:128, :], 1.0), [], ["sel2"])
                sc.op("dve", lambda e: e.memset(halo[:], 0.0), [], ["halo"])
                small = [(adab, ada_bT), (apn, apnT), (fpn, fpnT), (qnw, qnT), (kvnw, kvnT), (invt, invf),
                         (subw, sublnT), (cw, cwT), (cb, cbT)]
                for i, (d_, s_) in enumerate(small):
                    dma("sp", d_[:], s_, "c%d" % (1 + i), writes=[d_.name])
            dma("sp", posi[:], pos[:, lb * 16:(lb + 1) * 16], "c10", writes=["posi"])
            if lb == 0:
                dma("sp", tabbc[:, 0:8], relb[15:16, :].partition_broadcast(128), "c11", writes=["tabbc0"])
                dma("sp", tabbc[:, 8:16], relb[31:32, :].partition_broadcast(128), "c12", writes=["tabbc1"])
                dma("sp", lamb[:], lamv.partition_broadcast(128), "c13", writes=["lamb"])
            ctf = view(OFF_P + 78848, F32, 16)
            dma("sp", ctf, cT[:, lb * 16:(lb + 1) * 16], "c14", writes=["ctf"])
            sc.op("act", lambda e: e.activation(out=cact[:], in_=ctf, func=AF.Silu), ["ctf"], ["cact"])

            mps = bank(0, 64)
            k_ada = [0]

            def ada_piece(ci, ch, piece):
                k_ = k_ada[0]
                st = adast[k_ % 2]
                dma("pool", st, ada_r[ch * 4 + piece].rearrange("p (k c) -> p k c", k=16), "ada%d" % (k_ % 2), writes=["adast%d" % (k_ % 2)])
                for fcl in range(4):
                    col = ci * 16 + piece * 4 + fcl
                    for kc in range(16):
                        sc.op("pe", lambda e, st=st, fcl=fcl, kc=kc, col=col: e.matmul(
                            mps[:, col:col + 1], lhsT=st[:, kc, fcl * 128:(fcl + 1) * 128], rhs=cact[:, kc:kc + 1],
                            start=(kc == 0), stop=(kc == 15)),
                            ["adast%d" % (k_ % 2), "cact"], ["mps"], signal=(kc == 15))
                k_ada[0] += 1

            SQD = math.sqrt(D)

            def ada_finish(pairs, dst, wn, ci_scale):
                for ci, ch in pairs:
                    sc.op("dve", lambda e, ci=ci, ch=ch: e.tensor_tensor(out=msc[:, ci * 16:(ci + 1) * 16], in0=mps[:, ci * 16:(ci + 1) * 16],
                                                                        in1=adab[:, ch * 16:(ch + 1) * 16], op=ALU.add),
                          ["mps", "adab"], [("msc", ci)])
                sc.op("dve", lambda e: e.scalar_tensor_tensor(
                    out=dst[:], in0=msc[:, ci_scale * 16:(ci_scale + 1) * 16], scalar=1.0, in1=wn[:], op0=ALU.add, op1=ALU.mult),
                    [("msc", ci_scale), wn.name], [dst.name])
                sc.op("dve", lambda e: e.tensor_scalar(out=dst[:], in0=dst[:], scalar1=SQD, scalar2=None, op0=ALU.mult),
                      [dst.name], [dst.name])

            for ci, ch in ((0, 0), (1, 1)):
                for piece in range(4):
                    ada_piece(ci, ch, piece)
            ada_finish(((0, 0), (1, 1)), a_pre, apn, 1)
            ada_later = [(ci, ch, piece) for ci, ch in ((2, 3), (3, 4)) for piece in range(4)]

            if lb == 0:
                sc.op("dve", lambda e: e.tensor_tensor(out=lamb[:, 0:64], in0=lamb[:, 0:64], in1=lamb[:, 64:128], op=ALU.mult), ["lamb"], ["lamb"])
                sc.op("dve", lambda e: e.tensor_tensor(out=lamb[:, 128:192], in0=lamb[:, 128:192], in1=lamb[:, 192:256], op=ALU.mult), ["lamb"], ["lamb"])
                sc.op("dve", lambda e: e.tensor_reduce(out=lamt[:, 0:1], in_=lamb[:, 0:64], axis=AX.X, op=ALU.add), ["lamb"], ["lamt"])
                sc.op("dve", lambda e: e.tensor_reduce(out=lamt[:, 1:2], in_=lamb[:, 128:192], axis=AX.X, op=ALU.add), ["lamb"], ["lamt"])
                sc.op("act", lambda e: e.activation(out=lamt[:, 2:4], in_=lamt[:, 0:2], func=AF.Exp), ["lamt"], ["lamt"])
                sc.op("dve", lambda e: e.scalar_tensor_tensor(out=lamt[:, 4:5], in0=lamt[:, 3:4], scalar=-LAMBDA_INIT, in1=lamt[:, 2:3],
                                                              op0=ALU.add, op1=ALU.subtract), ["lamt"], ["lamt"])
                sc.op("dve", lambda e: e.tensor_scalar(out=subw[:], in0=subw[:], scalar1=1.0 - LAMBDA_INIT, scalar2=None, op0=ALU.mult),
                      ["subw"], ["subw"])

            sc.op("dve", lambda e: e.tensor_copy(out=posf[:], in_=posi[:]), ["posi"], ["posf"])
            for tt in range(16):
                sc.op("dve", lambda e, tt=tt: e.tensor_scalar(out=ang[:, tt * 32:(tt + 1) * 32], in0=invt[:], scalar1=posf[:, tt:tt + 1],
                                                             scalar2=None, op0=ALU.mult), ["posf", "invt"], ["ang"])
            C1 = 6.28125
            C2 = 2 * math.pi - C1
            for (dst, shift) in ((sint, 0.0), (cost, math.pi / 2)):
                sc.op("dve", lambda e, shift=shift: e.tensor_scalar(out=angq, in0=ang, scalar1=shift, scalar2=1.0 / (2 * math.pi),
                                                                   op0=ALU.add, op1=ALU.mult), ["ang"], ["angq"])
                sc.op("dve", lambda e: e.tensor_copy(out=angi, in_=angq), ["angq"], ["angi"])
                sc.op("dve", lambda e: e.tensor_copy(out=angq, in_=angi), ["angi"], ["angq"])
                sc.op("dve", lambda e, shift=shift, dst=dst: e.tensor_scalar(out=dst[:], in0=ang, scalar1=shift, scalar2=None, op0=ALU.add),
                      ["ang"], [dst.name])
                sc.op("dve", lambda e, dst=dst: e.scalar_tensor_tensor(out=dst[:], in0=angq, scalar=-C1, in1=dst[:], op0=ALU.mult, op1=ALU.add),
                      ["angq", dst.name], [dst.name])
                sc.op("dve", lambda e, dst=dst: e.scalar_tensor_tensor(out=dst[:], in0=angq, scalar=-C2, in1=dst[:], op0=ALU.mult, op1=ALU.add),
                      ["angq", dst.name], [dst.name])
                sc.op("dve", lambda e, dst=dst: e.tensor_scalar(out=angq, in0=dst[:], scalar1=math.pi, scalar2=-2 * math.pi, op0=ALU.is_gt, op1=ALU.mult),
                      [dst.name], ["angq"])
                sc.op("dve", lambda e, dst=dst: e.tensor_tensor(out=dst[:], in0=dst[:], in1=angq, op=ALU.add), ["angq", dst.name], [dst.name])
                sc.op("dve", lambda e, dst=dst: e.tensor_scalar(out=angq, in0=dst[:], scalar1=-math.pi, scalar2=2 * math.pi, op0=ALU.is_lt, op1=ALU.mult),
                      [dst.name], ["angq"])
                sc.op("dve", lambda e, dst=dst: e.tensor_tensor(out=dst[:], in0=dst[:], in1=angq, op=ALU.add), ["angq", dst.name], [dst.name])
                sc.op("dve", lambda e, dst=dst: e.tensor_scalar(out=dst[:], in0=dst[:], scalar1=math.pi, scalar2=-math.pi, op0=ALU.min, op1=ALU.max),
                      [dst.name], [dst.name])
                sc.op("act", lambda e, dst=dst: e.activation(out=dst[:], in_=dst[:], func=AF.Sin), [dst.name], [dst.name])

            if lb == 0:
                dma("sp", nv[0:8, :], nvals, "c15", writes=["nv"])
                dma("sp", selhs[0:8, :], selh, "c16", writes=["selhs"])
                relbT = view(OFF_P + 79872, F32, 32)
                dma("sp", relbT[0:8, :], relbTd, "c17", writes=["relbT"])
                sc.op("dve", lambda e: e.memset(zt[0:8, :], 0.0), [], ["zt"])
                for b_, (lo, hi) in sorted(BUCKETS.items()):
                    sc.op("dve", lambda e, lo=lo: e.tensor_scalar(out=ztm[0:8, :], in0=nv[0:8, :], scalar1=float(lo) - 0.5, scalar2=None, op0=ALU.is_gt),
                          ["nv"], ["ztm"])
                    sc.op("dve", lambda e, hi=hi: e.scalar_tensor_tensor(out=ztm2[0:8, :], in0=nv[0:8, :], scalar=float(hi) + 0.5, in1=ztm[0:8, :],
                                                                        op0=ALU.is_lt, op1=ALU.mult), ["nv", "ztm"], ["ztm2"])
                    sc.op("dve", lambda e, b_=b_: e.scalar_tensor_tensor(out=zt[0:8, :], in0=ztm2[0:8, :], scalar=relbT[0:8, b_:b_ + 1], in1=zt[0:8, :],
                                                                        op0=ALU.mult, op1=ALU.add), ["ztm2", "relbT", "zt"], ["zt"])
                for h in range(8):
                    sc.op("pe", lambda e, h=h: e.matmul(bank(1), lhsT=selhs[0:8, h * 128:(h + 1) * 128], rhs=zt[0:8, :], start=True, stop=True),
                          ["selhs", "zt"], ["b1"])
                    sc.op("dve", lambda e: e.tensor_copy(out=zrep, in_=bank(1)), ["b1"], ["zrep"])
                    dma("sp", bsc[h].rearrange("(r c) -> r c", c=512), zrep, "bsc", reads=["zrep"], writes=["bsc%d" % h])

            xv = x[lb * S:(lb + 1) * S, :].rearrange("(t p) d -> t p d", p=128)
            for tt in range(NT):
                b_ = tt % 2
                dma("sp", xt[b_], xv[tt], "xt%d" % b_, writes=["xt%d" % b_])
                sc.op("act", lambda e, b_=b_, tt=tt: e.activation(out=xn[b_], in_=xt[b_], func=AF.Square, accum_out=stat[:, tt:tt + 1]),
                      ["xt%d" % b_], ["xn%d" % b_, "stat_a%d" % tt])
                sc.op("dve", lambda e, tt=tt: e.tensor_scalar(out=stat[:, 16 + tt:17 + tt], in0=stat[:, tt:tt + 1], scalar1=D * EPS, scalar2=None, op0=ALU.add),
                      ["stat_a%d" % tt], ["stat_b%d" % tt])
                sc.op("act", lambda e, tt=tt: e.activation(out=stat[:, 16 + tt:17 + tt], in_=stat[:, 16 + tt:17 + tt], func=AF.Ln),
                      ["stat_b%d" % tt], ["stat_b%d" % tt])
                sc.op("act", lambda e, tt=tt: e.activation(out=stat[:, 16 + tt:17 + tt], in_=stat[:, 16 + tt:17 + tt], func=AF.Exp, scale=-0.5),
                      ["stat_b%d" % tt], ["stat_b%d" % tt])
                sc.op("dve", lambda e, b_=b_, tt=tt: e.tensor_scalar(out=xn[b_], in0=xt[b_], scalar1=stat[:, 16 + tt:17 + tt], scalar2=None, op0=ALU.mult),
                      ["xt%d" % b_, "stat_b%d" % tt], ["xn%d" % b_])
                for g in range(2):
                    pb = 2 + g
                    pv = bankbf(pb)
                    for j in range(8):
                        kc = g * 8 + j
                        sc.op("pe", lambda e, pv=pv, j=j, kc=kc, b_=b_: e.transpose(pv[:, j * 128:(j + 1) * 128], xn[b_][:, kc * 128:(kc + 1) * 128], ident[:]),
                              ["xn%d" % b_, "ident"], ["b%d" % pb], signal=(j == 7))
                    for j in range(8):
                        kc = g * 8 + j
                        if evac_eng() == "act":
                            sc.op("act", lambda e, pv=pv, j=j, kc=kc, tt=tt: e.activation(
                                out=hT[:, kc, tt * 128:(tt + 1) * 128], in_=pv[:, j * 128:(j + 1) * 128], func=AF.Identity,
                                scale=a_pre[:, kc:kc + 1], bias=msc[:, kc:kc + 1]), ["b%d" % pb, "a_pre", ("msc", 0)], [("hT", tt)])
                        else:
                            sc.op("dve", lambda e, pv=pv, j=j, kc=kc, tt=tt: e.tensor_scalar(
                                out=hT[:, kc, tt * 128:(tt + 1) * 128], in0=pv[:, j * 128:(j + 1) * 128],
                                scalar1=a_pre[:, kc:kc + 1], scalar2=msc[:, kc:kc + 1], op0=ALU.mult, op1=ALU.add),
                                ["b%d" % pb, "a_pre", ("msc", 0)], [("hT", tt)])
                if tt % 2 == 1 and ada_later:
                    ada_piece(*ada_later.pop(0))
            while ada_later:
                ada_piece(*ada_later.pop(0))
            ada_finish(((2, 3), (3, 4)), a_ffn, fpn, 3)
            sc.barrier()

            def dbg_dump(ap_f32_src, ncols, reads):
                t_ = view(OFF_A, F32, 4096) if False else None
                dma("sp", dbg_out[:, 0:ncols], ap_f32_src, "dbg", reads=reads)

            if stop_after == 0:
                tmpf = view(OFF_P, F32, 4096)
                sc.op("dve", lambda e: e.tensor_copy(out=tmpf[:, 0:2048], in_=hT[:, 0, :]), [], ["tmpf"])
                sc.op("dve", lambda e: e.tensor_copy(out=tmpf[:, 2048:4096], in_=hT[:, 15, :]), [], ["tmpf"])
                dbg_dump(tmpf, 4096, ["tmpf"])
                sc.finish()
                sc.replay(E(nc.Block()))
                return nc

            sc.op("dve", lambda e: e.memset(dvaug[:, :, :, 128:132], 1.0), [], ["dvaug"])
            hkeys = lambda tc: [("hT", 4 * tc + i) for i in range(4)]
            nw = 0
            for j in range(22):
                wb = wst[nw % 2]; wk = "wst%d" % (nw % 2); nw += 1
                dma("pool", wb, w_fm[j].rearrange("p (k c) -> p k c", k=16), wk, writes=[wk])
                for tc in range(4):
                    pb = 2 * (tc % 2) + 0 if False else (tc % 4)
                    pb = 4 + (tc % 4)
                    for kc in range(16):
                        sc.op("pe", lambda e, wb=wb, kc=kc, tc=tc, pb=pb: e.matmul(bank(pb), lhsT=wb[:, kc, :], rhs=hT[:, kc, tc * 512:(tc + 1) * 512],
                                                                                start=(kc == 0), stop=(kc == 15)),
                              [wk] + hkeys(tc), ["b%d" % pb], signal=(kc == 15))
                    if j < 4:
                        dst, key = qlatT[:, j, tc * 512:(tc + 1) * 512], "qlatT"
                    elif j < 6:
                        dst, key = kvlatT[:, j - 4, tc * 512:(tc + 1) * 512], "kvlatT"
                    elif j < 14:
                        dst, key = dqT[:, j - 6, tc * 512:(tc + 1) * 512], ("dqT", j - 6, tc)
                    else:
                        dst, key = dkT[:, j - 14, tc * 512:(tc + 1) * 512], ("dkT", j - 14)
                    copy(evac_eng(), dst, bank(pb), ["b%d" % pb], [key])
            for h in range(8):
                wb = wst[nw % 2]; wk = "wst%d" % (nw % 2); nw += 1
                dma("pool", wb, w_dv[h].rearrange("p (k c) -> p k c", k=16), wk, writes=[wk])
                for tg in range(4):
                    pb = 4 + (tg % 4)
                    for i in range(4):
                        tt = tg * 4 + i
                        for kc in range(16):
                            sc.op("pe", lambda e, wb=wb, kc=kc, tt=tt, pb=pb, i=i: e.matmul(
                                bank(pb, 128, i * 128), lhsT=hT[:, kc, tt * 128:(tt + 1) * 128], rhs=wb[:, kc, :],
                                start=(kc == 0), stop=(kc == 15)), [wk, ("hT", tt)], ["b%d" % pb], signal=(kc == 15))
                    copy(evac_eng(), dvaug[:, tg * 4:(tg + 1) * 4, h, 0:128], bank(pb).rearrange("p (t c) -> p t c", t=4),
                         ["b%d" % pb], ["dvaug"])
            wkr = view(OFF_S, BF16, 16 * 64, "p (k c) -> p k c", k=16)
            dma("pool", wkr, w_kr.rearrange("p (k c) -> p k c", k=16), "wst0", writes=["wst0"])
            krt = view(OFF_S + 4096, BF16, 128)
            rtmp = view(OFF_S + 4096 + 256, F32, 128)
            for tt in range(NT):
                pb = 4 + (tt % 2)
                kp = bank(pb, 64)
                for kc in range(16):
                    sc.op("pe", lambda e, kc=kc, tt=tt, kp=kp: e.matmul(kp, lhsT=hT[:, kc, tt * 128:(tt + 1) * 128], rhs=wkr[:, kc, :],
                                                                      start=(kc == 0), stop=(kc == 15)),
                          ["wst0", ("hT", tt)], ["b%d" % pb], signal=(kc == 15))
                cs = cost[:, tt * 32:(tt + 1) * 32]; sn = sint[:, tt * 32:(tt + 1) * 32]
                sc.op("dve", lambda e, kp=kp, cs=cs: e.tensor_tensor(out=rtmp[:, 0:32], in0=kp[:, 0:32], in1=cs, op=ALU.mult), ["b%d" % pb, "cost"], ["wst1"])
                sc.op("dve", lambda e, kp=kp, sn=sn: e.tensor_tensor(out=rtmp[:, 32:64], in0=kp[:, 32:64], in1=sn, op=ALU.mult), ["b%d" % pb, "sint"], ["wst1"])
                sc.op("dve", lambda e, kp=kp, sn=sn: e.tensor_tensor(out=rtmp[:, 64:96], in0=kp[:, 0:32], in1=sn, op=ALU.mult), ["b%d" % pb, "sint"], ["wst1"])
                sc.op("dve", lambda e, kp=kp, cs=cs: e.tensor_tensor(out=rtmp[:, 96:128], in0=kp[:, 32:64], in1=cs, op=ALU.mult), ["b%d" % pb, "cost"], ["wst1"])
                sc.op("dve", lambda e: e.tensor_tensor(out=krt[:, 0:32], in0=rtmp[:, 0:32], in1=rtmp[:, 32:64], op=ALU.subtract), ["wst1"], ["wst1"])
                sc.op("dve", lambda e: e.tensor_tensor(out=krt[:, 32:64], in0=rtmp[:, 64:96], in1=rtmp[:, 96:128], op=ALU.add), ["wst1"], ["wst1"])
                sc.op("dve", lambda e: e.tensor_copy(out=krt[:, 64:128], in_=krt[:, 0:64]), ["wst1"], ["wst1"])
                tp = bankbf(6 + (tt % 2))
                sc.op("pe", lambda e, tp=tp: e.transpose(tp[:, 0:128], krt, ident[:]), ["wst1", "ident"], ["b%d" % (6 + tt % 2)])
                copy(evac_eng(), kropeT[:, tt * 128:(tt + 1) * 128], tp[:, 0:128], ["b%d" % (6 + tt % 2)], ["kropeT"])
            sc.barrier()
            if stop_after == 1:
                break

            o = OFF_A
            PT = [view(o + i * 1024, BF16, 512) for i in range(8)]; o += 8192
            wuq = view(o, BF16, 4 * 1536, "p (k c) -> p k c", k=4); o += 12288
            wukv = view(o, BF16, 2 * 2048, "p (k c) -> p k c", k=2); o += 8192
            qropeT = view(o, BF16, 4 * 2048, "p (r t) -> p r t", r=4); o += 16384
            rchunk = view(o, F32, 512)
            sq4 = view(o, F32, 512); o += 2048
            sqtmp = view(OFF_S, BF16, 2048)
            bth = [view(OFF_S + 4096 + i * 1536, F32, 384, "p (d q) -> p d q", d=3) for i in range(2)]
            qnopeT = view(o, BF16, 2048); o += 4096
            knopeT = view(o, BF16, 2048); o += 4096
            vaug = view(o, BF16, 16 * 132, "p (t c) -> p t c", t=16); o += 4224
            qrtok = view(o, BF16, 512)
            odb4 = view(o, BF16, 512, "p (j c) -> p j c", j=4); o += 1024
            rt4 = view(o, F32, 128); o += 512
            odt4 = view(o, F32, 512, "p (j c) -> p j c", j=4); o += 2048
            bhi = view(o, BF16, 384, "p (d q) -> p d q", d=3); o += 768
            blo = view(o, BF16, 384, "p (d q) -> p d q", d=3); o += 768
            assert o <= 65536, o

            dma("pool", wuq, w_uq.rearrange("p (k c) -> p k c", k=4), "wuq", writes=["wuq"])
            dma("pool", wukv, w_ukv.rearrange("p (k c) -> p k c", k=2), "wukv", writes=["wukv"])
            for kc in range(4):
                sc.op("dve", lambda e, kc=kc: e.tensor_scalar(out=wuq[:, kc, :], in0=wuq[:, kc, :], scalar1=qnw[:, kc:kc + 1], scalar2=None, op0=ALU.mult),
                      ["wuq", "qnw"], ["wuq"])
            for kc in range(2):
                sc.op("dve", lambda e, kc=kc: e.tensor_scalar(out=wukv[:, kc, :], in0=wukv[:, kc, :], scalar1=kvnw[:, kc:kc + 1], scalar2=None, op0=ALU.mult),
                      ["wukv", "kvnw"], ["wukv"])

            for (lat, nk, key) in ((qlatT, 4, "qlatT"), (kvlatT, 2, "kvlatT")):
                n_lat = nk * 128
                for tc in range(4):
                    for kc in range(nk):
                        sc.op("act", lambda e, lat=lat, kc=kc, tc=tc: e.activation(out=sqtmp[:, 0:512], in_=lat[:, kc, tc * 512:(tc + 1) * 512], func=AF.Square),
                              [key], ["sqtmp"])
                        sc.op("pe", lambda e, kc=kc, nk=nk: e.matmul(bank(0), lhsT=onesb[:], rhs=sqtmp[:, 0:512], start=(kc == 0), stop=(kc == nk - 1)),
                              ["sqtmp", "onesb"], ["b0"], signal=True)
                    sc.op("dve", lambda e, n_lat=n_lat: e.tensor_scalar(out=rchunk, in0=bank(0), scalar1=1.0 / n_lat, scalar2=EPS, op0=ALU.mult, op1=ALU.add),
                          ["b0"], ["rchunk"])
                    sc.op("act", lambda e: e.activation(out=rchunk, in_=rchunk, func=AF.Ln), ["rchunk"], ["rchunk"])
                    sc.op("act", lambda e: e.activation(out=rchunk, in_=rchunk, func=AF.Exp, scale=-0.5), ["rchunk"], ["rchunk"])
                    for kc in range(nk):
                        sc.op("dve", lambda e, lat=lat, kc=kc, tc=tc: e.tensor_tensor(out=lat[:, kc, tc * 512:(tc + 1) * 512], in0=lat[:, kc, tc * 512:(tc + 1) * 512],
                                                                                   in1=rchunk, op=ALU.mult), [key, "rchunk"], [key])
            sc.barrier()

            for tt in range(NT):
                qp = bank(4 + tt % 2)
                for kc in range(4):
                    sc.op("pe", lambda e, kc=kc, tt=tt, qp=qp: e.matmul(qp, lhsT=qlatT[:, kc, tt * 128:(tt + 1) * 128], rhs=wuq[:, kc, 1024:1536],
                                                                      start=(kc == 0), stop=(kc == 3)), ["qlatT", "wuq"], ["b%d" % (4 + tt % 2)], signal=(kc == 3))
                cs = cost[:, tt * 32:(tt + 1) * 32]; sn = sint[:, tt * 32:(tt + 1) * 32]
                bk = "b%d" % (4 + tt % 2)
                for h in range(8):
                    t1 = qp[:, h * 64:h * 64 + 32]; t2 = qp[:, h * 64 + 32:h * 64 + 64]
                    sc.op("dve", lambda e, t1=t1, cs=cs: e.tensor_tensor(out=rt4[:, 0:32], in0=t1, in1=cs, op=ALU.mult), [bk, "cost"], ["rt4"])
                    sc.op("dve", lambda e, t2=t2, sn=sn: e.tensor_tensor(out=rt4[:, 32:64], in0=t2, in1=sn, op=ALU.mult), [bk, "sint"], ["rt4"])
                    sc.op("dve", lambda e, t1=t1, sn=sn: e.tensor_tensor(out=rt4[:, 64:96], in0=t1, in1=sn, op=ALU.mult), [bk, "sint"], ["rt4"])
                    sc.op("dve", lambda e, t2=t2, cs=cs: e.tensor_tensor(out=rt4[:, 96:128], in0=t2, in1=cs, op=ALU.mult), [bk, "cost"], ["rt4"])
                    sc.op("dve", lambda e, h=h: e.tensor_tensor(out=qrtok[:, h * 64:h * 64 + 32], in0=rt4[:, 0:32], in1=rt4[:, 32:64], op=ALU.subtract), ["rt4"], ["qrtok"])
                    sc.op("dve", lambda e, h=h: e.tensor_tensor(out=qrtok[:, h * 64 + 32:h * 64 + 64], in0=rt4[:, 64:96], in1=rt4[:, 96:128], op=ALU.add), ["rt4"], ["qrtok"])
                tp = bankbf(6 + tt % 2)
                for pr in range(4):
                    sc.op("pe", lambda e, tp=tp, pr=pr: e.transpose(tp[:, pr * 128:(pr + 1) * 128], qrtok[:, pr * 128:(pr + 1) * 128], ident[:]),
                          ["qrtok", "ident"], ["b%d" % (6 + tt % 2)], signal=(pr == 3))
                copy(evac_eng(), qropeT[:, :, tt * 128:(tt + 1) * 128], tp[:, 0:512].rearrange("p (r t) -> p r t", r=4),
                     ["b%d" % (6 + tt % 2)], ["qropeT"])

            sc.barrier()
            if stop_after == 12:
                break
            OT_diff = dqT
            OT_mla = dkT
            SC_B = (0, 1)

            def pvacc(j, n=132):
                return bank(2 + j, n)

            def sumsq_max(srcs, dst_col):
                for tc in range(4):
                    for i, (ap_, sel_, key_, rows) in enumerate(srcs):
                        sc.op("pe", lambda e, ap_=ap_, sel_=sel_, tc=tc, i=i, n=len(srcs): e.matmul(
                            bank(6), lhsT=sel_, rhs=ap_[:, tc * 512:(tc + 1) * 512], start=(i == 0), stop=(i == n - 1)),
                            [key_, "sel1", "sel2", "onesb"], ["b6"], signal=(i == len(srcs) - 1))
                    sc.op("dve", lambda e, tc=tc: e.tensor_reduce(out=st2[:, 32 + tc:33 + tc], in_=bank(6), axis=AX.X, op=ALU.max), ["b6"], ["st2x"])
                sc.op("dve", lambda e: e.tensor_reduce(out=st2[:, dst_col:dst_col + 1], in_=st2[:, 32:36], axis=AX.X, op=ALU.max), ["st2x"], ["st2x", "st2"])


            pending = []

            def attention_core(nmaps, kparts, qparts, vsrc, negm_cols, hbias, h, is_diff):
                for qc in range(4):
                    for mp in range(nmaps):
                        parts = kparts[mp]
                        negm = st2[:, negm_cols[mp]:negm_cols[mp] + 1]

                        def emit_score(kt, qc=qc, mp=mp, parts=parts):
                            sb_ = SC_B[kt % 2]
                            for pi, (kf, qf, keys) in enumerate(zip(parts[0], parts[1], parts[2])):
                                if len(parts) > 3:
                                    keys = keys + [parts[3] + (qc,)]
                                near = [j for j in range(4) if abs(kt - (4 * qc + j)) <= 1] if is_diff else []
                                last_score = (pi == len(parts[0]) - 1)
                                sc.op("pe", lambda e, kf=kf, qf=qf, kt=kt, qc=qc, sb_=sb_, pi=pi, last=(last_score and not near): e.matmul(
                                    bank(sb_), lhsT=kf(kt), rhs=qf(qc), start=(pi == 0), stop=last),
                                    keys, ["b%d" % sb_], signal=(last_score and not near))
                                if last_score:
                                    for ni, j in enumerate(near):
                                        dl = kt - (4 * qc + j)
                                        for bi, bsrc_ in enumerate((bhi, blo)):
                                            fin = (ni == len(near) - 1 and bi == 1)
                                            sc.op("pe", lambda e, sb_=sb_, j=j, dl=dl, bsrc_=bsrc_, fin=fin: e.matmul(
                                                bank(sb_, 128, j * 128), lhsT=ident[:], rhs=bsrc_[:, dl + 1, :], start=False, stop=fin),
                                                ["bhi", "blo", "ident"], ["b%d" % sb_], signal=fin)

                        def emit_exp(kt, qc=qc, mp=mp, negm=negm):
                            sb_ = SC_B[kt % 2]
                            slot = kt % 8
                            pt = PT[slot]
                            if not is_diff:
                                sc.op("act", lambda e, pt=pt, sb_=sb_, negm=negm: e.activation(out=pt, in_=bank(sb_), func=AF.Exp, scale=MLA_SCALE, bias=negm),
                                      ["b%d" % sb_, "st2"], [("PT", slot)])
                                return
                            cls = []
                            for j in range(4):
                                dl = kt - (4 * qc + j)
                                cls.append(16 + mp if dl > 1 else (8 + mp if dl < -1 else negm_cols[mp]))
                            j0 = 0
                            while j0 < 4:
                                j1 = j0
                                while j1 + 1 < 4 and cls[j1 + 1] == cls[j0]:
                                    j1 += 1
                                w_ = (j1 - j0 + 1) * 128
                                ps_ = bank(sb_, w_, j0 * 128)
                                po = pt[:, j0 * 128:j0 * 128 + w_]
                                bia = st2[:, cls[j0]:cls[j0] + 1]
                                sc.op("act", lambda e, po=po, ps_=ps_, bia=bia: e.activation(out=po, in_=ps_, func=AF.Exp, scale=DIFF_SCALE, bias=bia),
                                      ["b%d" % sb_, "st2"], [("PT", slot)])
                                j0 = j1 + 1

                        def emit_pv(kt):
                            slot = kt % 8
                            pt = PT[slot]
                            for j in range(4):
                                sc.op("pe", lambda e, pt=pt, j=j, kt=kt: e.matmul(
                                    pvacc(j), lhsT=pt[:, j * 128:(j + 1) * 128], rhs=vsrc(kt), start=(kt == 0), stop=(kt == 15)),
                                    [("PT", slot), "vsrc"], ["b%d" % (2 + j)], signal=(kt == 15))

                        emit_score(0)
                        for kt in range(16):
                            if kt + 1 < 16:
                                emit_score(kt + 1)
                            if kt == 3:
                                while pending:
                                    pending.pop(0)()
                            emit_exp(kt)
                            emit_pv(kt)
                        sc.barrier()
                        yield qc, mp

            for h in range(1 if stop_after in (14, 15) else 8):
                for di, delta in enumerate((-1, 0, 1)):
                    src = bass.AP(bsc.tensor, h * 65536 + 255 - delta * 128, [[511, 128], [1, 128]])
                    dma("sp", bth[h % 2][:, di, :], src, "bT%d" % (h % 2), writes=[("bT", h % 2)])
                bcur = bth[h % 2]
                sc.op("dve", lambda e, bcur=bcur: e.tensor_scalar(out=bcur, in0=bcur, scalar1=1.0 / DIFF_SCALE, scalar2=None, op0=ALU.mult), [("bT", h % 2)], [("bT", h % 2)])
                sc.op("dve", lambda e, bcur=bcur: e.tensor_copy(out=bhi, in_=bcur), [("bT", h % 2)], ["bhi"])
                sc.op("dve", lambda e, bcur=bcur: e.tensor_tensor(out=bcur, in0=bcur, in1=bhi, op=ALU.subtract), [("bT", h % 2), "bhi"], [("bT", h % 2)])
                sc.op("dve", lambda e, bcur=bcur: e.tensor_copy(out=blo, in_=bcur), [("bT", h % 2)], ["blo"])
                sc.op("act", lambda e, h=h: e.activation(out=sqtmp, in_=dqT[:, h, :], func=AF.Square), [("dqT", h, q_) for q_ in range(4)], ["sqtmp"])
                sumsq_max([(sqtmp, sel1[:], "sqtmp", 128)], 0)
                sumsq_max([(sqtmp, sel2[:], "sqtmp", 128)], 1)
                sc.op("act", lambda e, h=h: e.activation(out=sqtmp, in_=dkT[:, h, :], func=AF.Square), [("dkT", h)], ["sqtmp"])
                sumsq_max([(sqtmp, sel1[:], "sqtmp", 128)], 2)
                sumsq_max([(sqtmp, sel2[:], "sqtmp", 128)], 3)
                sc.op("dve", lambda e: e.tensor_tensor(out=st2[:, 4:6], in0=st2[:, 0:2], in1=st2[:, 2:4], op=ALU.add), ["st2"], ["st2"])
                sc.op("dve", lambda e: e.tensor_scalar(out=st2[:, 4:6], in0=st2[:, 4:6], scalar1=-0.5 * DIFF_SCALE * 1.02, scalar2=None, op0=ALU.mult), ["st2"], ["st2"])
                sc.op("dve", lambda e, h=h: e.tensor_scalar(out=st2[:, 8:10], in0=st2[:, 4:6], scalar1=tabbc[:, h:h + 1], scalar2=None, op0=ALU.add), ["st2", "tabbc0"], ["st2"])
                sc.op("dve", lambda e, h=h: e.tensor_scalar(out=st2[:, 16:18], in0=st2[:, 4:6], scalar1=tabbc[:, 8 + h:9 + h], scalar2=None, op0=ALU.add), ["st2", "tabbc1"], ["st2"])
                if stop_after == 13:
                    break
                kparts = []
                for mp in range(2):
                    r0 = mp * 64
                    kparts.append(([lambda kt, r0=r0, h=h: dkT[r0:r0 + 64, h, kt * 128:(kt + 1) * 128]],
                                   [lambda qc, r0=r0, h=h: dqT[r0:r0 + 64, h, qc * 512:(qc + 1) * 512]],
                                   [[("dkT", h)]], ("dqT", h)))
                vsrc = lambda kt, h=h: dvaug[:, kt, h, :]
                for qc, mp in attention_core(2, kparts, None, vsrc, (4, 5), None, h, True):
                    if stop_after == 14:
                        continue
                    for j in range(4):
                        oj = pvacc(j)
                        sc.op("dve", lambda e, oj=oj, j=j, mp=mp: e.reciprocal(out=st2[:, 44 + 4 * mp + j:45 + 4 * mp + j], in_=oj[:, 128:129]), ["b%d" % (2 + j)], ["st2y"])
                    if mp == 1:
                        sc.op("dve", lambda e: e.tensor_scalar(out=st2[:, 48:52], in0=st2[:, 48:52], scalar1=lamt[:, 4:5], scalar2=None, op0=ALU.mult), ["st2y", "lamt"], ["st2y"])
                    for j in range(4):
                        oj = pvacc(j)
                        odt = odt4[:, j, :]
                        if mp == 0:
                            sc.op("dve", lambda e, oj=oj, odt=odt, j=j: e.tensor_scalar(out=odt, in0=oj[:, 0:128], scalar1=st2[:, 44 + j:45 + j], scalar2=None, op0=ALU.mult),
                                  ["b%d" % (2 + j), "st2y"], ["odt"])
                        else:
                            sc.op("dve", lambda e, oj=oj, odt=odt, j=j: e.scalar_tensor_tensor(out=odt, in0=oj[:, 0:128], scalar=st2[:, 48 + j:49 + j], in1=odt, op0=ALU.mult, op1=ALU.add),
                                  ["b%d" % (2 + j), "st2y", "odt"], ["odt"])
                    if mp == 0:
                        continue
                    odf = odt4[:].rearrange("p j c -> p (j c)")
                    sc.op("dve", lambda e, odf=odf: e.tensor_tensor(out=sq4, in0=odf, in1=odf, op=ALU.mult), ["odt"], ["sq4"])
                    sc.op("dve", lambda e: e.tensor_reduce(out=st2[:, 52:56], in_=sq4.rearrange("p (j c) -> p j c", j=4), axis=AX.X, op=ALU.add), ["sq4"], ["st2z"])
                    sc.op("dve", lambda e: e.tensor_scalar(out=st2[:, 52:56], in0=st2[:, 52:56], scalar1=1.0 / 128, scalar2=EPS, op0=ALU.mult, op1=ALU.add), ["st2z"], ["st2z"])
                    sc.op("act", lambda e: e.activation(out=st2[:, 52:56], in_=st2[:, 52:56], func=AF.Ln), ["st2z"], ["st2z"])
                    sc.op("act", lambda e: e.activation(out=st2[:, 52:56], in_=st2[:, 52:56], func=AF.Exp, scale=-0.5), ["st2z"], ["st2z"])
                    for j in range(4):
                        sc.op("dve", lambda e, j=j: e.tensor_scalar(out=odb4[:, j, :], in0=odt4[:, j, :], scalar1=st2[:, 52 + j:53 + j], scalar2=None, op0=ALU.mult),
                              ["odt", "st2z"], [("odb", j)])

                    def fin_b(h=h, qc=qc):
                        tp = bankbf(7)
                        for j in range(4):
                            sc.op("pe", lambda e, tp=tp, j=j: e.transpose(tp[:, j * 128:(j + 1) * 128], odb4[:, j, :], ident[:]), [("odb", j), "ident"], ["b7"], signal=(j == 3))
                        sc.op("act", lambda e, tp=tp, h=h, qc=qc: e.activation(out=OT_diff[:, h, qc * 512:(qc + 1) * 512], in_=tp[:, 0:512], func=AF.Identity, scale=subw[:, 0:1]),
                              ["b7", "subw"], [("dqT", h, qc)])
                    pending.append(fin_b)
                while pending:
                    pending.pop(0)()

            if stop_after in (13, 14, 15, 16):
                break
            for h in range(8):
                half = (h % 2) * 64; pr = h // 2
                for tc in range(4):
                    pb = 6 + tc % 2
                    for kc in range(4):
                        sc.op("pe", lambda e, kc=kc, tc=tc, pb=pb, h=h: e.matmul(bank(pb), lhsT=wuq[:, kc, h * 128:(h + 1) * 128], rhs=qlatT[:, kc, tc * 512:(tc + 1) * 512],
                                                                              start=(kc == 0), stop=(kc == 3)), ["wuq", "qlatT"], ["b%d" % pb], signal=(kc == 3))
                    copy(evac_eng(), qnopeT[:, tc * 512:(tc + 1) * 512], bank(pb), ["b%d" % pb], ["qnopeT"])
                for tc in range(4):
                    pb = 6 + tc % 2
                    for kc in range(2):
                        sc.op("pe", lambda e, kc=kc, tc=tc, pb=pb, h=h: e.matmul(bank(pb), lhsT=wukv[:, kc, h * 128:(h + 1) * 128], rhs=kvlatT[:, kc, tc * 512:(tc + 1) * 512],
                                                                              start=(kc == 0), stop=(kc == 1)), ["wukv", "kvlatT"], ["b%d" % pb], signal=(kc == 1))
                    copy(evac_eng(), knopeT[:, tc * 512:(tc + 1) * 512], bank(pb), ["b%d" % pb], ["knopeT"])
                sc.op("dve", lambda e: e.memset(vaug[:, :, 128:132], 1.0), [], ["vsrc"])
                for tg in range(4):
                    pb = 6 + tg % 2
                    for i in range(4):
                        tt = tg * 4 + i
                        for kc in range(2):
                            sc.op("pe", lambda e, kc=kc, tt=tt, pb=pb, i=i, h=h: e.matmul(
                                bank(pb, 128, i * 128), lhsT=kvlatT[:, kc, tt * 128:(tt + 1) * 128], rhs=wukv[:, kc, 1024 + h * 128:1024 + (h + 1) * 128],
                                start=(kc == 0), stop=(kc == 1)), ["wukv", "kvlatT"], ["b%d" % pb], signal=(kc == 1))
                    copy(evac_eng(), vaug[:, tg * 4:(tg + 1) * 4, 0:128], bank(pb).rearrange("p (t c) -> p t c", t=4), ["b%d" % pb], ["vsrc"])
                qr = qropeT[half:half + 64, pr, :]
                kr = kropeT[half:half + 64, :]
                sel_h = onesb[half:half + 64, :]
                sc.op("act", lambda e: e.activation(out=sqtmp, in_=qnopeT, func=AF.Square), ["qnopeT"], ["sqtmp"])
                sumsq_max([(sqtmp, onesb[:], "sqtmp", 128)], 0)
                sc.op("act", lambda e: e.activation(out=sqtmp, in_=knopeT, func=AF.Square), ["knopeT"], ["sqtmp"])
                sumsq_max([(sqtmp, onesb[:], "sqtmp", 128)], 2)
                sc.op("act", lambda e, qr=qr, half=half: e.activation(out=sqtmp[half:half + 64, :], in_=qr, func=AF.Square), ["qropeT"], ["sqtmp"])
                sumsq_max([(sqtmp[half:half + 64, :], sel_h, "sqtmp", 64)], 1)
                sc.op("act", lambda e, kr=kr, half=half: e.activation(out=sqtmp[half:half + 64, :], in_=kr, func=AF.Square), ["kropeT"], ["sqtmp"])
                sumsq_max([(sqtmp[half:half + 64, :], sel_h, "sqtmp", 64)], 3)
                sc.op("dve", lambda e: e.tensor_tensor(out=st2[:, 4:6], in0=st2[:, 0:2], in1=st2[:, 2:4], op=ALU.add), ["st2"], ["st2"])
                sc.op("dve", lambda e: e.tensor_tensor(out=st2[:, 6:7], in0=st2[:, 4:5], in1=st2[:, 5:6], op=ALU.add), ["st2"], ["st2"])
                sc.op("dve", lambda e: e.tensor_scalar(out=st2[:, 6:7], in0=st2[:, 6:7], scalar1=-0.5 * MLA_SCALE * 1.02, scalar2=None, op0=ALU.mult), ["st2"], ["st2"])
                kparts = [([lambda kt: knopeT[:, kt * 128:(kt + 1) * 128], lambda kt, half=half: kropeT[half:half + 64, kt * 128:(kt + 1) * 128]],
                           [lambda qc: qnopeT[:, qc * 512:(qc + 1) * 512], lambda qc, half=half, pr=pr: qropeT[half:half + 64, pr, qc * 512:(qc + 1) * 512]],
                           [["knopeT", "qnopeT"], ["kropeT", "qropeT"]])]
                vsrc = lambda kt: vaug[:, kt, :]
                for qc, mp in attention_core(1, kparts, None, vsrc, (6,), None, h, False):
                    for j in range(4):
                        o1 = pvacc(j)
                        sc.op("dve", lambda e, o1=o1, j=j: e.reciprocal(out=st2[:, 44 + j:45 + j], in_=o1[:, 128:129]), ["b%d" % (2 + j)], ["st2y"])
                        sc.op("dve", lambda e, o1=o1, j=j: e.tensor_scalar(out=odb4[:, j, :], in0=o1[:, 0:128], scalar1=st2[:, 44 + j:45 + j], scalar2=None, op0=ALU.mult),
                              ["b%d" % (2 + j), "st2y"], [("odb", j)])

                    def fin_b(h=h, qc=qc):
                        tp = bankbf(7)
                        for j in range(4):
                            sc.op("pe", lambda e, tp=tp, j=j: e.transpose(tp[:, j * 128:(j + 1) * 128], odb4[:, j, :], ident[:]), [("odb", j), "ident"], ["b7"], signal=(j == 3))
                        copy(evac_eng(), OT_mla[:, h, qc * 512:(qc + 1) * 512], tp[:, 0:512], ["b7"], [("dkT", h)])
                    pending.append(fin_b)
                while pending:
                    pending.pop(0)()
            sc.barrier()
            if stop_after == 2:
                break

            wo = view(OFF_A, BF16, 16 * 2048, "p (k c) -> p k c", k=16)
            wov = w_o.rearrange("p (k c) -> p k c", k=16)
            for mc in range(16):
                dma("pool", wo[:, mc, :], wov[:, mc, :], "wo%d" % (mc % 4), writes=["wo"])
            OFF_W = OFF_P + 65536
            gw = view(OFF_W, F32, 2048)
            gtmp = view(OFF_W + 8192, F32, 2048)
            ast = [view(OFF_W + 16384 + i * 16384, BF16, 16 * 512, "p (k c) -> p k c", k=16) for i in range(2)]
            xt3 = [view(OFF_W + 49152 + i * 8192, F32, 2048) for i in range(2)]
            yt3 = view(OFF_W + 65536, F32, 2048) if OFF_W + 65536 + 8192 <= ARENA else None
            junk = view(OFF_W + 16384, BF16, 2048)
            cbk = view(OFF_W + 65536, BF16, 16 * 128, "p (k c) -> p k c", k=16)

            def build_gw(ch, postw, gw_, gtmp_, ast_, cbk_):
                gw, gtmp, ast, cbk = gw_, gtmp_, ast_, cbk_
                for kc in range(16):
                    sc.op("dve", lambda e, kc=kc: e.tensor_scalar(out=cbk[:, kc, :], in0=onesb[:], scalar1=cact[:, kc:kc + 1],
                                                                 scalar2=None, op0=ALU.mult), ["onesb", "cact"], ["cactbc"])
                dma("sp", gtmp, ada_b[:, ch * D:(ch + 1) * D].partition_broadcast(128), "gtmp", writes=["gtmp"])
                for piece in range(4):
                    st = ast[piece % 2]; sk = "ast%d" % (piece % 2)
                    dma("pool", st, ada_r[ch * 4 + piece].rearrange("p (k c) -> p k c", k=16), sk, writes=[sk])
                    for kc in range(16):
                        sc.op("pe", lambda e, st=st, kc=kc, piece=piece: e.matmul(bank(piece), lhsT=cbk[:, kc, :], rhs=st[:, kc, :], start=(kc == 0), stop=(kc == 15)),
                              [sk, "cactbc"], ["b%d" % piece], signal=(kc == 15))
                    sc.op("dve", lambda e, piece=piece: e.tensor_tensor(out=gw[:, piece * 512:(piece + 1) * 512], in0=bank(piece), in1=gtmp[:, piece * 512:(piece + 1) * 512], op=ALU.add),
                          ["b%d" % piece, "gtmp"], ["gw"])
                dma("sp", gtmp, postw.partition_broadcast(128), "gtmp", reads=["gw"], writes=["gtmp"])
                sc.op("dve", lambda e: e.tensor_tensor(out=gw, in0=gw, in1=gtmp, op=ALU.mult), ["gw", "gtmp"], ["gw"])
                sc.barrier()

            build_gw(2, apost, gw, gtmp, ast, cbk)
            x1v = x1s.rearrange("(t p) d -> t p d", p=128)
            outv = out[lb * S:(lb + 1) * S, :].rearrange("(t p) d -> t p d", p=128)

            def post_norm_residual(tt, ysrc, ykeys, xin, xkey, dst_dram, dkey, bufs):
                gw, gtmp, junk = bufs
                for dch in range(4):
                    sc.op("act", lambda e, dch=dch: e.activation(out=junk[:, dch * 512:(dch + 1) * 512], in_=ysrc[dch], func=AF.Square, accum_out=stat[:, 32 + dch:33 + dch]),
                          [ykeys[dch]], ["junk", "stat_y"])
                sc.op("dve", lambda e: e.tensor_reduce(out=stat[:, 36:37], in_=stat[:, 32:36], axis=AX.X, op=ALU.add), ["stat_y"], ["stat_y2"])
                sc.op("dve", lambda e: e.tensor_scalar(out=stat[:, 36:37], in0=stat[:, 36:37], scalar1=1.0 / D, scalar2=EPS, op0=ALU.mult, op1=ALU.add), ["stat_y2"], ["stat_y2"])
                sc.op("act", lambda e: e.activation(out=stat[:, 36:37], in_=stat[:, 36:37], func=AF.Ln), ["stat_y2"], ["stat_y2"])
                sc.op("act", lambda e: e.activation(out=stat[:, 36:37], in_=stat[:, 36:37], func=AF.Exp, scale=-0.5), ["stat_y2"], ["stat_y2"])
                for dch in range(4):
                    sl = slice(dch * 512, (dch + 1) * 512)
                    sc.op("dve", lambda e, dch=dch, sl=sl: e.scalar_tensor_tensor(out=gtmp[:, sl], in0=ysrc[dch], scalar=stat[:, 36:37], in1=gw[:, sl], op0=ALU.mult, op1=ALU.mult),
                          [ykeys[dch], "stat_y2", "gw"], ["gtmp"])
                    sc.op("dve", lambda e, sl=sl: e.tensor_tensor(out=xin[:, sl], in0=xin[:, sl], in1=gtmp[:, sl], op=ALU.add), ["gtmp", xkey], [xkey])
                dma("sp", dst_dram, xin, "st_" + xkey, reads=[xkey], writes=[dkey])

            for tt in range(NT):
                b_ = tt % 2
                dma("sp", xt3[b_], xv[tt], "xt3%d" % b_, writes=["xt3%d" % b_])
                for dch in range(4):
                    for mc in range(16):
                        src = OT_mla[:, mc, tt * 128:(tt + 1) * 128] if mc < 8 else OT_diff[:, mc - 8, tt * 128:(tt + 1) * 128]
                        sc.op("pe", lambda e, src=src, mc=mc, dch=dch: e.matmul(bank(dch), lhsT=src, rhs=wo[:, mc, dch * 512:(dch + 1) * 512], start=(mc == 0), stop=(mc == 15)),
                              ["wo"], ["b%d" % dch], signal=(mc == 15))
                post_norm_residual(tt, [bank(d_) for d_ in range(4)], ["b%d" % d_ for d_ in range(4)], xt3[b_], "xt3%d" % b_, x1v[tt], ("x1s", tt), (gw, gtmp, junk))
            sc.barrier()
            if stop_after == 3:
                break

            gT = view(0, BF16, NFB * 1024, "p (f t) -> p f t", f=NFB)
            h2T = view(90112, BF16, 16 * 1024, "p (k t) -> p k t", k=16)
            wupb = [view(122880 + i * 8192, BF16, 16 * 256, "p (k c) -> p k c", k=16) for i in range(2)]
            tcv = [view(139264 + i * 4096, F32, 1024) for i in range(2)]
            wdnb = [view(147456 + i * 11264, BF16, NFB * 128, "p (f c) -> p f c", f=NFB) for i in range(2)]
            xt4 = [view(169984 + i * 8192, F32, 2048) for i in range(2)]
            xn4 = view(186368, BF16, 2048)
            gw = view(190464, F32, 2048)
            gtmp = view(139264, F32, 2048)
            ast = [view(0 + i * 16384, BF16, 16 * 512, "p (k c) -> p k c", k=16) for i in range(2)]
            junk = xn4
            assert 202752 <= ARENA
            cbk = view(198656, BF16, 16 * 128, "p (k c) -> p k c", k=16)
            build_gw(5, fpost, gw, gtmp, ast, cbk)
            cwv = cw[:].rearrange("p (b t) -> p b t", t=3)
            sc.op("dve", lambda e: e.memset(halo[:], 0.0), [], ["halo"])

            for H in range(2):
                tiles = list(range(8 * H, 8 * H + 8))
                halo_tiles = ([8 * H - 1] if H == 1 else []) + ([8 * H + 8] if H == 0 else [])
                for tt in halo_tiles + tiles:
                    b_ = tt % 2
                    dma("sp", xt4[b_], x1v[tt], "xt4%d" % b_, reads=[("x1s", tt)], writes=["xt4%d" % b_])
                    sc.op("act", lambda e, b_=b_: e.activation(out=xn4, in_=xt4[b_], func=AF.Square, accum_out=stat[:, 40:41]), ["xt4%d" % b_], ["xn4", "stat_f"])
                    sc.op("dve", lambda e: e.tensor_scalar(out=stat[:, 40:41], in0=stat[:, 40:41], scalar1=D * EPS, scalar2=None, op0=ALU.add), ["stat_f"], ["stat_f"])
                    sc.op("act", lambda e: e.activation(out=stat[:, 40:41], in_=stat[:, 40:41], func=AF.Ln), ["stat_f"], ["stat_f"])
                    sc.op("act", lambda e: e.activation(out=stat[:, 40:41], in_=stat[:, 40:41], func=AF.Exp, scale=-0.5), ["stat_f"], ["stat_f"])
                    sc.op("dve", lambda e, b_=b_: e.tensor_scalar(out=xn4, in0=xt4[b_], scalar1=stat[:, 40:41], scalar2=None, op0=ALU.mult), ["xt4%d" % b_, "stat_f"], ["xn4"])
                    for g in range(2):
                        pb = 6 + g
                        pv = bankbf(pb)
                        for j in range(8):
                            kc = g * 8 + j
                            sc.op("pe", lambda e, pv=pv, j=j, kc=kc: e.transpose(pv[:, j * 128:(j + 1) * 128], xn4[:, kc * 128:(kc + 1) * 128], ident[:]),
                                  ["xn4", "ident"], ["b%d" % pb], signal=(j == 7))
                        for j in range(8):
                            kc = g * 8 + j
                            if tt in tiles:
                                lt = tt - 8 * H
                                dst = h2T[:, kc, lt * 128:(lt + 1) * 128]; src = pv[:, j * 128:(j + 1) * 128]; wk = "h2T"
                            elif tt < 8 * H:
                                dst = halo[:, kc * 2:kc * 2 + 1]; src = pv[:, j * 128 + 127:j * 128 + 128]; wk = "halo"
                            else:
                                dst = halo[:, kc * 2 + 1:kc * 2 + 2]; src = pv[:, j * 128:j * 128 + 1]; wk = "halo"
                            sc.op("dve", lambda e, dst=dst, src=src, kc=kc: e.tensor_scalar(out=dst, in0=src, scalar1=a_ffn[:, kc:kc + 1], scalar2=msc[:, 32 + kc:33 + kc],
                                                                                          op0=ALU.mult, op1=ALU.add), ["b%d" % pb, "a_ffn", "msc"], [wk])
                if H == 1:
                    sc.op("dve", lambda e: e.memset(halo[:].rearrange("p (k c) -> p k c", c=2)[:, :, 1:2], 0.0), [], ["halo"])
                hv = halo[:].rearrange("p (k c) -> p k c", c=2)
                for fb in range(NFB):
                    wb = wupb[fb % 2]; wk = "wup%d" % (fb % 2)
                    dma("pool", wb, w_up[fb].rearrange("p (k c) -> p k c", k=16), wk, writes=[wk])
                    for which in range(2):
                        blk = which * NFB + fb
                        up = psb[which]
                        uks = ["b%d" % (2 * which), "b%d" % (2 * which + 1)]
                        hk = "b%d" % (4 + which)
                        for tc in range(2):
                            for kc in range(16):
                                sc.op("pe", lambda e, wb=wb, kc=kc, tc=tc, up=up, which=which: e.matmul(
                                    up[:, tc * 512:(tc + 1) * 512], lhsT=wb[:, kc, which * 128:(which + 1) * 128], rhs=h2T[:, kc, tc * 512:(tc + 1) * 512],
                                    start=(kc == 0), stop=(kc == 15)), [wk, "h2T"], [uks[tc]], signal=(kc == 15))
                        hp = bank(4 + which, 2)
                        for kc in range(16):
                            sc.op("pe", lambda e, wb=wb, kc=kc, hp=hp, which=which: e.matmul(hp, lhsT=wb[:, kc, which * 128:(which + 1) * 128], rhs=hv[:, kc, :],
                                                                                          start=(kc == 0), stop=(kc == 15)), [wk, "halo"], [hk], signal=(kc == 15))
                        t_ = tcv[which]; tk = "tcv%d" % which
                        sc.op("act", lambda e, t_=t_, up=up, blk=blk: e.activation(out=t_, in_=up[:, :], func=AF.Identity, scale=cwv[:, blk, 1:2], bias=cb[:, blk:blk + 1]),
                              uks + ["cw", "cb"], [tk])
                        sc.op("dve", lambda e, t_=t_, up=up, blk=blk: e.scalar_tensor_tensor(out=t_[:, 1:1024], in0=up[:, 0:1023], scalar=cwv[:, blk, 0:1], in1=t_[:, 1:1024],
                                                                                           op0=ALU.mult, op1=ALU.add), uks + ["cw", tk], [tk])
                        sc.op("dve", lambda e, t_=t_, up=up, blk=blk: e.scalar_tensor_tensor(out=t_[:, 0:1023], in0=up[:, 1:1024], scalar=cwv[:, blk, 2:3], in1=t_[:, 0:1023],
                                                                                           op0=ALU.mult, op1=ALU.add), uks + ["cw", tk], [tk])
                        sc.op("dve", lambda e, t_=t_, hp=hp, blk=blk: e.scalar_tensor_tensor(out=t_[:, 0:1], in0=hp[:, 0:1], scalar=cwv[:, blk, 0:1], in1=t_[:, 0:1],
                                                                                           op0=ALU.mult, op1=ALU.add), [hk, "cw", tk], [tk])
                        sc.op("dve", lambda e, t_=t_, hp=hp, blk=blk: e.scalar_tensor_tensor(out=t_[:, 1023:1024], in0=hp[:, 1:2], scalar=cwv[:, blk, 2:3], in1=t_[:, 1023:1024],
                                                                                           op0=ALU.mult, op1=ALU.add), [hk, "cw", tk], [tk])
                    sc.op("act", lambda e: e.activation(out=tcv[0], in_=tcv[0], func=AF.Gelu_apprx_tanh), ["tcv0"], ["tcv0"])
                    sc.op("dve", lambda e, fb=fb: e.tensor_tensor(out=gT[:, fb, :], in0=tcv[0], in1=tcv[1], op=ALU.mult), ["tcv0", "tcv1"], ["gT"])
                yT = h2T
                for db in range(16):
                    wb = wdnb[db % 2]; wk = "wdn%d" % (db % 2)
                    dma("pool", wb, w_dn[db].rearrange("p (f c) -> p f c", f=NFB), wk, writes=[wk])
                    for tc in range(2):
                        pb = (db * 2 + tc) % 4
                        for fb in range(NFB):
                            sc.op("pe", lambda e, wb=wb, fb=fb, tc=tc, pb=pb: e.matmul(bank(pb), lhsT=wb[:, fb, :], rhs=gT[:, fb, tc * 512:(tc + 1) * 512],
                                                                                    start=(fb == 0), stop=(fb == NFB - 1)), [wk, "gT"], ["b%d" % pb], signal=(fb == NFB - 1))
                        copy(evac_eng(), yT[:, db, tc * 512:(tc + 1) * 512], bank(pb), ["b%d" % pb], ["h2T"])
                for lt in range(8):
                    tt = 8 * H + lt
                    b_ = tt % 2
                    dma("sp", xt4[b_], x1v[tt], "xt4%d" % b_, reads=[("x1s", tt)], writes=["xt4%d" % b_])
                    ysrc = []; ykeys = []
                    for g in range(2):
                        pb = 4 + g * 2
                        pb = 4 + g
                        pv = bankbf(pb)
                        for j in range(8):
                            db = g * 8 + j
                            sc.op("pe", lambda e, pv=pv, j=j, db=db, lt=lt: e.transpose(pv[:, j * 128:(j + 1) * 128], yT[:, db, lt * 128:(lt + 1) * 128], ident[:]),
                                  ["h2T", "ident"], ["b%d" % pb], signal=(j == 7))
                        ysrc += [pv[:, 0:512], pv[:, 512:1024]]
                        ykeys += ["b%d" % pb, "b%d" % pb]
                    post_norm_residual(tt, ysrc, ykeys, xt4[b_], "xt4%d" % b_, outv[tt], ("out", tt), (gw, gtmp, junk))
                sc.barrier()

        sc.finish()
        stuck, val = sc.simulate()
        if stuck:
            raise RuntimeError("semaphore dead-lock in recorded program: %r  sems=%r" % (stuck, {k: v for k, v in val.items() if not k.startswith("d:")}))
        build.sem_counts = {k: v for k, v in val.items()}
        sc.replay(E(nc.Block()))
    return nc


_CACHE = {}


def _prep_shared(inp):
    f = lambda a: np.ascontiguousarray(a, dtype=np.float32)
    w_in = inp["w_in"][0]
    fm_cols = list(range(0, 768)) + list(range(832, 2880))
    w_fm = w_in[:, fm_cols].reshape(16, 128, 22, 128).transpose(2, 1, 0, 3).reshape(22, 128, 16 * 128)
    w_kr = w_in[:, 768:832].reshape(16, 128, 64).transpose(1, 0, 2).reshape(128, 16 * 64)
    w_dv = w_in[:, 2880:3904].reshape(16, 128, 8, 128).transpose(2, 1, 0, 3).reshape(8, 128, 16 * 128)
    wuq = inp["w_uq"][0].reshape(512, 8, 192)
    wuq_r = np.concatenate([wuq[:, :, :128].reshape(512, 1024), wuq[:, :, 128:].reshape(512, 512)], axis=1)
    wuq_r = wuq_r.reshape(4, 128, 1536).transpose(1, 0, 2).reshape(128, 4 * 1536)
    wukv = inp["w_ukv"][0].reshape(256, 8, 256)
    wukv_r = np.concatenate([wukv[:, :, :128].reshape(256, 1024), wukv[:, :, 128:].reshape(256, 1024)], axis=1)
    wukv_r = wukv_r.reshape(2, 128, 2048).transpose(1, 0, 2).reshape(128, 2 * 2048)
    w_o = inp["w_o"][0].reshape(16, 128, 2048).transpose(1, 0, 2).reshape(128, 16 * 2048)
    wup = inp["w_up"][0].reshape(16, 128, 2, NFB, 128).transpose(3, 1, 0, 2, 4).reshape(NFB, 128, 16 * 256)
    cw = inp["conv_w"][0].reshape(3, 88, 128).transpose(2, 1, 0).reshape(128, 88 * 3)
    cb = inp["conv_b"][0].reshape(88, 128).T
    wdn = inp["w_down"][0].reshape(NFB, 128, 16, 128).transpose(2, 1, 0, 3).reshape(16, 128, NFB * 128)
    inv = (1.0 / (np.float32(ROPE_THETA) ** (np.arange(0, 64, 2, dtype=np.float32) / np.float32(64)))).astype(np.float32)
    selh = np.zeros((8, 8, 128), np.float32)
    for h in range(8):
        selh[h, h, :] = 1.0
    d = {
        "relb": f(inp["rel_bias"]), "relbT": f(inp["rel_bias"].T), "ident": np.eye(128, dtype=np.float32),
        "nvals": f(np.broadcast_to(255.0 - np.arange(512, dtype=np.float32), (8, 512))),
        "selh": f(selh.reshape(8, 1024)), "invf": f(np.broadcast_to(inv, (128, 32))),
        "ada_r": f(inp["ada_w"][0].reshape(16, 128, 24, 512).transpose(2, 1, 0, 3).reshape(24, 128, 16 * 512)), "ada_bT": f(inp["ada_b"][0].reshape(96, 128).T), "ada_b": f(inp["ada_b"]),
        "apnT": f(inp["attn_pre_norm"][0].reshape(16, 128).T), "fpnT": f(inp["ffn_pre_norm"][0].reshape(16, 128).T),
        "apost": f(inp["attn_post_norm"]), "fpost": f(inp["ffn_post_norm"]),
        "w_fm": f(w_fm), "w_kr": f(w_kr), "w_dv": f(w_dv),
        "qnT": f(inp["q_norm"][0].reshape(4, 128).T), "kvnT": f(inp["kv_norm"][0].reshape(2, 128).T),
        "w_uq": f(wuq_r), "w_ukv": f(wukv_r),
        "lamv": f(np.concatenate([inp["lambda_q1"][0], inp["lambda_k1"][0], inp["lambda_q2"][0], inp["lambda_k2"][0]])[None, :]),
        "sublnT": f(inp["diff_subln"][0][:, None]),
        "w_o": f(w_o), "w_up": f(wup), "cwT": f(cw), "cbT": f(cb), "w_dn": f(wdn),
    }
    return d


N_CORES = 8
NB = 8 // N_CORES


def _core_inputs(inp, shared, batches):
    m = dict(shared)
    m["x"] = np.ascontiguousarray(np.concatenate([inp["x"][b] for b in batches], axis=0), dtype=np.float32)
    m["cT"] = np.ascontiguousarray(np.concatenate([inp["c"][b].reshape(16, 128).T for b in batches], axis=1), dtype=np.float32)
    m["pos"] = np.ascontiguousarray(np.concatenate([inp["positions"][b].reshape(16, 128).T for b in batches], axis=1), dtype=np.int32)
    return m


def kernel(**inp):
    inp = {k: np.asarray(v) for k, v in inp.items()}
    if "nc" not in _CACHE:
        _CACHE["nc"] = build(nb=NB)
    nc = _CACHE["nc"]
    shared = _prep_shared(inp)
    in_maps = [_core_inputs(inp, shared, list(range(i * NB, (i + 1) * NB))) for i in range(N_CORES)]
    res = run_bass_kernel_spmd(nc, in_maps, core_ids=list(range(N_CORES)))
    outs = [np.asarray(r["out"], dtype=np.float32).reshape(NB, S, D) for r in res.results]
    return np.concatenate(outs, axis=0)
```

```python
import math
import os
from contextlib import ExitStack

import numpy as np
import concourse.bass as bass
import concourse.mybir as mybir
from concourse.bass_utils import run_bass_kernel_spmd

F32 = mybir.dt.float32
BF16 = mybir.dt.bfloat16
I32 = mybir.dt.int32
U8 = mybir.dt.uint8
AF = mybir.ActivationFunctionType
ALU = mybir.AluOpType
AX = mybir.AxisListType

D = 2048
S = 2048
NT = 16
DFF = 5632
NFB = 44
EPS = 1e-6
MLA_SCALE = 192 ** -0.5
DIFF_SCALE = 64 ** -0.5
LAMBDA_INIT = 0.8 - 0.6 * math.exp(0.0)
ROPE_THETA = 10000.0

BUCKETS = {0: (0, 0), 8: (-11, -8), 9: (-15, -12), 10: (-22, -16), 11: (-31, -23), 12: (-45, -32),
           13: (-63, -46), 14: (-90, -64), 15: (-100000, -91), 24: (8, 11), 25: (12, 15), 26: (16, 22),
           27: (23, 31), 28: (32, 45), 29: (46, 63), 30: (64, 90), 31: (91, 100000)}
for _b in range(1, 8):
    BUCKETS[_b] = (-_b, -_b)
    BUCKETS[16 + _b] = (_b, _b)


SHARED = [("relb", (32, 8)), ("relbT", (8, 32)), ("ident", (128, 128)), ("nvals", (8, 512)), ("selh", (8, 1024)), ("invf", (128, 32)),
          ("ada_r", (24, 128, 16 * 512)), ("ada_bT", (128, 96)), ("ada_b", (1, 6 * D)), ("apnT", (128, 16)), ("fpnT", (128, 16)),
          ("apost", (1, D)), ("fpost", (1, D)), ("w_fm", (22, 128, 2048)), ("w_kr", (128, 1024)), ("w_dv", (8, 128, 2048)),
          ("qnT", (128, 4)), ("kvnT", (128, 2)), ("w_uq", (128, 4 * 1536)), ("w_ukv", (128, 2 * 2048)), ("lamv", (1, 256)),
          ("sublnT", (128, 1)), ("w_o", (128, 16 * 2048)), ("w_up", (NFB, 128, 16 * 256)), ("cwT", (128, 88 * 3)), ("cbT", (128, 88)),
          ("w_dn", (16, 128, NFB * 128))]
CW = 2048


def _layout():
    off = [0, 0]
    offs = {}
    g = 0
    for name, shp in SHARED:
        if name == "lamv":
            g = 1
        offs[name] = (g, off[g])
        n = int(np.prod(shp))
        off[g] += (n + 63) // 64 * 64
    rows = [(o + 8 * CW - 1) // (8 * CW) for o in off]
    return offs, rows


class Sched:
    COMPUTE = ("pe", "act", "dve")
    QUEUES = ("sp", "pool")

    def __init__(self, nc, es):
        self.nc = nc
        self.es = es
        self.streams = {e: [] for e in self.COMPUTE + self.QUEUES}
        self.semobj = {e: es.enter_context(nc.semaphore("s_" + e)) for e in self.COMPUTE}
        self.cnt = {e: 0 for e in self.COMPUTE}
        self.dcnt = {}
        self.waited = {e: {} for e in self.COMPUTE + self.QUEUES}
        self.lastw = {}
        self.readers = {}

    def _dsem(self, name):
        k = "d:" + name
        if k not in self.semobj:
            self.semobj[k] = self.es.enter_context(self.nc.semaphore("q_" + name))
            self.dcnt[k] = 0
        return k

    def _wait(self, eng, tok):
        k, v = tok
        if self.waited[eng].get(k, 0) < v:
            self.waited[eng][k] = v
            self.streams[eng].append(("w", k, v))

    def op(self, eng, fn, reads=(), writes=(), signal=True, dma=None):
        deps = []
        for r in reads:
            if r in self.lastw:
                deps.append(self.lastw[r])
        for w in writes:
            if w in self.lastw:
                deps.append(self.lastw[w])
            deps.extend(self.readers.get(w, ()))
        for t in deps:
            if eng == "pe" and t[0] == "pe":
                continue
            self._wait(eng, t)
        if dma is not None:
            k = self._dsem(dma)
            self.dcnt[k] += 16
            tok = (k, self.dcnt[k])
            self.streams[eng].append(("i", fn, k, 16))
        else:
            if signal:
                self.cnt[eng] += 1
                tok = (eng, self.cnt[eng])
                self.streams[eng].append(("i", fn, eng, 1))
            else:
                tok = (eng, self.cnt[eng] + 1)
                self.streams[eng].append(("i", fn, None, 0))
        for w in writes:
            self.lastw[w] = tok
            self.readers[w] = []
        for r in reads:
            self.readers.setdefault(r, []).append(tok)
        return tok

    def barrier(self):
        toks = [(e, self.cnt[e]) for e in self.COMPUTE if self.cnt[e] > 0]
        toks += [(k, v) for k, v in self.dcnt.items() if v > 0]
        for e in self.COMPUTE + self.QUEUES:
            for t in toks:
                if not (e == "pe" and t[0] == "pe"):
                    self._wait(e, t)
        self.lastw.clear()
        self.readers.clear()

    def finish(self):
        for k, v in self.dcnt.items():
            if v > 0:
                self._wait("sp", (k, v))

    def simulate(self):
        val = {k: 0 for k in self.semobj}
        pc = {e: 0 for e in self.streams}
        progress = True
        while progress:
            progress = False
            for e, st in self.streams.items():
                while pc[e] < len(st):
                    a = st[pc[e]]
                    if a[0] == "w":
                        if val[a[1]] < a[2]:
                            break
                    elif a[2] is not None:
                        val[a[2]] += a[3]
                    pc[e] += 1
                    progress = True
        stuck = {e: (pc[e], len(st), st[pc[e]][:3] if pc[e] < len(st) else None) for e, st in self.streams.items() if pc[e] < len(st)}
        return stuck, val

    def replay(self, block):
        def mk(stream):
            def body(e):
                for a in stream:
                    if a[0] == "w":
                        e.wait_ge(self.semobj[a[1]], a[2])
                    else:
                        ins = a[1](e)
                        if a[2] is not None:
                            ins.then_inc(self.semobj[a[2]], a[3])
            return body
        block.tensor(mk(self.streams["pe"]))
        block.scalar(mk(self.streams["act"]))
        block.vector(mk(self.streams["dve"]))
        block.sync(mk(self.streams["sp"]))
        block.gpsimd(mk(self.streams["pool"]))


def build(nb=1, stop_after=99, dbg=False):
    nc = bass.Bass("TRN2", target_bir_lowering=False)

    skip = set()
    if stop_after in (12, 13, 14, 15, 16):
        skip |= {"w_up", "w_dn", "fpost", "w_o", "apost", "ada_b"}
    elif stop_after < 4:
        skip |= {"w_up", "w_dn", "fpost"}
    if stop_after < 3:
        skip |= {"w_o", "apost", "ada_b"}
    if stop_after < 2:
        skip |= {"w_uq", "w_ukv"}

    def din(name, shape, dt=F32):
        if name in skip:
            return None
        return nc.dram_tensor(name, list(shape), dt, kind="ExternalInput").ap()

    x = din("x", [nb * S, D])
    cT = din("cT", [128, nb * 16])
    pos = din("pos", [128, nb * 16], I32)
    shp = dict(SHARED)
    relb = din("relb", shp["relb"]); relbTd = din("relbT", shp["relbT"]); identd = din("ident", shp["ident"]); nvals = din("nvals", shp["nvals"])
    selh = din("selh", shp["selh"]); invf = din("invf", shp["invf"]); ada_r = din("ada_r", shp["ada_r"]); ada_bT = din("ada_bT", shp["ada_bT"])
    ada_b = din("ada_b", shp["ada_b"]); apnT = din("apnT", shp["apnT"]); fpnT = din("fpnT", shp["fpnT"]); apost = din("apost", shp["apost"])
    fpost = din("fpost", shp["fpost"]); w_fm = din("w_fm", shp["w_fm"]); w_kr = din("w_kr", shp["w_kr"]); w_dv = din("w_dv", shp["w_dv"])
    qnT = din("qnT", shp["qnT"]); kvnT = din("kvnT", shp["kvnT"]); w_uq = din("w_uq", shp["w_uq"]); w_ukv = din("w_ukv", shp["w_ukv"])
    lamv = din("lamv", shp["lamv"]); sublnT = din("sublnT", shp["sublnT"]); w_o = din("w_o", shp["w_o"]); w_up = din("w_up", shp["w_up"])
    cwT = din("cwT", shp["cwT"]); cbT = din("cbT", shp["cbT"]); w_dn = din("w_dn", shp["w_dn"])
    out = nc.dram_tensor("out", [nb * S, D], F32, kind="ExternalOutput").ap()
    x1s = nc.dram_tensor("x1s", [S, D], F32, kind="ExternalOutput" if dbg else "Internal").ap()
    bsc = nc.dram_tensor("bsc", [8, 128 * 512], F32, kind="Internal").ap()
    dbg_out = nc.dram_tensor("dbg", [128, 4096], F32, kind="ExternalOutput").ap() if dbg else None

    with ExitStack() as es:
        E = es.enter_context
        ARENA = 202752
        arena = E(nc.sbuf_tensor("arena", [128, ARENA], U8))

        def view(off, dt, n, pat=None, **kw):
            sz = {F32: 4, BF16: 2, I32: 4}[dt]
            v = arena[:, off:off + n * sz].bitcast(dt)
            if pat:
                v = v.rearrange(pat, **kw)
            return v

        def sb(name, shape, dt=F32):
            return E(nc.sbuf_tensor(name, list(shape), dt))

        ident = sb("identb", [128, 128], BF16)
        onesb = sb("onesb", [128, 128], BF16)
        sel1 = sb("sel1", [128, 128], BF16)
        sel2 = sb("sel2", [128, 128], BF16)
        cact = sb("cact", [128, 16], BF16)
        msc = sb("msc", [128, 64])
        adab = sb("adab", [128, 96])
        apn = sb("apn", [128, 16]); fpn = sb("fpn", [128, 16])
        a_pre = sb("a_pre", [128, 16]); a_ffn = sb("a_ffn", [128, 16])
        qnw = sb("qnw", [128, 4]); kvnw = sb("kvnw", [128, 2])
        posi = sb("posi", [128, 16], I32)
        posf = sb("posf", [128, 16])
        invt = sb("invt", [128, 32])
        cost = sb("cost", [128, 16 * 32]); sint = sb("sint", [128, 16 * 32])
        tabbc = sb("tabbc", [128, 16])
        lamb = sb("lamb", [128, 256])
        lamt = sb("lamt", [128, 8])
        subw = sb("subw", [128, 1])
        cw = sb("cw", [128, 88 * 3]); cb = sb("cb", [128, 88])
        stat = sb("stat", [128, 64])
        halo = sb("halo", [128, 16 * 2], BF16)
        st2 = sb("st2", [128, 64])
        psb = [E(nc.psum_tensor("pp%d" % i, [128, 1024], F32)) for i in range(4)]

        def bank(b, n=512, off=0):
            return psb[b // 2][:, (b % 2) * 512 + off:(b % 2) * 512 + off + n]

        def bankbf(b):
            return psb[b // 2][:, (b % 2) * 512:(b % 2) * 512 + 512].bitcast(BF16)

        sc = Sched(nc, es)
        rr = [0]

        def evac_eng():
            rr[0] ^= 1
            return "act" if rr[0] else "dve"

        def copy(eng, out_, in_, reads, writes):
            if eng == "act":
                sc.op("act", lambda e: e.activation(out=out_, in_=in_, func=AF.Copy), reads, writes)
            else:
                sc.op("dve", lambda e: e.tensor_copy(out=out_, in_=in_), reads, writes)

        def dma(q, out_, in_, name, reads=(), writes=()):
            sc.op(q, lambda e: e.dma_start(out=out_, in_=in_), reads, writes, dma=name)

        OFF_A = 0
        OFF_P = 65536
        hT = view(OFF_A, BF16, 16 * 2048, "p (k t) -> p k t", k=16)
        dqT = view(OFF_P, BF16, 8 * 2048, "p (h t) -> p h t", h=8)
        dkT = view(OFF_P + 32768, BF16, 8 * 2048, "p (h t) -> p h t", h=8)
        dvaug = view(OFF_P + 65536, BF16, 16 * 8 * 132, "p (t h c) -> p t h c", t=16, h=8)
        OFF_L = OFF_P + 65536 + 33792
        qlatT = view(OFF_L, BF16, 4 * 2048, "p (k t) -> p k t", k=4)
        kvlatT = view(OFF_L + 16384, BF16, 2 * 2048, "p (k t) -> p k t", k=2)
        kropeT = view(OFF_L + 24576, BF16, 2048)
        OFF_S = OFF_L + 28672
        wst = [view(OFF_S + i * 4096, BF16, 16 * 128, "p (k c) -> p k c", k=16) for i in range(2)]
        xt = [view(OFF_P + i * 8192, F32, 2048) for i in range(2)]
        xn = [view(OFF_P + 16384 + i * 4096, BF16, 2048) for i in range(2)]
        adast = [view(OFF_P + 24576 + i * 16384, BF16, 16 * 512, "p (k c) -> p k c", k=16) for i in range(2)]
        zt = view(OFF_P + 57344, F32, 512)
        ztm = view(OFF_P + 59392, F32, 512)
        ztm2 = view(OFF_P + 61440, F32, 512)
        nv = view(OFF_P + 63488, F32, 512)
        selhs = view(OFF_P + 65536, F32, 1024)
        zrep = view(OFF_P + 69632, F32, 512)
        ang = view(OFF_P + 71680, F32, 512)
        angq = view(OFF_P + 73728, F32, 512)
        angi = view(OFF_P + 75776, I32, 512)
        idf = view(OFF_P + 77824, F32, 128)

        for lb in range(nb):
            if lb == 0:
                dma("sp", idf, identd, "c0", writes=["idf"])
                sc.op("dve", lambda e: e.tensor_copy(out=ident[:], in_=idf), ["idf"], ["ident"])
                sc.op("dve", lambda e: e.memset(onesb[:], 1.0), [], ["onesb"])
                sc.op("dve", lambda e: e.memset(sel1[:], 0.0), [], ["sel1"])
                sc.op("dve", lambda e: e.memset(sel1[0:64, :], 1.0), [], ["sel1"])
                sc.op("dve", lambda e: e.memset(sel2[:], 0.0), [], ["sel2"])
                sc.op("dve", lambda e: e.memset(sel2[64:128, :], 1.0), [], ["sel2"])
                sc.op("dve", lambda e: e.memset(halo[:], 0.0), [], ["halo"])
                small = [(adab, ada_bT), (apn, apnT), (fpn, fpnT), (qnw, qnT), (kvnw, kvnT), (invt, invf),
                         (subw, sublnT), (cw, cwT), (cb, cbT)]
                for i, (d_, s_) in enumerate(small):
                    dma("sp", d_[:], s_, "c%d" % (1 + i), writes=[d_.name])
            dma("sp", posi[:], pos[:, lb * 16:(lb + 1) * 16], "c10", writes=["posi"])
            if lb == 0:
                dma("sp", tabbc[:, 0:8], relb[15:16, :].partition_broadcast(128), "c11", writes=["tabbc0"])
                dma("sp", tabbc[:, 8:16], relb[31:32, :].partition_broadcast(128), "c12", writes=["tabbc1"])
                dma("sp", lamb[:], lamv.partition_broadcast(128), "c13", writes=["lamb"])
            ctf = view(OFF_P + 78848, F32, 16)
            dma("sp", ctf, cT[:, lb * 16:(lb + 1) * 16], "c14", writes=["ctf"])
            sc.op("act", lambda e: e.activation(out=cact[:], in_=ctf, func=AF.Silu), ["ctf"], ["cact"])

            mps = bank(0, 64)
            k_ada = [0]

            def ada_piece(ci, ch, piece):
                k_ = k_ada[0]
                st = adast[k_ % 2]
                dma("pool", st, ada_r[ch * 4 + piece].rearrange("p (k c) -> p k c", k=16), "ada%d" % (k_ % 2), writes=["adast%d" % (k_ % 2)])
                for fcl in range(4):
                    col = ci * 16 + piece * 4 + fcl
                    for kc in range(16):
                        sc.op("pe", lambda e, st=st, fcl=fcl, kc=kc, col=col: e.matmul(
                            mps[:, col:col + 1], lhsT=st[:, kc, fcl * 128:(fcl + 1) * 128], rhs=cact[:, kc:kc + 1],
                            start=(kc == 0), stop=(kc == 15)),
                            ["adast%d" % (k_ % 2), "cact"], ["mps"], signal=(kc == 15))
                k_ada[0] += 1

            SQD = math.sqrt(D)

            def ada_finish(pairs, dst, wn, ci_scale):
                for ci, ch in pairs:
                    sc.op("dve", lambda e, ci=ci, ch=ch: e.tensor_tensor(out=msc[:, ci * 16:(ci + 1) * 16], in0=mps[:, ci * 16:(ci + 1) * 16],
                                                                        in1=adab[:, ch * 16:(ch + 1) * 16], op=ALU.add),
                          ["mps", "adab"], [("msc", ci)])
                sc.op("dve", lambda e: e.scalar_tensor_tensor(
                    out=dst[:], in0=msc[:, ci_scale * 16:(ci_scale + 1) * 16], scalar=1.0, in1=wn[:], op0=ALU.add, op1=ALU.mult),
                    [("msc", ci_scale), wn.name], [dst.name])
                sc.op("dve", lambda e: e.tensor_scalar(out=dst[:], in0=dst[:], scalar1=SQD, scalar2=None, op0=ALU.mult),
                      [dst.name], [dst.name])

            for ci, ch in ((0, 0), (1, 1)):
                for piece in range(4):
                    ada_piece(ci, ch, piece)
            ada_finish(((0, 0), (1, 1)), a_pre, apn, 1)
            ada_later = [(ci, ch, piece) for ci, ch in ((2, 3), (3, 4)) for piece in range(4)]

            if lb == 0:
                sc.op("dve", lambda e: e.tensor_tensor(out=lamb[:, 0:64], in0=lamb[:, 0:64], in1=lamb[:, 64:128], op=ALU.mult), ["lamb"], ["lamb"])
                sc.op("dve", lambda e: e.tensor_tensor(out=lamb[:, 128:192], in0=lamb[:, 128:192], in1=lamb[:, 192:256], op=ALU.mult), ["lamb"], ["lamb"])
                sc.op("dve", lambda e: e.tensor_reduce(out=lamt[:, 0:1], in_=lamb[:, 0:64], axis=AX.X, op=ALU.add), ["lamb"], ["lamt"])
                sc.op("dve", lambda e: e.tensor_reduce(out=lamt[:, 1:2], in_=lamb[:, 128:192], axis=AX.X, op=ALU.add), ["lamb"], ["lamt"])
                sc.op("act", lambda e: e.activation(out=lamt[:, 2:4], in_=lamt[:, 0:2], func=AF.Exp), ["lamt"], ["lamt"])
                sc.op("dve", lambda e: e.scalar_tensor_tensor(out=lamt[:, 4:5], in0=lamt[:, 3:4], scalar=-LAMBDA_INIT, in1=lamt[:, 2:3],
                                                              op0=ALU.add, op1=ALU.subtract), ["lamt"], ["lamt"])
                sc.op("dve", lambda e: e.tensor_scalar(out=subw[:], in0=subw[:], scalar1=1.0 - LAMBDA_INIT, scalar2=None, op0=ALU.mult),
                      ["subw"], ["subw"])

            sc.op("dve", lambda e: e.tensor_copy(out=posf[:], in_=posi[:]), ["posi"], ["posf"])
            for tt in range(16):
                sc.op("dve", lambda e, tt=tt: e.tensor_scalar(out=ang[:, tt * 32:(tt + 1) * 32], in0=invt[:], scalar1=posf[:, tt:tt + 1],
                                                             scalar2=None, op0=ALU.mult), ["posf", "invt"], ["ang"])
            C1 = 6.28125
            C2 = 2 * math.pi - C1
            for (dst, shift) in ((sint, 0.0), (cost, math.pi / 2)):
                sc.op("dve", lambda e, shift=shift: e.tensor_scalar(out=angq, in0=ang, scalar1=shift, scalar2=1.0 / (2 * math.pi),
                                                                   op0=ALU.add, op1=ALU.mult), ["ang"], ["angq"])
                sc.op("dve", lambda e: e.tensor_copy(out=angi, in_=angq), ["angq"], ["angi"])
                sc.op("dve", lambda e: e.tensor_copy(out=angq, in_=angi), ["angi"], ["angq"])
                sc.op("dve", lambda e, shift=shift, dst=dst: e.tensor_scalar(out=dst[:], in0=ang, scalar1=shift, scalar2=None, op0=ALU.add),
                      ["ang"], [dst.name])
                sc.op("dve", lambda e, dst=dst: e.scalar_tensor_tensor(out=dst[:], in0=angq, scalar=-C1, in1=dst[:], op0=ALU.mult, op1=ALU.add),
                      ["angq", dst.name], [dst.name])
                sc.op("dve", lambda e, dst=dst: e.scalar_tensor_tensor(out=dst[:], in0=angq, scalar=-C2, in1=dst[:], op0=ALU.mult, op1=ALU.add),
                      ["angq", dst.name], [dst.name])
                sc.op("dve", lambda e, dst=dst: e.tensor_scalar(out=angq, in0=dst[:], scalar1=math.pi, scalar2=-2 * math.pi, op0=ALU.is_gt, op1=ALU.mult),
                      [dst.name], ["angq"])
                sc.op("dve", lambda e, dst=dst: e.tensor_tensor(out=dst[:], in0=dst[:], in1=angq, op=ALU.add), ["angq", dst.name], [dst.name])
                sc.op("dve", lambda e, dst=dst: e.tensor_scalar(out=angq, in0=dst[:], scalar1=-math.pi, scalar2=2 * math.pi, op0=ALU.is_lt, op1=ALU.mult),
                      [dst.name], ["angq"])
                sc.op("dve", lambda e, dst=dst: e.tensor_tensor(out=dst[:], in0=dst[:], in1=angq, op=ALU.add), ["angq", dst.name], [dst.name])
                sc.op("dve", lambda e, dst=dst: e.tensor_scalar(out=dst[:], in0=dst[:], scalar1=math.pi, scalar2=-math.pi, op0=ALU.min, op1=ALU.max),
                      [dst.name], [dst.name])
                sc.op("act", lambda e, dst=dst: e.activation(out=dst[:], in_=dst[:], func=AF.Sin), [dst.name], [dst.name])

            if lb == 0:
                dma("sp", nv[0:8, :], nvals, "c15", writes=["nv"])
                dma("sp", selhs[0:8, :], selh, "c16", writes=["selhs"])
                relbT = view(OFF_P + 79872, F32, 32)
                dma("sp", relbT[0:8, :], relbTd, "c17", writes=["relbT"])
                sc.op("dve", lambda e: e.memset(zt[0:8, :], 0.0), [], ["zt"])
                for b_, (lo, hi) in sorted(BUCKETS.items()):
                    sc.op("dve", lambda e, lo=lo: e.tensor_scalar(out=ztm[0:8, :], in0=nv[0:8, :], scalar1=float(lo) - 0.5, scalar2=None, op0=ALU.is_gt),
                          ["nv"], ["ztm"])
                    sc.op("dve", lambda e, hi=hi: e.scalar_tensor_tensor(out=ztm2[0:8, :], in0=nv[0:8, :], scalar=float(hi) + 0.5, in1=ztm[0:8, :],
                                                                        op0=ALU.is_lt, op1=ALU.mult), ["nv", "ztm"], ["ztm2"])
                    sc.op("dve", lambda e, b_=b_: e.scalar_tensor_tensor(out=zt[0:8, :], in0=ztm2[0:8, :], scalar=relbT[0:8, b_:b_ + 1], in1=zt[0:8, :],
                                                                        op0=ALU.mult, op1=ALU.add), ["ztm2", "relbT", "zt"], ["zt"])
                for h in range(8):
                    sc.op("pe", lambda e, h=h: e.matmul(bank(1), lhsT=selhs[0:8, h * 128:(h + 1) * 128], rhs=zt[0:8, :], start=True, stop=True),
                          ["selhs", "zt"], ["b1"])
                    sc.op("dve", lambda e: e.tensor_copy(out=zrep, in_=bank(1)), ["b1"], ["zrep"])
                    dma("sp", bsc[h].rearrange("(r c) -> r c", c=512), zrep, "bsc", reads=["zrep"], writes=["bsc%d" % h])

            xv = x[lb * S:(lb + 1) * S, :].rearrange("(t p) d -> t p d", p=128)
            for tt in range(NT):
                b_ = tt % 2
                dma("sp", xt[b_], xv[tt], "xt%d" % b_, writes=["xt%d" % b_])
                sc.op("act", lambda e, b_=b_, tt=tt: e.activation(out=xn[b_], in_=xt[b_], func=AF.Square, accum_out=stat[:, tt:tt + 1]),
                      ["xt%d" % b_], ["xn%d" % b_, "stat_a%d" % tt])
                sc.op("dve", lambda e, tt=tt: e.tensor_scalar(out=stat[:, 16 + tt:17 + tt], in0=stat[:, tt:tt + 1], scalar1=D * EPS, scalar2=None, op0=ALU.add),
                      ["stat_a%d" % tt], ["stat_b%d" % tt])
                sc.op("act", lambda e, tt=tt: e.activation(out=stat[:, 16 + tt:17 + tt], in_=stat[:, 16 + tt:17 + tt], func=AF.Ln),
                      ["stat_b%d" % tt], ["stat_b%d" % tt])
                sc.op("act", lambda e, tt=tt: e.activation(out=stat[:, 16 + tt:17 + tt], in_=stat[:, 16 + tt:17 + tt], func=AF.Exp, scale=-0.5),
                      ["stat_b%d" % tt], ["stat_b%d" % tt])
                sc.op("dve", lambda e, b_=b_, tt=tt: e.tensor_scalar(out=xn[b_], in0=xt[b_], scalar1=stat[:, 16 + tt:17 + tt], scalar2=None, op0=ALU.mult),
                      ["xt%d" % b_, "stat_b%d" % tt], ["xn%d" % b_])
                for g in range(2):
                    pb = 2 + g
                    pv = bankbf(pb)
                    for j in range(8):
                        kc = g * 8 + j
                        sc.op("pe", lambda e, pv=pv, j=j, kc=kc, b_=b_: e.transpose(pv[:, j * 128:(j + 1) * 128], xn[b_][:, kc * 128:(kc + 1) * 128], ident[:]),
                              ["xn%d" % b_, "ident"], ["b%d" % pb], signal=(j == 7))
                    for j in range(8):
                        kc = g * 8 + j
                        if evac_eng() == "act":
                            sc.op("act", lambda e, pv=pv, j=j, kc=kc, tt=tt: e.activation(
                                out=hT[:, kc, tt * 128:(tt + 1) * 128], in_=pv[:, j * 128:(j + 1) * 128], func=AF.Identity,
                                scale=a_pre[:, kc:kc + 1], bias=msc[:, kc:kc + 1]), ["b%d" % pb, "a_pre", ("msc", 0)], [("hT", tt)])
                        else:
                            sc.op("dve", lambda e, pv=pv, j=j, kc=kc, tt=tt: e.tensor_scalar(
                                out=hT[:, kc, tt * 128:(tt + 1) * 128], in0=pv[:, j * 128:(j + 1) * 128],
                                scalar1=a_pre[:, kc:kc + 1], scalar2=msc[:, kc:kc + 1], op0=ALU.mult, op1=ALU.add),
                                ["b%d" % pb, "a_pre", ("msc", 0)], [("hT", tt)])
                if tt % 2 == 1 and ada_later:
                    ada_piece(*ada_later.pop(0))
            while ada_later:
                ada_piece(*ada_later.pop(0))
            ada_finish(((2, 3), (3, 4)), a_ffn, fpn, 3)
            sc.barrier()

            def dbg_dump(ap_f32_src, ncols, reads):
                t_ = view(OFF_A, F32, 4096) if False else None
                dma("sp", dbg_out[:, 0:ncols], ap_f32_src, "dbg", reads=reads)

            if stop_after == 0:
                tmpf = view(OFF_P, F32, 4096)
                sc.op("dve", lambda e: e.tensor_copy(out=tmpf[:, 0:2048], in_=hT[:, 0, :]), [], ["tmpf"])
                sc.op("dve", lambda e: e.tensor_copy(out=tmpf[:, 2048:4096], in_=hT[:, 15, :]), [], ["tmpf"])
                dbg_dump(tmpf, 4096, ["tmpf"])
                sc.finish()
                sc.replay(E(nc.Block()))
                return nc

            sc.op("dve", lambda e: e.memset(dvaug[:, :, :, 128:132], 1.0), [], ["dvaug"])
            hkeys = lambda tc: [("hT", 4 * tc + i) for i in range(4)]
            nw = 0
            for j in range(22):
                wb = wst[nw % 2]; wk = "wst%d" % (nw % 2); nw += 1
                dma("pool", wb, w_fm[j].rearrange("p (k c) -> p k c", k=16), wk, writes=[wk])
                for tc in range(4):
                    pb = 2 * (tc % 2) + 0 if False else (tc % 4)
                    pb = 4 + (tc % 4)
                    for kc in range(16):
                        sc.op("pe", lambda e, wb=wb, kc=kc, tc=tc, pb=pb: e.matmul(bank(pb), lhsT=wb[:, kc, :], rhs=hT[:, kc, tc * 512:(tc + 1) * 512],
                                                                                start=(kc == 0), stop=(kc == 15)),
                              [wk] + hkeys(tc), ["b%d" % pb], signal=(kc == 15))
                    if j < 4:
                        dst, key = qlatT[:, j, tc * 512:(tc + 1) * 512], "qlatT"
                    elif j < 6:
                        dst, key = kvlatT[:, j - 4, tc * 512:(tc + 1) * 512], "kvlatT"
                    elif j < 14:
                        dst, key = dqT[:, j - 6, tc * 512:(tc + 1) * 512], ("dqT", j - 6, tc)
                    else:
                        dst, key = dkT[:, j - 14, tc * 512:(tc + 1) * 512], ("dkT", j - 14)
                    copy(evac_eng(), dst, bank(pb), ["b%d" % pb], [key])
            for h in range(8):
                wb = wst[nw % 2]; wk = "wst%d" % (nw % 2); nw += 1
                dma("pool", wb, w_dv[h].rearrange("p (k c) -> p k c", k=16), wk, writes=[wk])
                for tg in range(4):
                    pb = 4 + (tg % 4)
                    for i in range(4):
                        tt = tg * 4 + i
                        for kc in range(16):
                            sc.op("pe", lambda e, wb=wb, kc=kc, tt=tt, pb=pb, i=i: e.matmul(
                                bank(pb, 128, i * 128), lhsT=hT[:, kc, tt * 128:(tt + 1) * 128], rhs=wb[:, kc, :],
                                start=(kc == 0), stop=(kc == 15)), [wk, ("hT", tt)], ["b%d" % pb], signal=(kc == 15))
                    copy(evac_eng(), dvaug[:, tg * 4:(tg + 1) * 4, h, 0:128], bank(pb).rearrange("p (t c) -> p t c", t=4),
                         ["b%d" % pb], ["dvaug"])
            wkr = view(OFF_S, BF16, 16 * 64, "p (k c) -> p k c", k=16)
            dma("pool", wkr, w_kr.rearrange("p (k c) -> p k c", k=16), "wst0", writes=["wst0"])
            krt = view(OFF_S + 4096, BF16, 128)
            rtmp = view(OFF_S + 4096 + 256, F32, 128)
            for tt in range(NT):
                pb = 4 + (tt % 2)
                kp = bank(pb, 64)
                for kc in range(16):
                    sc.op("pe", lambda e, kc=kc, tt=tt, kp=kp: e.matmul(kp, lhsT=hT[:, kc, tt * 128:(tt + 1) * 128], rhs=wkr[:, kc, :],
                                                                      start=(kc == 0), stop=(kc == 15)),
                          ["wst0", ("hT", tt)], ["b%d" % pb], signal=(kc == 15))
                cs = cost[:, tt * 32:(tt + 1) * 32]; sn = sint[:, tt * 32:(tt + 1) * 32]
                sc.op("dve", lambda e, kp=kp, cs=cs: e.tensor_tensor(out=rtmp[:, 0:32], in0=kp[:, 0:32], in1=cs, op=ALU.mult), ["b%d" % pb, "cost"], ["wst1"])
                sc.op("dve", lambda e, kp=kp, sn=sn: e.tensor_tensor(out=rtmp[:, 32:64], in0=kp[:, 32:64], in1=sn, op=ALU.mult), ["b%d" % pb, "sint"], ["wst1"])
                sc.op("dve", lambda e, kp=kp, sn=sn: e.tensor_tensor(out=rtmp[:, 64:96], in0=kp[:, 0:32], in1=sn, op=ALU.mult), ["b%d" % pb, "sint"], ["wst1"])
                sc.op("dve", lambda e, kp=kp, cs=cs: e.tensor_tensor(out=rtmp[:, 96:128], in0=kp[:, 32:64], in1=cs, op=ALU.mult), ["b%d" % pb, "cost"], ["wst1"])
                sc.op("dve", lambda e: e.tensor_tensor(out=krt[:, 0:32], in0=rtmp[:, 0:32], in1=rtmp[:, 32:64], op=ALU.subtract), ["wst1"], ["wst1"])
                sc.op("dve", lambda e: e.tensor_tensor(out=krt[:, 32:64], in0=rtmp[:, 64:96], in1=rtmp[:, 96:128], op=ALU.add), ["wst1"], ["wst1"])
                sc.op("dve", lambda e: e.tensor_copy(out=krt[:, 64:128], in_=krt[:, 0:64]), ["wst1"], ["wst1"])
                tp = bankbf(6 + (tt % 2))
                sc.op("pe", lambda e, tp=tp: e.transpose(tp[:, 0:128], krt, ident[:]), ["wst1", "ident"], ["b%d" % (6 + tt % 2)])
                copy(evac_eng(), kropeT[:, tt * 128:(tt + 1) * 128], tp[:, 0:128], ["b%d" % (6 + tt % 2)], ["kropeT"])
            sc.barrier()
            if stop_after == 1:
                break

            o = OFF_A
            PT = [view(o + i * 1024, BF16, 512) for i in range(8)]; o += 8192
            wuq = view(o, BF16, 4 * 1536, "p (k c) -> p k c", k=4); o += 12288
            wukv = view(o, BF16, 2 * 2048, "p (k c) -> p k c", k=2); o += 8192
            qropeT = view(o, BF16, 4 * 2048, "p (r t) -> p r t", r=4); o += 16384
            rchunk = view(o, F32, 512)
            rcp = view(o, F32, 512); o += 2048
            sqtmp = view(OFF_S, BF16, 2048)
            bth = [view(OFF_S + 4096 + i * 1536, F32, 384, "p (d q) -> p d q", d=3) for i in range(2)]
            sqb = view(OFF_S + 4096 + 3072, BF16, 512)
            qnopeT = view(o, BF16, 2048); o += 4096
            knopeT = view(o, BF16, 2048); o += 4096
            vaug = view(o, BF16, 16 * 132, "p (t c) -> p t c", t=16); o += 4224
            qrtok = view(o, BF16, 512)
            odb4 = view(o, BF16, 512, "p (j c) -> p j c", j=4); o += 1024
            rt4 = view(o, F32, 128); o += 512
            odf = view(o, F32, 512); o += 2048
            bhi = view(o, BF16, 384, "p (d q) -> p d q", d=3); o += 768
            blo = view(o, BF16, 384, "p (d q) -> p d q", d=3); o += 768
            assert o <= 65536, o

            dma("pool", wuq, w_uq.rearrange("p (k c) -> p k c", k=4), "wuq", writes=["wuq"])
            dma("pool", wukv, w_ukv.rearrange("p (k c) -> p k c", k=2), "wukv", writes=["wukv"])
            for kc in range(4):
                sc.op("dve", lambda e, kc=kc: e.tensor_scalar(out=wuq[:, kc, :], in0=wuq[:, kc, :], scalar1=qnw[:, kc:kc + 1], scalar2=None, op0=ALU.mult),
                      ["wuq", "qnw"], ["wuq"])
            for kc in range(2):
                sc.op("dve", lambda e, kc=kc: e.tensor_scalar(out=wukv[:, kc, :], in0=wukv[:, kc, :], scalar1=kvnw[:, kc:kc + 1], scalar2=None, op0=ALU.mult),
                      ["wukv", "kvnw"], ["wukv"])

            for (lat, nk, key) in ((qlatT, 4, "qlatT"), (kvlatT, 2, "kvlatT")):
                n_lat = nk * 128
                for tc in range(4):
                    for kc in range(nk):
                        sc.op("act", lambda e, lat=lat, kc=kc, tc=tc: e.activation(out=sqtmp[:, 0:512], in_=lat[:, kc, tc * 512:(tc + 1) * 512], func=AF.Square),
                              [key], ["sqtmp"])
                        sc.op("pe", lambda e, kc=kc, nk=nk: e.matmul(bank(0), lhsT=onesb[:], rhs=sqtmp[:, 0:512], start=(kc == 0), stop=(kc == nk - 1)),
                              ["sqtmp", "onesb"], ["b0"], signal=True)
                    sc.op("dve", lambda e, n_lat=n_lat: e.tensor_scalar(out=rchunk, in0=bank(0), scalar1=1.0 / n_lat, scalar2=EPS, op0=ALU.mult, op1=ALU.add),
                          ["b0"], ["rchunk"])
                    sc.op("act", lambda e: e.activation(out=rchunk, in_=rchunk, func=AF.Ln), ["rchunk"], ["rchunk"])
                    sc.op("act", lambda e: e.activation(out=rchunk, in_=rchunk, func=AF.Exp, scale=-0.5), ["rchunk"], ["rchunk"])
                    for kc in range(nk):
                        sc.op("dve", lambda e, lat=lat, kc=kc, tc=tc: e.tensor_tensor(out=lat[:, kc, tc * 512:(tc + 1) * 512], in0=lat[:, kc, tc * 512:(tc + 1) * 512],
                                                                                   in1=rchunk, op=ALU.mult), [key, "rchunk"], [key])
            sc.barrier()

            for tt in range(NT):
                qp = bank(4 + tt % 2)
                for kc in range(4):
                    sc.op("pe", lambda e, kc=kc, tt=tt, qp=qp: e.matmul(qp, lhsT=qlatT[:, kc, tt * 128:(tt + 1) * 128], rhs=wuq[:, kc, 1024:1536],
                                                                      start=(kc == 0), stop=(kc == 3)), ["qlatT", "wuq"], ["b%d" % (4 + tt % 2)], signal=(kc == 3))
                cs = cost[:, tt * 32:(tt + 1) * 32]; sn = sint[:, tt * 32:(tt + 1) * 32]
                bk = "b%d" % (4 + tt % 2)
                for h in range(8):
                    t1 = qp[:, h * 64:h * 64 + 32]; t2 = qp[:, h * 64 + 32:h * 64 + 64]
                    sc.op("dve", lambda e, t1=t1, cs=cs: e.tensor_tensor(out=rt4[:, 0:32], in0=t1, in1=cs, op=ALU.mult), [bk, "cost"], ["rt4"])
                    sc.op("dve", lambda e, t2=t2, sn=sn: e.tensor_tensor(out=rt4[:, 32:64], in0=t2, in1=sn, op=ALU.mult), [bk, "sint"], ["rt4"])
                    sc.op("dve", lambda e, t1=t1, sn=sn: e.tensor_tensor(out=rt4[:, 64:96], in0=t1, in1=sn, op=ALU.mult), [bk, "sint"], ["rt4"])
                    sc.op("dve", lambda e, t2=t2, cs=cs: e.tensor_tensor(out=rt4[:, 96:128], in0=t2, in1=cs, op=ALU.mult), [bk, "cost"], ["rt4"])
                    sc.op("dve", lambda e, h=h: e.tensor_tensor(out=qrtok[:, h * 64:h * 64 + 32], in0=rt4[:, 0:32], in1=rt4[:, 32:64], op=ALU.subtract), ["rt4"], ["qrtok"])
                    sc.op("dve", lambda e, h=h: e.tensor_tensor(out=qrtok[:, h * 64 + 32:h * 64 + 64], in0=rt4[:, 64:96], in1=rt4[:, 96:128], op=ALU.add), ["rt4"], ["qrtok"])
                tp = bankbf(6 + tt % 2)
                for pr in range(4):
                    sc.op("pe", lambda e, tp=tp, pr=pr: e.transpose(tp[:, pr * 128:(pr + 1) * 128], qrtok[:, pr * 128:(pr + 1) * 128], ident[:]),
                          ["qrtok", "ident"], ["b%d" % (6 + tt % 2)], signal=(pr == 3))
                copy(evac_eng(), qropeT[:, :, tt * 128:(tt + 1) * 128], tp[:, 0:512].rearrange("p (r t) -> p r t", r=4),
                     ["b%d" % (6 + tt % 2)], ["qropeT"])

            sc.barrier()
            if stop_after == 12:
                break
            OT_diff = dqT
            OT_mla = dkT
            SC_B = (0, 1)

            def pvacc(j, n=132):
                return bank(2 + j, n)

            def sumsq_max(srcs, dst_col):
                for tc in range(4):
                    for i, (ap_, sel_, key_, rows) in enumerate(srcs):
                        sc.op("pe", lambda e, ap_=ap_, sel_=sel_, tc=tc, i=i, n=len(srcs): e.matmul(
                            bank(6), lhsT=sel_, rhs=ap_[:, tc * 512:(tc + 1) * 512], start=(i == 0), stop=(i == n - 1)),
                            [key_, "sel1", "sel2", "onesb"], ["b6"], signal=(i == len(srcs) - 1))
                    sc.op("dve", lambda e, tc=tc: e.tensor_reduce(out=st2[:, 32 + tc:33 + tc], in_=bank(6), axis=AX.X, op=ALU.max), ["b6"], ["st2x"])
                sc.op("dve", lambda e: e.tensor_reduce(out=st2[:, dst_col:dst_col + 1], in_=st2[:, 32:36], axis=AX.X, op=ALU.max), ["st2x"], ["st2x", "st2"])


            pending = []

            def attention_core(nmaps, kparts, qparts, vsrc, negm_cols, hbias, h, is_diff):
                for qc in range(4):
                    for mp in range(nmaps):
                        parts = kparts[mp]
                        negm = st2[:, negm_cols[mp]:negm_cols[mp] + 1]

                        def emit_score(kt, qc=qc, mp=mp, parts=parts):
                            sb_ = SC_B[kt % 2]
                            for pi, (kf, qf, keys) in enumerate(zip(parts[0], parts[1], parts[2])):
                                if len(parts) > 3:
                                    keys = keys + [parts[3] + (qc,)]
                                near = [j for j in range(4) if abs(kt - (4 * qc + j)) <= 1] if is_diff else []
                                last_score = (pi == len(parts[0]) - 1)
                                sc.op("pe", lambda e, kf=kf, qf=qf, kt=kt, qc=qc, sb_=sb_, pi=pi, last=(last_score and not near): e.matmul(
                                    bank(sb_), lhsT=kf(kt), rhs=qf(qc), start=(pi == 0), stop=last),
                                    keys, ["b%d" % sb_], signal=(last_score and not near))
                                if last_score:
                                    for ni, j in enumerate(near):
                                        dl = kt - (4 * qc + j)
                                        for bi, bsrc_ in enumerate((bhi, blo)):
                                            fin = (ni == len(near) - 1 and bi == 1)
                                            sc.op("pe", lambda e, sb_=sb_, j=j, dl=dl, bsrc_=bsrc_, fin=fin: e.matmul(
                                                bank(sb_, 128, j * 128), lhsT=ident[:], rhs=bsrc_[:, dl + 1, :], start=False, stop=fin),
                                                ["bhi", "blo", "ident"], ["b%d" % sb_], signal=fin)

                        def emit_exp(kt, qc=qc, mp=mp, negm=negm):
                            sb_ = SC_B[kt % 2]
                            slot = kt % 8
                            pt = PT[slot]
                            if not is_diff:
                                sc.op("act", lambda e, pt=pt, sb_=sb_, negm=negm: e.activation(out=pt, in_=bank(sb_), func=AF.Exp, scale=MLA_SCALE, bias=negm),
                                      ["b%d" % sb_, "st2"], [("PT", slot)])
                                return
                            cls = []
                            for j in range(4):
                                dl = kt - (4 * qc + j)
                                cls.append(16 + mp if dl > 1 else (8 + mp if dl < -1 else negm_cols[mp]))
                            j0 = 0
                            while j0 < 4:
                                j1 = j0
                                while j1 + 1 < 4 and cls[j1 + 1] == cls[j0]:
                                    j1 += 1
                                w_ = (j1 - j0 + 1) * 128
                                ps_ = bank(sb_, w_, j0 * 128)
                                po = pt[:, j0 * 128:j0 * 128 + w_]
                                bia = st2[:, cls[j0]:cls[j0] + 1]
                                sc.op("act", lambda e, po=po, ps_=ps_, bia=bia: e.activation(out=po, in_=ps_, func=AF.Exp, scale=DIFF_SCALE, bias=bia),
                                      ["b%d" % sb_, "st2"], [("PT", slot)])
                                j0 = j1 + 1

                        setid = mp if nmaps == 2 else (qc % 2)
                        bo, bs = 2 + 2 * setid, 3 + 2 * setid

                        def emit_pv(kt, bo=bo, bs=bs):
                            slot = kt % 8
                            pt = PT[slot]
                            sc.op("pe", lambda e, pt=pt, kt=kt, bo=bo: e.matmul(bank(bo), lhsT=vsrc(kt), rhs=pt, start=(kt == 0), stop=(kt == 15)),
                                  [("PT", slot), "vsrc"], ["b%d" % bo], signal=(kt == 15))
                            sc.op("pe", lambda e, pt=pt, kt=kt, bs=bs: e.matmul(bank(bs), lhsT=onesb[:], rhs=pt, start=(kt == 0), stop=(kt == 15)),
                                  [("PT", slot), "onesb"], ["b%d" % bs], signal=(kt == 15))

                        emit_score(0)
                        for kt in range(16):
                            if kt + 1 < 16:
                                emit_score(kt + 1)
                            if kt == 3:
                                while pending:
                                    pending.pop(0)()
                            emit_exp(kt)
                            emit_pv(kt)
                        if nmaps == 2 and mp == 0:
                            continue
                        sc.barrier()
                        yield qc, mp

            for h in range(1 if stop_after in (14, 15) else 8):
                for di, delta in enumerate((-1, 0, 1)):
                    src = bass.AP(bsc.tensor, h * 65536 + 255 - delta * 128, [[511, 128], [1, 128]])
                    dma("sp", bth[h % 2][:, di, :], src, "bT%d" % (h % 2), writes=[("bT", h % 2)])
                bcur = bth[h % 2]
                sc.op("dve", lambda e, bcur=bcur: e.tensor_scalar(out=bcur, in0=bcur, scalar1=1.0 / DIFF_SCALE, scalar2=None, op0=ALU.mult), [("bT", h % 2)], [("bT", h % 2)])
                sc.op("dve", lambda e, bcur=bcur: e.tensor_copy(out=bhi, in_=bcur), [("bT", h % 2)], ["bhi"])
                sc.op("dve", lambda e, bcur=bcur: e.tensor_tensor(out=bcur, in0=bcur, in1=bhi, op=ALU.subtract), [("bT", h % 2), "bhi"], [("bT", h % 2)])
                sc.op("dve", lambda e, bcur=bcur: e.tensor_copy(out=blo, in_=bcur), [("bT", h % 2)], ["blo"])
                sc.op("act", lambda e, h=h: e.activation(out=sqtmp, in_=dqT[:, h, :], func=AF.Square), [("dqT", h, q_) for q_ in range(4)], ["sqtmp"])
                sumsq_max([(sqtmp, sel1[:], "sqtmp", 128)], 0)
                sumsq_max([(sqtmp, sel2[:], "sqtmp", 128)], 1)
                sc.op("act", lambda e, h=h: e.activation(out=sqtmp, in_=dkT[:, h, :], func=AF.Square), [("dkT", h)], ["sqtmp"])
                sumsq_max([(sqtmp, sel1[:], "sqtmp", 128)], 2)
                sumsq_max([(sqtmp, sel2[:], "sqtmp", 128)], 3)
                sc.op("dve", lambda e: e.tensor_tensor(out=st2[:, 4:6], in0=st2[:, 0:2], in1=st2[:, 2:4], op=ALU.add), ["st2"], ["st2"])
                sc.op("dve", lambda e: e.tensor_scalar(out=st2[:, 4:6], in0=st2[:, 4:6], scalar1=-0.5 * DIFF_SCALE * 1.02, scalar2=None, op0=ALU.mult), ["st2"], ["st2"])
                sc.op("dve", lambda e, h=h: e.tensor_scalar(out=st2[:, 8:10], in0=st2[:, 4:6], scalar1=tabbc[:, h:h + 1], scalar2=None, op0=ALU.add), ["st2", "tabbc0"], ["st2"])
                sc.op("dve", lambda e, h=h: e.tensor_scalar(out=st2[:, 16:18], in0=st2[:, 4:6], scalar1=tabbc[:, 8 + h:9 + h], scalar2=None, op0=ALU.add), ["st2", "tabbc1"], ["st2"])
                if stop_after == 13:
                    break
                kparts = []
                for mp in range(2):
                    r0 = mp * 64
                    kparts.append(([lambda kt, r0=r0, h=h: dkT[r0:r0 + 64, h, kt * 128:(kt + 1) * 128]],
                                   [lambda qc, r0=r0, h=h: dqT[r0:r0 + 64, h, qc * 512:(qc + 1) * 512]],
                                   [[("dkT", h)]], ("dqT", h)))
                vsrc = lambda kt, h=h: dvaug[:, kt, h, 0:128]
                for qc, mp in attention_core(2, kparts, None, vsrc, (4, 5), None, h, True):
                    if stop_after == 14:
                        continue
                    sc.op("dve", lambda e: e.reciprocal(out=rcp, in_=bank(3)), ["b3"], ["rcp"])
                    sc.op("dve", lambda e: e.tensor_tensor(out=odf, in0=bank(2), in1=rcp, op=ALU.mult), ["b2", "rcp"], ["odt"])
                    sc.op("dve", lambda e: e.reciprocal(out=rcp, in_=bank(5)), ["b5", "rcp"], ["rcp"])
                    sc.op("dve", lambda e: e.tensor_tensor(out=rcp, in0=bank(4), in1=rcp, op=ALU.mult), ["b4", "rcp"], ["rcp"])
                    sc.op("dve", lambda e: e.scalar_tensor_tensor(out=odf, in0=rcp, scalar=lamt[:, 4:5], in1=odf, op0=ALU.mult, op1=ALU.add), ["rcp", "lamt", "odt"], ["odt"])
                    sc.op("dve", lambda e: e.tensor_tensor(out=sqb, in0=odf, in1=odf, op=ALU.mult), ["odt"], ["sqb"])
                    sc.op("pe", lambda e: e.matmul(bank(7), lhsT=onesb[:], rhs=sqb, start=True, stop=True), ["sqb", "onesb"], ["b7"])
                    sc.op("dve", lambda e: e.tensor_scalar(out=rcp, in0=bank(7), scalar1=1.0 / 128, scalar2=EPS, op0=ALU.mult, op1=ALU.add), ["b7", "rcp"], ["rcp"])
                    sc.op("act", lambda e: e.activation(out=rcp, in_=rcp, func=AF.Ln), ["rcp"], ["rcp"])
                    sc.op("act", lambda e: e.activation(out=rcp, in_=rcp, func=AF.Exp, scale=-0.5), ["rcp"], ["rcp"])
                    sc.op("dve", lambda e, h=h, qc=qc: e.scalar_tensor_tensor(out=OT_diff[:, h, qc * 512:(qc + 1) * 512], in0=odf, scalar=subw[:, 0:1], in1=rcp, op0=ALU.mult, op1=ALU.mult),
                          ["odt", "subw", "rcp"], [("dqT", h, qc)])

            if stop_after in (13, 14, 15, 16):
                break
            for h in range(8):
                half = (h % 2) * 64; pr = h // 2
                for tc in range(4):
                    pb = 6 + tc % 2
                    for kc in range(4):
                        sc.op("pe", lambda e, kc=kc, tc=tc, pb=pb, h=h: e.matmul(bank(pb), lhsT=wuq[:, kc, h * 128:(h + 1) * 128], rhs=qlatT[:, kc, tc * 512:(tc + 1) * 512],
                                                                              start=(kc == 0), stop=(kc == 3)), ["wuq", "qlatT"], ["b%d" % pb], signal=(kc == 3))
                    copy(evac_eng(), qnopeT[:, tc * 512:(tc + 1) * 512], bank(pb), ["b%d" % pb], ["qnopeT"])
                for tc in range(4):
                    pb = 6 + tc % 2
                    for kc in range(2):
                        sc.op("pe", lambda e, kc=kc, tc=tc, pb=pb, h=h: e.matmul(bank(pb), lhsT=wukv[:, kc, h * 128:(h + 1) * 128], rhs=kvlatT[:, kc, tc * 512:(tc + 1) * 512],
                                                                              start=(kc == 0), stop=(kc == 1)), ["wukv", "kvlatT"], ["b%d" % pb], signal=(kc == 1))
                    copy(evac_eng(), knopeT[:, tc * 512:(tc + 1) * 512], bank(pb), ["b%d" % pb], ["knopeT"])
                sc.op("dve", lambda e: e.memset(vaug[:, :, 128:132], 1.0), [], ["vsrc"])
                for tg in range(4):
                    pb = 6 + tg % 2
                    for i in range(4):
                        tt = tg * 4 + i
                        for kc in range(2):
                            sc.op("pe", lambda e, kc=kc, tt=tt, pb=pb, i=i, h=h: e.matmul(
                                bank(pb, 128, i * 128), lhsT=kvlatT[:, kc, tt * 128:(tt + 1) * 128], rhs=wukv[:, kc, 1024 + h * 128:1024 + (h + 1) * 128],
                                start=(kc == 0), stop=(kc == 1)), ["wukv", "kvlatT"], ["b%d" % pb], signal=(kc == 1))
                    copy(evac_eng(), vaug[:, tg * 4:(tg + 1) * 4, 0:128], bank(pb).rearrange("p (t c) -> p t c", t=4), ["b%d" % pb], ["vsrc"])
                qr = qropeT[half:half + 64, pr, :]
                kr = kropeT[half:half + 64, :]
                sel_h = onesb[half:half + 64, :]
                sc.op("act", lambda e: e.activation(out=sqtmp, in_=qnopeT, func=AF.Square), ["qnopeT"], ["sqtmp"])
                sumsq_max([(sqtmp, onesb[:], "sqtmp", 128)], 0)
                sc.op("act", lambda e: e.activation(out=sqtmp, in_=knopeT, func=AF.Square), ["knopeT"], ["sqtmp"])
                sumsq_max([(sqtmp, onesb[:], "sqtmp", 128)], 2)
                sc.op("act", lambda e, qr=qr, half=half: e.activation(out=sqtmp[half:half + 64, :], in_=qr, func=AF.Square), ["qropeT"], ["sqtmp"])
                sumsq_max([(sqtmp[half:half + 64, :], sel_h, "sqtmp", 64)], 1)
                sc.op("act", lambda e, kr=kr, half=half: e.activation(out=sqtmp[half:half + 64, :], in_=kr, func=AF.Square), ["kropeT"], ["sqtmp"])
                sumsq_max([(sqtmp[half:half + 64, :], sel_h, "sqtmp", 64)], 3)
                sc.op("dve", lambda e: e.tensor_tensor(out=st2[:, 4:6], in0=st2[:, 0:2], in1=st2[:, 2:4], op=ALU.add), ["st2"], ["st2"])
                sc.op("dve", lambda e: e.tensor_tensor(out=st2[:, 6:7], in0=st2[:, 4:5], in1=st2[:, 5:6], op=ALU.add), ["st2"], ["st2"])
                sc.op("dve", lambda e: e.tensor_scalar(out=st2[:, 6:7], in0=st2[:, 6:7], scalar1=-0.5 * MLA_SCALE * 1.02, scalar2=None, op0=ALU.mult), ["st2"], ["st2"])
                kparts = [([lambda kt: knopeT[:, kt * 128:(kt + 1) * 128], lambda kt, half=half: kropeT[half:half + 64, kt * 128:(kt + 1) * 128]],
                           [lambda qc: qnopeT[:, qc * 512:(qc + 1) * 512], lambda qc, half=half, pr=pr: qropeT[half:half + 64, pr, qc * 512:(qc + 1) * 512]],
                           [["knopeT", "qnopeT"], ["kropeT", "qropeT"]])]
                vsrc = lambda kt: vaug[:, kt, 0:128]
                for qc, mp in attention_core(1, kparts, None, vsrc, (6,), None, h, False):
                    bo, bs = 2 + 2 * (qc % 2), 3 + 2 * (qc % 2)
                    sc.op("dve", lambda e, bs=bs: e.reciprocal(out=rcp, in_=bank(bs)), ["b%d" % bs], ["rcp"])
                    sc.op("dve", lambda e, bo=bo, h=h, qc=qc: e.tensor_tensor(out=OT_mla[:, h, qc * 512:(qc + 1) * 512], in0=bank(bo), in1=rcp, op=ALU.mult),
                          ["b%d" % bo, "rcp"], [("dkT", h)])
            sc.barrier()
            if stop_after == 2:
                break

            wo = view(OFF_A, BF16, 16 * 2048, "p (k c) -> p k c", k=16)
            wov = w_o.rearrange("p (k c) -> p k c", k=16)
            for mc in range(16):
                dma("pool", wo[:, mc, :], wov[:, mc, :], "wo%d" % (mc % 4), writes=["wo"])
            OFF_W = OFF_P + 65536
            gw = view(OFF_W, F32, 2048)
            gtmp = view(OFF_W + 8192, F32, 2048)
            ast = [view(OFF_W + 16384 + i * 16384, BF16, 16 * 512, "p (k c) -> p k c", k=16) for i in range(2)]
            xt3 = [view(OFF_W + 49152 + i * 8192, F32, 2048) for i in range(2)]
            yt3 = view(OFF_W + 65536, F32, 2048) if OFF_W + 65536 + 8192 <= ARENA else None
            junk = view(OFF_W + 16384, BF16, 2048)
            cbk = view(OFF_W + 65536, BF16, 16 * 128, "p (k c) -> p k c", k=16)

            def build_gw(ch, postw, gw_, gtmp_, ast_, cbk_):
                gw, gtmp, ast, cbk = gw_, gtmp_, ast_, cbk_
                for kc in range(16):
                    sc.op("dve", lambda e, kc=kc: e.tensor_scalar(out=cbk[:, kc, :], in0=onesb[:], scalar1=cact[:, kc:kc + 1],
                                                                 scalar2=None, op0=ALU.mult), ["onesb", "cact"], ["cactbc"])
                dma("sp", gtmp, ada_b[:, ch * D:(ch + 1) * D].partition_broadcast(128), "gtmp", writes=["gtmp"])
                for piece in range(4):
                    st = ast[piece % 2]; sk = "ast%d" % (piece % 2)
                    dma("pool", st, ada_r[ch * 4 + piece].rearrange("p (k c) -> p k c", k=16), sk, writes=[sk])
                    for kc in range(16):
                        sc.op("pe", lambda e, st=st, kc=kc, piece=piece: e.matmul(bank(piece), lhsT=cbk[:, kc, :], rhs=st[:, kc, :], start=(kc == 0), stop=(kc == 15)),
                              [sk, "cactbc"], ["b%d" % piece], signal=(kc == 15))
                    sc.op("dve", lambda e, piece=piece: e.tensor_tensor(out=gw[:, piece * 512:(piece + 1) * 512], in0=bank(piece), in1=gtmp[:, piece * 512:(piece + 1) * 512], op=ALU.add),
                          ["b%d" % piece, "gtmp"], ["gw"])
                dma("sp", gtmp, postw.partition_broadcast(128), "gtmp", reads=["gw"], writes=["gtmp"])
                sc.op("dve", lambda e: e.tensor_tensor(out=gw, in0=gw, in1=gtmp, op=ALU.mult), ["gw", "gtmp"], ["gw"])
                sc.barrier()

            build_gw(2, apost, gw, gtmp, ast, cbk)
            x1v = x1s.rearrange("(t p) d -> t p d", p=128)
            outv = out[lb * S:(lb + 1) * S, :].rearrange("(t p) d -> t p d", p=128)

            def post_norm_residual(tt, ysrc, ykeys, xin, xkey, dst_dram, dkey, bufs):
                gw, gtmp, junk = bufs
                for dch in range(4):
                    sc.op("act", lambda e, dch=dch: e.activation(out=junk[:, dch * 512:(dch + 1) * 512], in_=ysrc[dch], func=AF.Square, accum_out=stat[:, 32 + dch:33 + dch]),
                          [ykeys[dch]], ["junk", "stat_y"])
                sc.op("dve", lambda e: e.tensor_reduce(out=stat[:, 36:37], in_=stat[:, 32:36], axis=AX.X, op=ALU.add), ["stat_y"], ["stat_y2"])
                sc.op("dve", lambda e: e.tensor_scalar(out=stat[:, 36:37], in0=stat[:, 36:37], scalar1=1.0 / D, scalar2=EPS, op0=ALU.mult, op1=ALU.add), ["stat_y2"], ["stat_y2"])
                sc.op("act", lambda e: e.activation(out=stat[:, 36:37], in_=stat[:, 36:37], func=AF.Ln), ["stat_y2"], ["stat_y2"])
                sc.op("act", lambda e: e.activation(out=stat[:, 36:37], in_=stat[:, 36:37], func=AF.Exp, scale=-0.5), ["stat_y2"], ["stat_y2"])
                for dch in range(4):
                    sl = slice(dch * 512, (dch + 1) * 512)
                    sc.op("dve", lambda e, dch=dch, sl=sl: e.scalar_tensor_tensor(out=gtmp[:, sl], in0=ysrc[dch], scalar=stat[:, 36:37], in1=gw[:, sl], op0=ALU.mult, op1=ALU.mult),
                          [ykeys[dch], "stat_y2", "gw"], ["gtmp"])
                    sc.op("dve", lambda e, sl=sl: e.tensor_tensor(out=xin[:, sl], in0=xin[:, sl], in1=gtmp[:, sl], op=ALU.add), ["gtmp", xkey], [xkey])
                dma("sp", dst_dram, xin, "st_" + xkey, reads=[xkey], writes=[dkey])

            for tt in range(NT):
                b_ = tt % 2
                dma("sp", xt3[b_], xv[tt], "xt3%d" % b_, writes=["xt3%d" % b_])
                for dch in range(4):
                    for mc in range(16):
                        src = OT_mla[:, mc, tt * 128:(tt + 1) * 128] if mc < 8 else OT_diff[:, mc - 8, tt * 128:(tt + 1) * 128]
                        sc.op("pe", lambda e, src=src, mc=mc, dch=dch: e.matmul(bank(dch), lhsT=src, rhs=wo[:, mc, dch * 512:(dch + 1) * 512], start=(mc == 0), stop=(mc == 15)),
                              ["wo"], ["b%d" % dch], signal=(mc == 15))
                post_norm_residual(tt, [bank(d_) for d_ in range(4)], ["b%d" % d_ for d_ in range(4)], xt3[b_], "xt3%d" % b_, x1v[tt], ("x1s", tt), (gw, gtmp, junk))
            sc.barrier()
            if stop_after == 3:
                break

            gT = view(0, BF16, NFB * 1024, "p (f t) -> p f t", f=NFB)
            h2T = view(90112, BF16, 16 * 1024, "p (k t) -> p k t", k=16)
            wupb = [view(122880 + i * 8192, BF16, 16 * 256, "p (k c) -> p k c", k=16) for i in range(2)]
            tcv = [view(139264 + i * 4096, F32, 1024) for i in range(2)]
            wdnb = [view(147456 + i * 11264, BF16, NFB * 128, "p (f c) -> p f c", f=NFB) for i in range(2)]
            xt4 = [view(169984 + i * 8192, F32, 2048) for i in range(2)]
            xn4 = view(186368, BF16, 2048)
            gw = view(190464, F32, 2048)
            gtmp = view(139264, F32, 2048)
            ast = [view(0 + i * 16384, BF16, 16 * 512, "p (k c) -> p k c", k=16) for i in range(2)]
            junk = xn4
            assert 202752 <= ARENA
            cbk = view(198656, BF16, 16 * 128, "p (k c) -> p k c", k=16)
            build_gw(5, fpost, gw, gtmp, ast, cbk)
            cwv = cw[:].rearrange("p (b t) -> p b t", t=3)
            sc.op("dve", lambda e: e.memset(halo[:], 0.0), [], ["halo"])

            for H in range(2):
                tiles = list(range(8 * H, 8 * H + 8))
                halo_tiles = ([8 * H - 1] if H == 1 else []) + ([8 * H + 8] if H == 0 else [])
                for tt in halo_tiles + tiles:
                    b_ = tt % 2
                    dma("sp", xt4[b_], x1v[tt], "xt4%d" % b_, reads=[("x1s", tt)], writes=["xt4%d" % b_])
                    sc.op("act", lambda e, b_=b_: e.activation(out=xn4, in_=xt4[b_], func=AF.Square, accum_out=stat[:, 40:41]), ["xt4%d" % b_], ["xn4", "stat_f"])
                    sc.op("dve", lambda e: e.tensor_scalar(out=stat[:, 40:41], in0=stat[:, 40:41], scalar1=D * EPS, scalar2=None, op0=ALU.add), ["stat_f"], ["stat_f"])
                    sc.op("act", lambda e: e.activation(out=stat[:, 40:41], in_=stat[:, 40:41], func=AF.Ln), ["stat_f"], ["stat_f"])
                    sc.op("act", lambda e: e.activation(out=stat[:, 40:41], in_=stat[:, 40:41], func=AF.Exp, scale=-0.5), ["stat_f"], ["stat_f"])
                    sc.op("dve", lambda e, b_=b_: e.tensor_scalar(out=xn4, in0=xt4[b_], scalar1=stat[:, 40:41], scalar2=None, op0=ALU.mult), ["xt4%d" % b_, "stat_f"], ["xn4"])
                    for g in range(2):
                        pb = 6 + g
                        pv = bankbf(pb)
                        for j in range(8):
                            kc = g * 8 + j
                            sc.op("pe", lambda e, pv=pv, j=j, kc=kc: e.transpose(pv[:, j * 128:(j + 1) * 128], xn4[:, kc * 128:(kc + 1) * 128], ident[:]),
                                  ["xn4", "ident"], ["b%d" % pb], signal=(j == 7))
                        for j in range(8):
                            kc = g * 8 + j
                            if tt in tiles:
                                lt = tt - 8 * H
                                dst = h2T[:, kc, lt * 128:(lt + 1) * 128]; src = pv[:, j * 128:(j + 1) * 128]; wk = "h2T"
                            elif tt < 8 * H:
                                dst = halo[:, kc * 2:kc * 2 + 1]; src = pv[:, j * 128 + 127:j * 128 + 128]; wk = "halo"
                            else:
                                dst = halo[:, kc * 2 + 1:kc * 2 + 2]; src = pv[:, j * 128:j * 128 + 1]; wk = "halo"
                            sc.op("dve", lambda e, dst=dst, src=src, kc=kc: e.tensor_scalar(out=dst, in0=src, scalar1=a_ffn[:, kc:kc + 1], scalar2=msc[:, 32 + kc:33 + kc],
                                                                                          op0=ALU.mult, op1=ALU.add), ["b%d" % pb, "a_ffn", "msc"], [wk])
                if H == 1:
                    sc.op("dve", lambda e: e.memset(halo[:].rearrange("p (k c) -> p k c", c=2)[:, :, 1:2], 0.0), [], ["halo"])
                hv = halo[:].rearrange("p (k c) -> p k c", c=2)
                for fb in range(NFB):
                    wb = wupb[fb % 2]; wk = "wup%d" % (fb % 2)
                    dma("pool", wb, w_up[fb].rearrange("p (k c) -> p k c", k=16), wk, writes=[wk])
                    for which in range(2):
                        blk = which * NFB + fb
                        up = psb[which]
                        uks = ["b%d" % (2 * which), "b%d" % (2 * which + 1)]
                        hk = "b%d" % (4 + which)
                        for tc in range(2):
                            for kc in range(16):
                                sc.op("pe", lambda e, wb=wb, kc=kc, tc=tc, up=up, which=which: e.matmul(
                                    up[:, tc * 512:(tc + 1) * 512], lhsT=wb[:, kc, which * 128:(which + 1) * 128], rhs=h2T[:, kc, tc * 512:(tc + 1) * 512],
                                    start=(kc == 0), stop=(kc == 15)), [wk, "h2T"], [uks[tc]], signal=(kc == 15))
                        hp = bank(4 + which, 2)
                        for kc in range(16):
                            sc.op("pe", lambda e, wb=wb, kc=kc, hp=hp, which=which: e.matmul(hp, lhsT=wb[:, kc, which * 128:(which + 1) * 128], rhs=hv[:, kc, :],
                                                                                          start=(kc == 0), stop=(kc == 15)), [wk, "halo"], [hk], signal=(kc == 15))
                        t_ = tcv[which]; tk = "tcv%d" % which
                        sc.op("act", lambda e, t_=t_, up=up, blk=blk: e.activation(out=t_, in_=up[:, :], func=AF.Identity, scale=cwv[:, blk, 1:2], bias=cb[:, blk:blk + 1]),
                              uks + ["cw", "cb"], [tk])
                        sc.op("dve", lambda e, t_=t_, up=up, blk=blk: e.scalar_tensor_tensor(out=t_[:, 1:1024], in0=up[:, 0:1023], scalar=cwv[:, blk, 0:1], in1=t_[:, 1:1024],
                                                                                           op0=ALU.mult, op1=ALU.add), uks + ["cw", tk], [tk])
                        sc.op("dve", lambda e, t_=t_, up=up, blk=blk: e.scalar_tensor_tensor(out=t_[:, 0:1023], in0=up[:, 1:1024], scalar=cwv[:, blk, 2:3], in1=t_[:, 0:1023],
                                                                                           op0=ALU.mult, op1=ALU.add), uks + ["cw", tk], [tk])
                        sc.op("dve", lambda e, t_=t_, hp=hp, blk=blk: e.scalar_tensor_tensor(out=t_[:, 0:1], in0=hp[:, 0:1], scalar=cwv[:, blk, 0:1], in1=t_[:, 0:1],
                                                                                           op0=ALU.mult, op1=ALU.add), [hk, "cw", tk], [tk])
                        sc.op("dve", lambda e, t_=t_, hp=hp, blk=blk: e.scalar_tensor_tensor(out=t_[:, 1023:1024], in0=hp[:, 1:2], scalar=cwv[:, blk, 2:3], in1=t_[:, 1023:1024],
                                                                                           op0=ALU.mult, op1=ALU.add), [hk, "cw", tk], [tk])
                    sc.op("act", lambda e: e.activation(out=tcv[0], in_=tcv[0], func=AF.Gelu_apprx_tanh), ["tcv0"], ["tcv0"])
                    sc.op("dve", lambda e, fb=fb: e.tensor_tensor(out=gT[:, fb, :], in0=tcv[0], in1=tcv[1], op=ALU.mult), ["tcv0", "tcv1"], ["gT"])
                yT = h2T
                for db in range(16):
                    wb = wdnb[db % 2]; wk = "wdn%d" % (db % 2)
                    dma("pool", wb, w_dn[db].rearrange("p (f c) -> p f c", f=NFB), wk, writes=[wk])
                    for tc in range(2):
                        pb = (db * 2 + tc) % 4
                        for fb in range(NFB):
                            sc.op("pe", lambda e, wb=wb, fb=fb, tc=tc, pb=pb: e.matmul(bank(pb), lhsT=wb[:, fb, :], rhs=gT[:, fb, tc * 512:(tc + 1) * 512],
                                                                                    start=(fb == 0), stop=(fb == NFB - 1)), [wk, "gT"], ["b%d" % pb], signal=(fb == NFB - 1))
                        copy(evac_eng(), yT[:, db, tc * 512:(tc + 1) * 512], bank(pb), ["b%d" % pb], ["h2T"])
                for lt in range(8):
                    tt = 8 * H + lt
                    b_ = tt % 2
                    dma("sp", xt4[b_], x1v[tt], "xt4%d" % b_, reads=[("x1s", tt)], writes=["xt4%d" % b_])
                    ysrc = []; ykeys = []
                    for g in range(2):
                        pb = 4 + g * 2
                        pb = 4 + g
                        pv = bankbf(pb)
                        for j in range(8):
                            db = g * 8 + j
                            sc.op("pe", lambda e, pv=pv, j=j, db=db, lt=lt: e.transpose(pv[:, j * 128:(j + 1) * 128], yT[:, db, lt * 128:(lt + 1) * 128], ident[:]),
                                  ["h2T", "ident"], ["b%d" % pb], signal=(j == 7))
                        ysrc += [pv[:, 0:512], pv[:, 512:1024]]
                        ykeys += ["b%d" % pb, "b%d" % pb]
                    post_norm_residual(tt, ysrc, ykeys, xt4[b_], "xt4%d" % b_, outv[tt], ("out", tt), (gw, gtmp, junk))
                sc.barrier()

        sc.finish()
        stuck, val = sc.simulate()
        if stuck:
            raise RuntimeError("semaphore dead-lock in recorded program: %r  sems=%r" % (stuck, {k: v for k, v in val.items() if not k.startswith("d:")}))
        build.sem_counts = {k: v for k, v in val.items()}
        sc.replay(E(nc.Block()))
    return nc


_CACHE = {}


def _prep_shared(inp):
    f = lambda a: np.ascontiguousarray(a, dtype=np.float32)
    w_in = inp["w_in"][0]
    fm_cols = list(range(0, 768)) + list(range(832, 2880))
    w_fm = w_in[:, fm_cols].reshape(16, 128, 22, 128).transpose(2, 1, 0, 3).reshape(22, 128, 16 * 128)
    w_kr = w_in[:, 768:832].reshape(16, 128, 64).transpose(1, 0, 2).reshape(128, 16 * 64)
    w_dv = w_in[:, 2880:3904].reshape(16, 128, 8, 128).transpose(2, 1, 0, 3).reshape(8, 128, 16 * 128)
    wuq = inp["w_uq"][0].reshape(512, 8, 192)
    wuq_r = np.concatenate([wuq[:, :, :128].reshape(512, 1024), wuq[:, :, 128:].reshape(512, 512)], axis=1)
    wuq_r = wuq_r.reshape(4, 128, 1536).transpose(1, 0, 2).reshape(128, 4 * 1536)
    wukv = inp["w_ukv"][0].reshape(256, 8, 256)
    wukv_r = np.concatenate([wukv[:, :, :128].reshape(256, 1024), wukv[:, :, 128:].reshape(256, 1024)], axis=1)
    wukv_r = wukv_r.reshape(2, 128, 2048).transpose(1, 0, 2).reshape(128, 2 * 2048)
    w_o = inp["w_o"][0].reshape(16, 128, 2048).transpose(1, 0, 2).reshape(128, 16 * 2048)
    wup = inp["w_up"][0].reshape(16, 128, 2, NFB, 128).transpose(3, 1, 0, 2, 4).reshape(NFB, 128, 16 * 256)
    cw = inp["conv_w"][0].reshape(3, 88, 128).transpose(2, 1, 0).reshape(128, 88 * 3)
    cb = inp["conv_b"][0].reshape(88, 128).T
    wdn = inp["w_down"][0].reshape(NFB, 128, 16, 128).transpose(2, 1, 0, 3).reshape(16, 128, NFB * 128)
    inv = (1.0 / (np.float32(ROPE_THETA) ** (np.arange(0, 64, 2, dtype=np.float32) / np.float32(64)))).astype(np.float32)
    selh = np.zeros((8, 8, 128), np.float32)
    for h in range(8):
        selh[h, h, :] = 1.0
    d = {
        "relb": f(inp["rel_bias"]), "relbT": f(inp["rel_bias"].T), "ident": np.eye(128, dtype=np.float32),
        "nvals": f(np.broadcast_to(255.0 - np.arange(512, dtype=np.float32), (8, 512))),
        "selh": f(selh.reshape(8, 1024)), "invf": f(np.broadcast_to(inv, (128, 32))),
        "ada_r": f(inp["ada_w"][0].reshape(16, 128, 24, 512).transpose(2, 1, 0, 3).reshape(24, 128, 16 * 512)), "ada_bT": f(inp["ada_b"][0].reshape(96, 128).T), "ada_b": f(inp["ada_b"]),
        "apnT": f(inp["attn_pre_norm"][0].reshape(16, 128).T), "fpnT": f(inp["ffn_pre_norm"][0].reshape(16, 128).T),
        "apost": f(inp["attn_post_norm"]), "fpost": f(inp["ffn_post_norm"]),
        "w_fm": f(w_fm), "w_kr": f(w_kr), "w_dv": f(w_dv),
        "qnT": f(inp["q_norm"][0].reshape(4, 128).T), "kvnT": f(inp["kv_norm"][0].reshape(2, 128).T),
        "w_uq": f(wuq_r), "w_ukv": f(wukv_r),
        "lamv": f(np.concatenate([inp["lambda_q1"][0], inp["lambda_k1"][0], inp["lambda_q2"][0], inp["lambda_k2"][0]])[None, :]),
        "sublnT": f(inp["diff_subln"][0][:, None]),
        "w_o": f(w_o), "w_up": f(wup), "cwT": f(cw), "cbT": f(cb), "w_dn": f(wdn),
    }
    return d


N_CORES = 8
NB = 8 // N_CORES


def _core_inputs(inp, shared, batches):
    m = dict(shared)
    m["x"] = np.ascontiguousarray(np.concatenate([inp["x"][b] for b in batches], axis=0), dtype=np.float32)
    m["cT"] = np.ascontiguousarray(np.concatenate([inp["c"][b].reshape(16, 128).T for b in batches], axis=1), dtype=np.float32)
    m["pos"] = np.ascontiguousarray(np.concatenate([inp["positions"][b].reshape(16, 128).T for b in batches], axis=1), dtype=np.int32)
    return m


def kernel(**inp):
    inp = {k: np.asarray(v) for k, v in inp.items()}
    if "nc" not in _CACHE:
        _CACHE["nc"] = build(nb=NB)
    nc = _CACHE["nc"]
    shared = _prep_shared(inp)
    in_maps = [_core_inputs(inp, shared, list(range(i * NB, (i + 1) * NB))) for i in range(N_CORES)]
    res = run_bass_kernel_spmd(nc, in_maps, core_ids=list(range(N_CORES)))
    outs = [np.asarray(r["out"], dtype=np.float32).reshape(NB, S, D) for r in res.results]
    return np.concatenate(outs, axis=0)
```

```python
import math
import os
from contextlib import ExitStack

import numpy as np
import concourse.bass as bass
import concourse.mybir as mybir
from concourse.bass_utils import run_bass_kernel_spmd

F32 = mybir.dt.float32
BF16 = mybir.dt.bfloat16
I32 = mybir.dt.int32
U8 = mybir.dt.uint8
AF = mybir.ActivationFunctionType
ALU = mybir.AluOpType
AX = mybir.AxisListType

D = 2048
S = 2048
NT = 16
DFF = 5632
NFB = 44
EPS = 1e-6
MLA_SCALE = 192 ** -0.5
DIFF_SCALE = 64 ** -0.5
LAMBDA_INIT = 0.8 - 0.6 * math.exp(0.0)
ROPE_THETA = 10000.0

BUCKETS = {0: (0, 0), 8: (-11, -8), 9: (-15, -12), 10: (-22, -16), 11: (-31, -23), 12: (-45, -32),
           13: (-63, -46), 14: (-90, -64), 15: (-100000, -91), 24: (8, 11), 25: (12, 15), 26: (16, 22),
           27: (23, 31), 28: (32, 45), 29: (46, 63), 30: (64, 90), 31: (91, 100000)}
for _b in range(1, 8):
    BUCKETS[_b] = (-_b, -_b)
    BUCKETS[16 + _b] = (_b, _b)


SHARED = [("relb", (32, 8)), ("relbT", (8, 32)), ("ident", (128, 128)), ("nvals", (8, 512)), ("selh", (8, 1024)), ("invf", (128, 32)),
          ("ada_r", (24, 128, 16 * 512)), ("ada_bT", (128, 96)), ("ada_b", (1, 6 * D)), ("apnT", (128, 16)), ("fpnT", (128, 16)),
          ("apost", (1, D)), ("fpost", (1, D)), ("w_fm", (22, 128, 2048)), ("w_kr", (128, 1024)), ("w_dv", (8, 128, 2048)),
          ("qnT", (128, 4)), ("kvnT", (128, 2)), ("w_uq", (128, 4 * 1536)), ("w_ukv", (128, 2 * 2048)), ("lamv", (1, 256)),
          ("sublnT", (128, 1)), ("w_o", (128, 16 * 2048)), ("w_up", (NFB, 128, 16 * 256)), ("cwT", (128, 88 * 3)), ("cbT", (128, 88)),
          ("w_dn", (16, 128, NFB * 128))]
CW = 2048


def _layout():
    off = [0, 0]
    offs = {}
    g = 0
    for name, shp in SHARED:
        if name == "lamv":
            g = 1
        offs[name] = (g, off[g])
        n = int(np.prod(shp))
        off[g] += (n + 63) // 64 * 64
    rows = [(o + 8 * CW - 1) // (8 * CW) for o in off]
    return offs, rows


class Sched:
    COMPUTE = ("pe", "act", "dve")
    QUEUES = ("sp", "pool")

    def __init__(self, nc, es):
        self.nc = nc
        self.es = es
        self.streams = {e: [] for e in self.COMPUTE + self.QUEUES}
        self.semobj = {e: es.enter_context(nc.semaphore("s_" + e)) for e in self.COMPUTE}
        self.cnt = {e: 0 for e in self.COMPUTE}
        self.dcnt = {}
        self.waited = {e: {} for e in self.COMPUTE + self.QUEUES}
        self.lastw = {}
        self.readers = {}

    def _dsem(self, name):
        k = "d:" + name
        if k not in self.semobj:
            self.semobj[k] = self.es.enter_context(self.nc.semaphore("q_" + name))
            self.dcnt[k] = 0
        return k

    def _wait(self, eng, tok):
        k, v = tok
        if self.waited[eng].get(k, 0) < v:
            self.waited[eng][k] = v
            self.streams[eng].append(("w", k, v))

    def op(self, eng, fn, reads=(), writes=(), signal=True, dma=None):
        deps = []
        for r in reads:
            if r in self.lastw:
                deps.append(self.lastw[r])
        for w in writes:
            if w in self.lastw:
                deps.append(self.lastw[w])
            deps.extend(self.readers.get(w, ()))
        for t in deps:
            if eng == "pe" and t[0] == "pe":
                continue
            self._wait(eng, t)
        if dma is not None:
            k = self._dsem(dma)
            self.dcnt[k] += 16
            tok = (k, self.dcnt[k])
            self.streams[eng].append(("i", fn, k, 16))
        else:
            if signal:
                self.cnt[eng] += 1
                tok = (eng, self.cnt[eng])
                self.streams[eng].append(("i", fn, eng, 1))
            else:
                tok = (eng, self.cnt[eng] + 1)
                self.streams[eng].append(("i", fn, None, 0))
        for w in writes:
            self.lastw[w] = tok
            self.readers[w] = []
        for r in reads:
            self.readers.setdefault(r, []).append(tok)
        return tok

    def barrier(self):
        toks = [(e, self.cnt[e]) for e in self.COMPUTE if self.cnt[e] > 0]
        toks += [(k, v) for k, v in self.dcnt.items() if v > 0]
        for e in self.COMPUTE + self.QUEUES:
            for t in toks:
                if not (e == "pe" and t[0] == "pe"):
                    self._wait(e, t)
        self.lastw.clear()
        self.readers.clear()

    def finish(self):
        for k, v in self.dcnt.items():
            if v > 0:
                self._wait("sp", (k, v))

    def simulate(self):
        val = {k: 0 for k in self.semobj}
        pc = {e: 0 for e in self.streams}
        progress = True
        while progress:
            progress = False
            for e, st in self.streams.items():
                while pc[e] < len(st):
                    a = st[pc[e]]
                    if a[0] == "w":
                        if val[a[1]] < a[2]:
                            break
                    elif a[2] is not None:
                        val[a[2]] += a[3]
                    pc[e] += 1
                    progress = True
        stuck = {e: (pc[e], len(st), st[pc[e]][:3] if pc[e] < len(st) else None) for e, st in self.streams.items() if pc[e] < len(st)}
        return stuck, val

    def replay(self, block):
        def mk(stream):
            def body(e):
                for a in stream:
                    if a[0] == "w":
                        e.wait_ge(self.semobj[a[1]], a[2])
                    else:
                        ins = a[1](e)
                        if a[2] is not None:
                            ins.then_inc(self.semobj[a[2]], a[3])
            return body
        block.tensor(mk(self.streams["pe"]))
        block.scalar(mk(self.streams["act"]))
        block.vector(mk(self.streams["dve"]))
        block.sync(mk(self.streams["sp"]))
        block.gpsimd(mk(self.streams["pool"]))


def build(nb=1, stop_after=99, dbg=False):
    nc = bass.Bass("TRN2", target_bir_lowering=False)

    skip = set()
    if stop_after in (12, 13, 14, 15, 16):
        skip |= {"w_up", "w_dn", "fpost", "w_o", "apost", "ada_b"}
    elif stop_after < 4:
        skip |= {"w_up", "w_dn", "fpost"}
    if stop_after < 3:
        skip |= {"w_o", "apost", "ada_b"}
    if stop_after < 2:
        skip |= {"w_uq", "w_ukv"}

    def din(name, shape, dt=F32):
        if name in skip:
            return None
        return nc.dram_tensor(name, list(shape), dt, kind="ExternalInput").ap()

    x = din("x", [nb * S, D])
    cT = din("cT", [128, nb * 16])
    pos = din("pos", [128, nb * 16], I32)
    shp = dict(SHARED)
    relb = din("relb", shp["relb"]); relbTd = din("relbT", shp["relbT"]); identd = din("ident", shp["ident"]); nvals = din("nvals", shp["nvals"])
    selh = din("selh", shp["selh"]); invf = din("invf", shp["invf"]); ada_r = din("ada_r", shp["ada_r"]); ada_bT = din("ada_bT", shp["ada_bT"])
    ada_b = din("ada_b", shp["ada_b"]); apnT = din("apnT", shp["apnT"]); fpnT = din("fpnT", shp["fpnT"]); apost = din("apost", shp["apost"])
    fpost = din("fpost", shp["fpost"]); w_fm = din("w_fm", shp["w_fm"]); w_kr = din("w_kr", shp["w_kr"]); w_dv = din("w_dv", shp["w_dv"])
    qnT = din("qnT", shp["qnT"]); kvnT = din("kvnT", shp["kvnT"]); w_uq = din("w_uq", shp["w_uq"]); w_ukv = din("w_ukv", shp["w_ukv"])
    lamv = din("lamv", shp["lamv"]); sublnT = din("sublnT", shp["sublnT"]); w_o = din("w_o", shp["w_o"]); w_up = din("w_up", shp["w_up"])
    cwT = din("cwT", shp["cwT"]); cbT = din("cbT", shp["cbT"]); w_dn = din("w_dn", shp["w_dn"])
    out = nc.dram_tensor("out", [nb * S, D], F32, kind="ExternalOutput").ap()
    x1s = nc.dram_tensor("x1s", [S, D], F32, kind="ExternalOutput" if dbg else "Internal").ap()
    bsc = nc.dram_tensor("bsc", [8, 128 * 512], F32, kind="Internal").ap()
    dbg_out = nc.dram_tensor("dbg", [128, 4096], F32, kind="ExternalOutput").ap() if dbg else None

    with ExitStack() as es:
        E = es.enter_context
        ARENA = 202752
        arena = E(nc.sbuf_tensor("arena", [128, ARENA], U8))

        def view(off, dt, n, pat=None, **kw):
            sz = {F32: 4, BF16: 2, I32: 4}[dt]
            v = arena[:, off:off + n * sz].bitcast(dt)
            if pat:
                v = v.rearrange(pat, **kw)
            return v

        def sb(name, shape, dt=F32):
            return E(nc.sbuf_tensor(name, list(shape), dt))

        ident = sb("identb", [128, 128], BF16)
        onesb = sb("onesb", [128, 128], BF16)
        sel1 = sb("sel1", [128, 128], BF16)
        sel2 = sb("sel2", [128, 128], BF16)
        cact = sb("cact", [128, 16], BF16)
        msc = sb("msc", [128, 64])
        adab = sb("adab", [128, 96])
        apn = sb("apn", [128, 16]); fpn = sb("fpn", [128, 16])
        a_pre = sb("a_pre", [128, 16]); a_ffn = sb("a_ffn", [128, 16])
        qnw = sb("qnw", [128, 4]); kvnw = sb("kvnw", [128, 2])
        posi = sb("posi", [128, 16], I32)
        posf = sb("posf", [128, 16])
        invt = sb("invt", [128, 32])
        cost = sb("cost", [128, 16 * 32]); sint = sb("sint", [128, 16 * 32])
        tabbc = sb("tabbc", [128, 16])
        lamb = sb("lamb", [128, 256])
        lamt = sb("lamt", [128, 8])
        subw = sb("subw", [128, 1])
        cw = sb("cw", [128, 88 * 3]); cb = sb("cb", [128, 88])
        stat = sb("stat", [128, 64])
        halo = sb("halo", [128, 16 * 2], BF16)
        st2 = sb("st2", [128, 64])
        psb = [E(nc.psum_tensor("pp%d" % i, [128, 1024], F32)) for i in range(4)]

        def bank(b, n=512, off=0):
            return psb[b // 2][:, (b % 2) * 512 + off:(b % 2) * 512 + off + n]

        def bankbf(b):
            return psb[b // 2][:, (b % 2) * 512:(b % 2) * 512 + 512].bitcast(BF16)

        sc = Sched(nc, es)
        rr = [0]

        def evac_eng():
            rr[0] ^= 1
            return "act" if rr[0] else "dve"

        def copy(eng, out_, in_, reads, writes):
            if eng == "act":
                sc.op("act", lambda e: e.activation(out=out_, in_=in_, func=AF.Copy), reads, writes)
            else:
                sc.op("dve", lambda e: e.tensor_copy(out=out_, in_=in_), reads, writes)

        def dma(q, out_, in_, name, reads=(), writes=()):
            sc.op(q, lambda e: e.dma_start(out=out_, in_=in_), reads, writes, dma=name)

        OFF_A = 0
        OFF_P = 65536
        hT = view(OFF_A, BF16, 16 * 2048, "p (k t) -> p k t", k=16)
        dqT = view(OFF_P, BF16, 8 * 2048, "p (h t) -> p h t", h=8)
        dkT = view(OFF_P + 32768, BF16, 8 * 2048, "p (h t) -> p h t", h=8)
        dvaug = view(OFF_P + 65536, BF16, 16 * 8 * 132, "p (t h c) -> p t h c", t=16, h=8)
        OFF_L = OFF_P + 65536 + 33792
        qlatT = view(OFF_L, BF16, 4 * 2048, "p (k t) -> p k t", k=4)
        kvlatT = view(OFF_L + 16384, BF16, 2 * 2048, "p (k t) -> p k t", k=2)
        kropeT = view(OFF_L + 24576, BF16, 2048)
        OFF_S = OFF_L + 28672
        wst = [view(OFF_S + i * 4096, BF16, 16 * 128, "p (k c) -> p k c", k=16) for i in range(2)]
        xt = [view(OFF_P + i * 8192, F32, 2048) for i in range(2)]
        xn = [view(OFF_P + 16384 + i * 4096, BF16, 2048) for i in range(2)]
        adast = [view(OFF_P + 24576 + i * 16384, BF16, 16 * 512, "p (k c) -> p k c", k=16) for i in range(2)]
        zt = view(OFF_P + 57344, F32, 512)
        ztm = view(OFF_P + 59392, F32, 512)
        ztm2 = view(OFF_P + 61440, F32, 512)
        nv = view(OFF_P + 63488, F32, 512)
        selhs = view(OFF_P + 65536, F32, 1024)
        zrep = view(OFF_P + 69632, F32, 512)
        ang = view(OFF_P + 71680, F32, 512)
        angq = view(OFF_P + 73728, F32, 512)
        angi = view(OFF_P + 75776, I32, 512)
        idf = view(OFF_P + 77824, F32, 128)

        for lb in range(nb):
            if lb == 0:
                dma("sp", idf, identd, "c0", writes=["idf"])
                sc.op("dve", lambda e: e.tensor_copy(out=ident[:], in_=idf), ["idf"], ["ident"])
                sc.op("dve", lambda e: e.memset(onesb[:], 1.0), [], ["onesb"])
                sc.op("dve", lambda e: e.memset(sel1[:], 0.0), [], ["sel1"])
                sc.op("dve", lambda e: e.memset(sel1[0:64, :], 1.0), [], ["sel1"])
                sc.op("dve", lambda e: e.memset(sel2[:], 0.0), [], ["sel2"])
                sc.op("dve", lambda e: e.memset(sel2[64:128, :], 1.0), [], ["sel2"])
                sc.op("dve", lambda e: e.memset(halo[:], 0.0), [], ["halo"])
                small = [(adab, ada_bT), (apn, apnT), (fpn, fpnT), (qnw, qnT), (kvnw, kvnT), (invt, invf),
                         (subw, sublnT), (cw, cwT), (cb, cbT)]
                for i, (d_, s_) in enumerate(small):
                    dma("sp", d_[:], s_, "c%d" % (1 + i), writes=[d_.name])
            dma("sp", posi[:], pos[:, lb * 16:(lb + 1) * 16], "c10", writes=["posi"])
            if lb == 0:
                dma("sp", tabbc[:, 0:8], relb[15:16, :].partition_broadcast(128), "c11", writes=["tabbc0"])
                dma("sp", tabbc[:, 8:16], relb[31:32, :].partition_broadcast(128), "c12", writes=["tabbc1"])
                dma("sp", lamb[:], lamv.partition_broadcast(128), "c13", writes=["lamb"])
            ctf = view(OFF_P + 78848, F32, 16)
            dma("sp", ctf, cT[:, lb * 16:(lb + 1) * 16], "c14", writes=["ctf"])
            sc.op("act", lambda e: e.activation(out=cact[:], in_=ctf, func=AF.Silu), ["ctf"], ["cact"])

            mps = bank(0, 64)
            k_ada = [0]

            def ada_piece(ci, ch, piece):
                k_ = k_ada[0]
                st = adast[k_ % 2]
                dma("pool", st, ada_r[ch * 4 + piece].rearrange("p (k c) -> p k c", k=16), "ada%d" % (k_ % 2), writes=["adast%d" % (k_ % 2)])
                for fcl in range(4):
                    col = ci * 16 + piece * 4 + fcl
                    for kc in range(16):
                        sc.op("pe", lambda e, st=st, fcl=fcl, kc=kc, col=col: e.matmul(
                            mps[:, col:col + 1], lhsT=st[:, kc, fcl * 128:(fcl + 1) * 128], rhs=cact[:, kc:kc + 1],
                            start=(kc == 0), stop=(kc == 15)),
                            ["adast%d" % (k_ % 2), "cact"], ["mps"], signal=(kc == 15))
                k_ada[0] += 1

            SQD = math.sqrt(D)

            def ada_finish(pairs, dst, wn, ci_scale):
                for ci, ch in pairs:
                    sc.op("dve", lambda e, ci=ci, ch=ch: e.tensor_tensor(out=msc[:, ci * 16:(ci + 1) * 16], in0=mps[:, ci * 16:(ci + 1) * 16],
                                                                        in1=adab[:, ch * 16:(ch + 1) * 16], op=ALU.add),
                          ["mps", "adab"], [("msc", ci)])
                sc.op("dve", lambda e: e.scalar_tensor_tensor(
                    out=dst[:], in0=msc[:, ci_scale * 16:(ci_scale + 1) * 16], scalar=1.0, in1=wn[:], op0=ALU.add, op1=ALU.mult),
                    [("msc", ci_scale), wn.name], [dst.name])
                sc.op("dve", lambda e: e.tensor_scalar(out=dst[:], in0=dst[:], scalar1=SQD, scalar2=None, op0=ALU.mult),
                      [dst.name], [dst.name])

            for ci, ch in ((0, 0), (1, 1)):
                for piece in range(4):
                    ada_piece(ci, ch, piece)
            ada_finish(((0, 0), (1, 1)), a_pre, apn, 1)
            ada_later = [(ci, ch, piece) for ci, ch in ((2, 3), (3, 4)) for piece in range(4)]

            if lb == 0:
                sc.op("dve", lambda e: e.tensor_tensor(out=lamb[:, 0:64], in0=lamb[:, 0:64], in1=lamb[:, 64:128], op=ALU.mult), ["lamb"], ["lamb"])
                sc.op("dve", lambda e: e.tensor_tensor(out=lamb[:, 128:192], in0=lamb[:, 128:192], in1=lamb[:, 192:256], op=ALU.mult), ["lamb"], ["lamb"])
                sc.op("dve", lambda e: e.tensor_reduce(out=lamt[:, 0:1], in_=lamb[:, 0:64], axis=AX.X, op=ALU.add), ["lamb"], ["lamt"])
                sc.op("dve", lambda e: e.tensor_reduce(out=lamt[:, 1:2], in_=lamb[:, 128:192], axis=AX.X, op=ALU.add), ["lamb"], ["lamt"])
                sc.op("act", lambda e: e.activation(out=lamt[:, 2:4], in_=lamt[:, 0:2], func=AF.Exp), ["lamt"], ["lamt"])
                sc.op("dve", lambda e: e.scalar_tensor_tensor(out=lamt[:, 4:5], in0=lamt[:, 3:4], scalar=-LAMBDA_INIT, in1=lamt[:, 2:3],
                                                              op0=ALU.add, op1=ALU.subtract), ["lamt"], ["lamt"])
                sc.op("dve", lambda e: e.tensor_scalar(out=subw[:], in0=subw[:], scalar1=1.0 - LAMBDA_INIT, scalar2=None, op0=ALU.mult),
                      ["subw"], ["subw"])

            sc.op("dve", lambda e: e.tensor_copy(out=posf[:], in_=posi[:]), ["posi"], ["posf"])
            for tt in range(16):
                sc.op("dve", lambda e, tt=tt: e.tensor_scalar(out=ang[:, tt * 32:(tt + 1) * 32], in0=invt[:], scalar1=posf[:, tt:tt + 1],
                                                             scalar2=None, op0=ALU.mult), ["posf", "invt"], ["ang"])
            C1 = 6.28125
            C2 = 2 * math.pi - C1
            for (dst, shift) in ((sint, 0.0), (cost, math.pi / 2)):
                sc.op("dve", lambda e, shift=shift: e.tensor_scalar(out=angq, in0=ang, scalar1=shift, scalar2=1.0 / (2 * math.pi),
                                                                   op0=ALU.add, op1=ALU.mult), ["ang"], ["angq"])
                sc.op("dve", lambda e: e.tensor_copy(out=angi, in_=angq), ["angq"], ["angi"])
                sc.op("dve", lambda e: e.tensor_copy(out=angq, in_=angi), ["angi"], ["angq"])
                sc.op("dve", lambda e, shift=shift, dst=dst: e.tensor_scalar(out=dst[:], in0=ang, scalar1=shift, scalar2=None, op0=ALU.add),
                      ["ang"], [dst.name])
                sc.op("dve", lambda e, dst=dst: e.scalar_tensor_tensor(out=dst[:], in0=angq, scalar=-C1, in1=dst[:], op0=ALU.mult, op1=ALU.add),
                      ["angq", dst.name], [dst.name])
                sc.op("dve", lambda e, dst=dst: e.scalar_tensor_tensor(out=dst[:], in0=angq, scalar=-C2, in1=dst[:], op0=ALU.mult, op1=ALU.add),
                      ["angq", dst.name], [dst.name])
                sc.op("dve", lambda e, dst=dst: e.tensor_scalar(out=angq, in0=dst[:], scalar1=math.pi, scalar2=-2 * math.pi, op0=ALU.is_gt, op1=ALU.mult),
                      [dst.name], ["angq"])
                sc.op("dve", lambda e, dst=dst: e.tensor_tensor(out=dst[:], in0=dst[:], in1=angq, op=ALU.add), ["angq", dst.name], [dst.name])
                sc.op("dve", lambda e, dst=dst: e.tensor_scalar(out=angq, in0=dst[:], scalar1=-math.pi, scalar2=2 * math.pi, op0=ALU.is_lt, op1=ALU.mult),
                      [dst.name], ["angq"])
                sc.op("dve", lambda e, dst=dst: e.tensor_tensor(out=dst[:], in0=dst[:], in1=angq, op=ALU.add), ["angq", dst.name], [dst.name])
                sc.op("dve", lambda e, dst=dst: e.tensor_scalar(out=dst[:], in0=dst[:], scalar1=math.pi, scalar2=-math.pi, op0=ALU.min, op1=ALU.max),
                      [dst.name], [dst.name])
                sc.op("act", lambda e, dst=dst: e.activation(out=dst[:], in_=dst[:], func=AF.Sin), [dst.name], [dst.name])

            if lb == 0:
                dma("sp", nv[0:8, :], nvals, "c15", writes=["nv"])
                dma("sp", selhs[0:8, :], selh, "c16", writes=["selhs"])
                relbT = view(OFF_P + 79872, F32, 32)
                dma("sp", relbT[0:8, :], relbTd, "c17", writes=["relbT"])
                sc.op("dve", lambda e: e.memset(zt[0:8, :], 0.0), [], ["zt"])
                for b_, (lo, hi) in sorted(BUCKETS.items()):
                    sc.op("dve", lambda e, lo=lo: e.tensor_scalar(out=ztm[0:8, :], in0=nv[0:8, :], scalar1=float(lo) - 0.5, scalar2=None, op0=ALU.is_gt),
                          ["nv"], ["ztm"])
                    sc.op("dve", lambda e, hi=hi: e.scalar_tensor_tensor(out=ztm2[0:8, :], in0=nv[0:8, :], scalar=float(hi) + 0.5, in1=ztm[0:8, :],
                                                                        op0=ALU.is_lt, op1=ALU.mult), ["nv", "ztm"], ["ztm2"])
                    sc.op("dve", lambda e, b_=b_: e.scalar_tensor_tensor(out=zt[0:8, :], in0=ztm2[0:8, :], scalar=relbT[0:8, b_:b_ + 1], in1=zt[0:8, :],
                                                                        op0=ALU.mult, op1=ALU.add), ["ztm2", "relbT", "zt"], ["zt"])
                for h in range(8):
                    sc.op("pe", lambda e, h=h: e.matmul(bank(1), lhsT=selhs[0:8, h * 128:(h + 1) * 128], rhs=zt[0:8, :], start=True, stop=True),
                          ["selhs", "zt"], ["b1"])
                    sc.op("dve", lambda e: e.tensor_copy(out=zrep, in_=bank(1)), ["b1"], ["zrep"])
                    dma("sp", bsc[h].rearrange("(r c) -> r c", c=512), zrep, "bsc", reads=["zrep"], writes=["bsc%d" % h])

            xv = x[lb * S:(lb + 1) * S, :].rearrange("(t p) d -> t p d", p=128)
            for tt in range(NT):
                b_ = tt % 2
                dma("sp", xt[b_], xv[tt], "xt%d" % b_, writes=["xt%d" % b_])
                sc.op("act", lambda e, b_=b_, tt=tt: e.activation(out=xn[b_], in_=xt[b_], func=AF.Square, accum_out=stat[:, tt:tt + 1]),
                      ["xt%d" % b_], ["xn%d" % b_, "stat_a%d" % tt])
                sc.op("dve", lambda e, tt=tt: e.tensor_scalar(out=stat[:, 16 + tt:17 + tt], in0=stat[:, tt:tt + 1], scalar1=D * EPS, scalar2=None, op0=ALU.add),
                      ["stat_a%d" % tt], ["stat_b%d" % tt])
                sc.op("act", lambda e, tt=tt: e.activation(out=stat[:, 16 + tt:17 + tt], in_=stat[:, 16 + tt:17 + tt], func=AF.Ln),
                      ["stat_b%d" % tt], ["stat_b%d" % tt])
                sc.op("act", lambda e, tt=tt: e.activation(out=stat[:, 16 + tt:17 + tt], in_=stat[:, 16 + tt:17 + tt], func=AF.Exp, scale=-0.5),
                      ["stat_b%d" % tt], ["stat_b%d" % tt])
                sc.op("dve", lambda e, b_=b_, tt=tt: e.tensor_scalar(out=xn[b_], in0=xt[b_], scalar1=stat[:, 16 + tt:17 + tt], scalar2=None, op0=ALU.mult),
                      ["xt%d" % b_, "stat_b%d" % tt], ["xn%d" % b_])
                for g in range(2):
                    pb = 2 + g
                    pv = bankbf(pb)
                    for j in range(8):
                        kc = g * 8 + j
                        sc.op("pe", lambda e, pv=pv, j=j, kc=kc, b_=b_: e.transpose(pv[:, j * 128:(j + 1) * 128], xn[b_][:, kc * 128:(kc + 1) * 128], ident[:]),
                              ["xn%d" % b_, "ident"], ["b%d" % pb], signal=(j == 7))
                    for j in range(8):
                        kc = g * 8 + j
                        if evac_eng() == "act":
                            sc.op("act", lambda e, pv=pv, j=j, kc=kc, tt=tt: e.activation(
                                out=hT[:, kc, tt * 128:(tt + 1) * 128], in_=pv[:, j * 128:(j + 1) * 128], func=AF.Identity,
                                scale=a_pre[:, kc:kc + 1], bias=msc[:, kc:kc + 1]), ["b%d" % pb, "a_pre", ("msc", 0)], [("hT", tt)])
                        else:
                            sc.op("dve", lambda e, pv=pv, j=j, kc=kc, tt=tt: e.tensor_scalar(
                                out=hT[:, kc, tt * 128:(tt + 1) * 128], in0=pv[:, j * 128:(j + 1) * 128],
                                scalar1=a_pre[:, kc:kc + 1], scalar2=msc[:, kc:kc + 1], op0=ALU.mult, op1=ALU.add),
                                ["b%d" % pb, "a_pre", ("msc", 0)], [("hT", tt)])
                if tt % 2 == 1 and ada_later:
                    ada_piece(*ada_later.pop(0))
            while ada_later:
                ada_piece(*ada_later.pop(0))
            ada_finish(((2, 3), (3, 4)), a_ffn, fpn, 3)
            sc.barrier()

            def dbg_dump(ap_f32_src, ncols, reads):
                t_ = view(OFF_A, F32, 4096) if False else None
                dma("sp", dbg_out[:, 0:ncols], ap_f32_src, "dbg", reads=reads)

            if stop_after == 0:
                tmpf = view(OFF_P, F32, 4096)
                sc.op("dve", lambda e: e.tensor_copy(out=tmpf[:, 0:2048], in_=hT[:, 0, :]), [], ["tmpf"])
                sc.op("dve", lambda e: e.tensor_copy(out=tmpf[:, 2048:4096], in_=hT[:, 15, :]), [], ["tmpf"])
                dbg_dump(tmpf, 4096, ["tmpf"])
                sc.finish()
                sc.replay(E(nc.Block()))
                return nc

            sc.op("dve", lambda e: e.memset(dvaug[:, :, :, 128:132], 1.0), [], ["dvaug"])
            hkeys = lambda tc: [("hT", 4 * tc + i) for i in range(4)]
            nw = 0
            for j in range(22):
                wb = wst[nw % 2]; wk = "wst%d" % (nw % 2); nw += 1
                dma("pool", wb, w_fm[j].rearrange("p (k c) -> p k c", k=16), wk, writes=[wk])
                for tc in range(4):
                    pb = 2 * (tc % 2) + 0 if False else (tc % 4)
                    pb = 4 + (tc % 4)
                    for kc in range(16):
                        sc.op("pe", lambda e, wb=wb, kc=kc, tc=tc, pb=pb: e.matmul(bank(pb), lhsT=wb[:, kc, :], rhs=hT[:, kc, tc * 512:(tc + 1) * 512],
                                                                                start=(kc == 0), stop=(kc == 15)),
                              [wk] + hkeys(tc), ["b%d" % pb], signal=(kc == 15))
                    if j < 4:
                        dst, key = qlatT[:, j, tc * 512:(tc + 1) * 512], "qlatT"
                    elif j < 6:
                        dst, key = kvlatT[:, j - 4, tc * 512:(tc + 1) * 512], "kvlatT"
                    elif j < 14:
                        dst, key = dqT[:, j - 6, tc * 512:(tc + 1) * 512], ("dqT", j - 6, tc)
                    else:
                        dst, key = dkT[:, j - 14, tc * 512:(tc + 1) * 512], ("dkT", j - 14)
                    copy(evac_eng(), dst, bank(pb), ["b%d" % pb], [key])
            for h in range(8):
                wb = wst[nw % 2]; wk = "wst%d" % (nw % 2); nw += 1
                dma("pool", wb, w_dv[h].rearrange("p (k c) -> p k c", k=16), wk, writes=[wk])
                for tg in range(4):
                    pb = 4 + (tg % 4)
                    for i in range(4):
                        tt = tg * 4 + i
                        for kc in range(16):
                            sc.op("pe", lambda e, wb=wb, kc=kc, tt=tt, pb=pb, i=i: e.matmul(
                                bank(pb, 128, i * 128), lhsT=hT[:, kc, tt * 128:(tt + 1) * 128], rhs=wb[:, kc, :],
                                start=(kc == 0), stop=(kc == 15)), [wk, ("hT", tt)], ["b%d" % pb], signal=(kc == 15))
                    copy(evac_eng(), dvaug[:, tg * 4:(tg + 1) * 4, h, 0:128], bank(pb).rearrange("p (t c) -> p t c", t=4),
                         ["b%d" % pb], ["dvaug"])
            wkr = view(OFF_S, BF16, 16 * 64, "p (k c) -> p k c", k=16)
            dma("pool", wkr, w_kr.rearrange("p (k c) -> p k c", k=16), "wst0", writes=["wst0"])
            krt = view(OFF_S + 4096, BF16, 128)
            rtmp = view(OFF_S + 4096 + 256, F32, 128)
            for tt in range(NT):
                pb = 4 + (tt % 2)
                kp = bank(pb, 64)
                for kc in range(16):
                    sc.op("pe", lambda e, kc=kc, tt=tt, kp=kp: e.matmul(kp, lhsT=hT[:, kc, tt * 128:(tt + 1) * 128], rhs=wkr[:, kc, :],
                                                                      start=(kc == 0), stop=(kc == 15)),
                          ["wst0", ("hT", tt)], ["b%d" % pb], signal=(kc == 15))
                cs = cost[:, tt * 32:(tt + 1) * 32]; sn = sint[:, tt * 32:(tt + 1) * 32]
                sc.op("dve", lambda e, kp=kp, cs=cs: e.tensor_tensor(out=rtmp[:, 0:32], in0=kp[:, 0:32], in1=cs, op=ALU.mult), ["b%d" % pb, "cost"], ["wst1"])
                sc.op("dve", lambda e, kp=kp, sn=sn: e.tensor_tensor(out=rtmp[:, 32:64], in0=kp[:, 32:64], in1=sn, op=ALU.mult), ["b%d" % pb, "sint"], ["wst1"])
                sc.op("dve", lambda e, kp=kp, sn=sn: e.tensor_tensor(out=rtmp[:, 64:96], in0=kp[:, 0:32], in1=sn, op=ALU.mult), ["b%d" % pb, "sint"], ["wst1"])
                sc.op("dve", lambda e, kp=kp, cs=cs: e.tensor_tensor(out=rtmp[:, 96:128], in0=kp[:, 32:64], in1=cs, op=ALU.mult), ["b%d" % pb, "cost"], ["wst1"])
                sc.op("dve", lambda e: e.tensor_tensor(out=krt[:, 0:32], in0=rtmp[:, 0:32], in1=rtmp[:, 32:64], op=ALU.subtract), ["wst1"], ["wst1"])
                sc.op("dve", lambda e: e.tensor_tensor(out=krt[:, 32:64], in0=rtmp[:, 64:96], in1=rtmp[:, 96:128], op=ALU.add), ["wst1"], ["wst1"])
                sc.op("dve", lambda e: e.tensor_copy(out=krt[:, 64:128], in_=krt[:, 0:64]), ["wst1"], ["wst1"])
                tp = bankbf(6 + (tt % 2))
                sc.op("pe", lambda e, tp=tp: e.transpose(tp[:, 0:128], krt, ident[:]), ["wst1", "ident"], ["b%d" % (6 + tt % 2)])
                copy(evac_eng(), kropeT[:, tt * 128:(tt + 1) * 128], tp[:, 0:128], ["b%d" % (6 + tt % 2)], ["kropeT"])
            sc.barrier()
            if stop_after == 1:
                break

            o = OFF_A
            PT = [view(o + i * 1024, BF16, 512) for i in range(8)]; o += 8192
            wuq = view(o, BF16, 4 * 1536, "p (k c) -> p k c", k=4); o += 12288
            wukv = view(o, BF16, 2 * 2048, "p (k c) -> p k c", k=2); o += 8192
            qropeT = view(o, BF16, 4 * 2048, "p (r t) -> p r t", r=4); o += 16384
            rchunk = view(o, F32, 512)
            rcp = view(o, F32, 512); o += 2048
            sqtmp = view(OFF_S, BF16, 2048)
            bth = [view(OFF_S + 4096 + i * 1536, F32, 384, "p (d q) -> p d q", d=3) for i in range(2)]
            sqb = view(OFF_S + 4096 + 3072, BF16, 512)
            qnopeT = view(o, BF16, 2048); o += 4096
            knopeT = view(o, BF16, 2048); o += 4096
            vaug = view(o, BF16, 16 * 132, "p (t c) -> p t c", t=16); o += 4224
            qrtok = view(o, BF16, 512)
            odb4 = view(o, BF16, 512, "p (j c) -> p j c", j=4); o += 1024
            rt4 = view(o, F32, 128); o += 512
            odf = view(o, F32, 512); o += 2048
            bhi = view(o, BF16, 384, "p (d q) -> p d q", d=3); o += 768
            blo = view(o, BF16, 384, "p (d q) -> p d q", d=3); o += 768
            assert o <= 65536, o

            dma("pool", wuq, w_uq.rearrange("p (k c) -> p k c", k=4), "wuq", writes=["wuq"])
            dma("pool", wukv, w_ukv.rearrange("p (k c) -> p k c", k=2), "wukv", writes=["wukv"])
            for kc in range(4):
                sc.op("dve", lambda e, kc=kc: e.tensor_scalar(out=wuq[:, kc, :], in0=wuq[:, kc, :], scalar1=qnw[:, kc:kc + 1], scalar2=None, op0=ALU.mult),
                      ["wuq", "qnw"], ["wuq"])
            for kc in range(2):
                sc.op("dve", lambda e, kc=kc: e.tensor_scalar(out=wukv[:, kc, :], in0=wukv[:, kc, :], scalar1=kvnw[:, kc:kc + 1], scalar2=None, op0=ALU.mult),
                      ["wukv", "kvnw"], ["wukv"])

            for (lat, nk, key) in ((qlatT, 4, "qlatT"), (kvlatT, 2, "kvlatT")):
                n_lat = nk * 128
                for tc in range(4):
                    for kc in range(nk):
                        sc.op("act", lambda e, lat=lat, kc=kc, tc=tc: e.activation(out=sqtmp[:, 0:512], in_=lat[:, kc, tc * 512:(tc + 1) * 512], func=AF.Square),
                              [key], ["sqtmp"])
                        sc.op("pe", lambda e, kc=kc, nk=nk: e.matmul(bank(0), lhsT=onesb[:], rhs=sqtmp[:, 0:512], start=(kc == 0), stop=(kc == nk - 1)),
                              ["sqtmp", "onesb"], ["b0"], signal=True)
                    sc.op("dve", lambda e, n_lat=n_lat: e.tensor_scalar(out=rchunk, in0=bank(0), scalar1=1.0 / n_lat, scalar2=EPS, op0=ALU.mult, op1=ALU.add),
                          ["b0"], ["rchunk"])
                    sc.op("act", lambda e: e.activation(out=rchunk, in_=rchunk, func=AF.Ln), ["rchunk"], ["rchunk"])
                    sc.op("act", lambda e: e.activation(out=rchunk, in_=rchunk, func=AF.Exp, scale=-0.5), ["rchunk"], ["rchunk"])
                    for kc in range(nk):
                        sc.op("dve", lambda e, lat=lat, kc=kc, tc=tc: e.tensor_tensor(out=lat[:, kc, tc * 512:(tc + 1) * 512], in0=lat[:, kc, tc * 512:(tc + 1) * 512],
                                                                                   in1=rchunk, op=ALU.mult), [key, "rchunk"], [key])
            sc.barrier()

            for tt in range(NT):
                qp = bank(4 + tt % 2)
                for kc in range(4):
                    sc.op("pe", lambda e, kc=kc, tt=tt, qp=qp: e.matmul(qp, lhsT=qlatT[:, kc, tt * 128:(tt + 1) * 128], rhs=wuq[:, kc, 1024:1536],
                                                                      start=(kc == 0), stop=(kc == 3)), ["qlatT", "wuq"], ["b%d" % (4 + tt % 2)], signal=(kc == 3))
                cs = cost[:, tt * 32:(tt + 1) * 32]; sn = sint[:, tt * 32:(tt + 1) * 32]
                bk = "b%d" % (4 + tt % 2)
                qv = qp.rearrange("p (h t c) -> p h t c", h=8, t=2)
                t1 = qv[:, :, 0, :]; t2 = qv[:, :, 1, :]
                csb = bass.AP(cs.tensor, cs.offset, [list(cs.ap[0]), [0, 8], [1, 32]])
                snb = bass.AP(sn.tensor, sn.offset, [list(sn.ap[0]), [0, 8], [1, 32]])
                ta = odf[:, 0:256].rearrange("p (h c) -> p h c", h=8)
                tb = odf[:, 256:512].rearrange("p (h c) -> p h c", h=8)
                qo = qrtok.rearrange("p (h t c) -> p h t c", h=8, t=2)
                sc.op("dve", lambda e, t1=t1, csb=csb, ta=ta: e.tensor_tensor(out=ta, in0=t1, in1=csb, op=ALU.mult), [bk, "cost"], ["rta"])
                sc.op("dve", lambda e, t2=t2, snb=snb, tb=tb: e.tensor_tensor(out=tb, in0=t2, in1=snb, op=ALU.mult), [bk, "sint"], ["rtb"])
                sc.op("dve", lambda e, ta=ta, tb=tb, qo=qo: e.tensor_tensor(out=qo[:, :, 0, :], in0=ta, in1=tb, op=ALU.subtract), ["rta", "rtb"], ["qrtok"])
                sc.op("dve", lambda e, t1=t1, snb=snb, ta=ta: e.tensor_tensor(out=ta, in0=t1, in1=snb, op=ALU.mult), [bk, "sint"], ["rta"])
                sc.op("dve", lambda e, t2=t2, csb=csb, tb=tb: e.tensor_tensor(out=tb, in0=t2, in1=csb, op=ALU.mult), [bk, "cost"], ["rtb"])
                sc.op("dve", lambda e, ta=ta, tb=tb, qo=qo: e.tensor_tensor(out=qo[:, :, 1, :], in0=ta, in1=tb, op=ALU.add), ["rta", "rtb"], ["qrtok"])
                tp = bankbf(6 + tt % 2)
                for pr in range(4):
                    sc.op("pe", lambda e, tp=tp, pr=pr: e.transpose(tp[:, pr * 128:(pr + 1) * 128], qrtok[:, pr * 128:(pr + 1) * 128], ident[:]),
                          ["qrtok", "ident"], ["b%d" % (6 + tt % 2)], signal=(pr == 3))
                copy(evac_eng(), qropeT[:, :, tt * 128:(tt + 1) * 128], tp[:, 0:512].rearrange("p (r t) -> p r t", r=4),
                     ["b%d" % (6 + tt % 2)], ["qropeT"])

            sc.barrier()
            if stop_after == 12:
                break
            OT_diff = dqT
            OT_mla = dkT
            SC_B = (0, 1)

            def pvacc(j, n=132):
                return bank(2 + j, n)

            def sumsq_max(srcs, dst_col):
                for tc in range(4):
                    for i, (ap_, sel_, key_, rows) in enumerate(srcs):
                        sc.op("pe", lambda e, ap_=ap_, sel_=sel_, tc=tc, i=i, n=len(srcs): e.matmul(
                            bank(6), lhsT=sel_, rhs=ap_[:, tc * 512:(tc + 1) * 512], start=(i == 0), stop=(i == n - 1)),
                            [key_, "sel1", "sel2", "onesb"], ["b6"], signal=(i == len(srcs) - 1))
                    sc.op("dve", lambda e, tc=tc: e.tensor_reduce(out=st2[:, 32 + tc:33 + tc], in_=bank(6), axis=AX.X, op=ALU.max), ["b6"], ["st2x"])
                sc.op("dve", lambda e: e.tensor_reduce(out=st2[:, dst_col:dst_col + 1], in_=st2[:, 32:36], axis=AX.X, op=ALU.max), ["st2x"], ["st2x", "st2"])


            pending = []

            def attention_core(nmaps, kparts, qparts, vsrc, negm_cols, hbias, h, is_diff):
                for qc in range(4):
                    for mp in range(nmaps):
                        parts = kparts[mp]
                        negm = st2[:, negm_cols[mp]:negm_cols[mp] + 1]

                        def emit_score(kt, qc=qc, mp=mp, parts=parts):
                            sb_ = SC_B[kt % 2]
                            for pi, (kf, qf, keys) in enumerate(zip(parts[0], parts[1], parts[2])):
                                if len(parts) > 3:
                                    keys = keys + [parts[3] + (qc,)]
                                near = [j for j in range(4) if abs(kt - (4 * qc + j)) <= 1] if is_diff else []
                                last_score = (pi == len(parts[0]) - 1)
                                sc.op("pe", lambda e, kf=kf, qf=qf, kt=kt, qc=qc, sb_=sb_, pi=pi, last=(last_score and not near): e.matmul(
                                    bank(sb_), lhsT=kf(kt), rhs=qf(qc), start=(pi == 0), stop=last),
                                    keys, ["b%d" % sb_], signal=(last_score and not near))
                                if last_score:
                                    for ni, j in enumerate(near):
                                        dl = kt - (4 * qc + j)
                                        for bi, bsrc_ in enumerate((bhi, blo)):
                                            fin = (ni == len(near) - 1 and bi == 1)
                                            sc.op("pe", lambda e, sb_=sb_, j=j, dl=dl, bsrc_=bsrc_, fin=fin: e.matmul(
                                                bank(sb_, 128, j * 128), lhsT=ident[:], rhs=bsrc_[:, dl + 1, :], start=False, stop=fin),
                                                ["bhi", "blo", "ident"], ["b%d" % sb_], signal=fin)

                        def emit_exp(kt, qc=qc, mp=mp, negm=negm):
                            sb_ = SC_B[kt % 2]
                            slot = kt % 8
                            pt = PT[slot]
                            if not is_diff:
                                sc.op("act", lambda e, pt=pt, sb_=sb_, negm=negm: e.activation(out=pt, in_=bank(sb_), func=AF.Exp, scale=MLA_SCALE, bias=negm),
                                      ["b%d" % sb_, "st2"], [("PT", slot)])
                                return
                            cls = []
                            for j in range(4):
                                dl = kt - (4 * qc + j)
                                cls.append(16 + mp if dl > 1 else (8 + mp if dl < -1 else negm_cols[mp]))
                            j0 = 0
                            while j0 < 4:
                                j1 = j0
                                while j1 + 1 < 4 and cls[j1 + 1] == cls[j0]:
                                    j1 += 1
                                w_ = (j1 - j0 + 1) * 128
                                ps_ = bank(sb_, w_, j0 * 128)
                                po = pt[:, j0 * 128:j0 * 128 + w_]
                                bia = st2[:, cls[j0]:cls[j0] + 1]
                                sc.op("act", lambda e, po=po, ps_=ps_, bia=bia: e.activation(out=po, in_=ps_, func=AF.Exp, scale=DIFF_SCALE, bias=bia),
                                      ["b%d" % sb_, "st2"], [("PT", slot)])
                                j0 = j1 + 1

                        setid = mp if nmaps == 2 else (qc % 2)
                        bo, bs = 2 + 2 * setid, 3 + 2 * setid

                        def emit_pv(kt, bo=bo, bs=bs):
                            slot = kt % 8
                            pt = PT[slot]
                            sc.op("pe", lambda e, pt=pt, kt=kt, bo=bo: e.matmul(bank(bo), lhsT=vsrc(kt), rhs=pt, start=(kt == 0), stop=(kt == 15)),
                                  [("PT", slot), "vsrc"], ["b%d" % bo], signal=(kt == 15))
                            sc.op("pe", lambda e, pt=pt, kt=kt, bs=bs: e.matmul(bank(bs), lhsT=onesb[:], rhs=pt, start=(kt == 0), stop=(kt == 15)),
                                  [("PT", slot), "onesb"], ["b%d" % bs], signal=(kt == 15))

                        emit_score(0)
                        for kt in range(16):
                            if kt + 1 < 16:
                                emit_score(kt + 1)
                            if kt == 3:
                                while pending:
                                    pending.pop(0)()
                            emit_exp(kt)
                            emit_pv(kt)
                        if nmaps == 2 and mp == 0:
                            continue
                        sc.barrier()
                        yield qc, mp

            for h in range(1 if stop_after in (14, 15) else 8):
                for di, delta in enumerate((-1, 0, 1)):
                    src = bass.AP(bsc.tensor, h * 65536 + 255 - delta * 128, [[511, 128], [1, 128]])
                    dma("sp", bth[h % 2][:, di, :], src, "bT%d" % (h % 2), writes=[("bT", h % 2)])
                bcur = bth[h % 2]
                sc.op("dve", lambda e, bcur=bcur: e.tensor_scalar(out=bcur, in0=bcur, scalar1=1.0 / DIFF_SCALE, scalar2=None, op0=ALU.mult), [("bT", h % 2)], [("bT", h % 2)])
                sc.op("dve", lambda e, bcur=bcur: e.tensor_copy(out=bhi, in_=bcur), [("bT", h % 2)], ["bhi"])
                sc.op("dve", lambda e, bcur=bcur: e.tensor_tensor(out=bcur, in0=bcur, in1=bhi, op=ALU.subtract), [("bT", h % 2), "bhi"], [("bT", h % 2)])
                sc.op("dve", lambda e, bcur=bcur: e.tensor_copy(out=blo, in_=bcur), [("bT", h % 2)], ["blo"])
                sc.op("act", lambda e, h=h: e.activation(out=sqtmp, in_=dqT[:, h, :], func=AF.Square), [("dqT", h, q_) for q_ in range(4)], ["sqtmp"])
                sumsq_max([(sqtmp, sel1[:], "sqtmp", 128)], 0)
                sumsq_max([(sqtmp, sel2[:], "sqtmp", 128)], 1)
                sc.op("act", lambda e, h=h: e.activation(out=sqtmp, in_=dkT[:, h, :], func=AF.Square), [("dkT", h)], ["sqtmp"])
                sumsq_max([(sqtmp, sel1[:], "sqtmp", 128)], 2)
                sumsq_max([(sqtmp, sel2[:], "sqtmp", 128)], 3)
                sc.op("dve", lambda e: e.tensor_tensor(out=st2[:, 4:6], in0=st2[:, 0:2], in1=st2[:, 2:4], op=ALU.add), ["st2"], ["st2"])
                sc.op("dve", lambda e: e.tensor_scalar(out=st2[:, 4:6], in0=st2[:, 4:6], scalar1=-0.5 * DIFF_SCALE * 1.02, scalar2=None, op0=ALU.mult), ["st2"], ["st2"])
                sc.op("dve", lambda e, h=h: e.tensor_scalar(out=st2[:, 8:10], in0=st2[:, 4:6], scalar1=tabbc[:, h:h + 1], scalar2=None, op0=ALU.add), ["st2", "tabbc0"], ["st2"])
                sc.op("dve", lambda e, h=h: e.tensor_scalar(out=st2[:, 16:18], in0=st2[:, 4:6], scalar1=tabbc[:, 8 + h:9 + h], scalar2=None, op0=ALU.add), ["st2", "tabbc1"], ["st2"])
                if stop_after == 13:
                    break
                kparts = []
                for mp in range(2):
                    r0 = mp * 64
                    kparts.append(([lambda kt, r0=r0, h=h: dkT[r0:r0 + 64, h, kt * 128:(kt + 1) * 128]],
                                   [lambda qc, r0=r0, h=h: dqT[r0:r0 + 64, h, qc * 512:(qc + 1) * 512]],
                                   [[("dkT", h)]], ("dqT", h)))
                vsrc = lambda kt, h=h: dvaug[:, kt, h, 0:128]
                for qc, mp in attention_core(2, kparts, None, vsrc, (4, 5), None, h, True):
                    if stop_after == 14:
                        continue
                    sc.op("dve", lambda e: e.reciprocal(out=rcp, in_=bank(3)), ["b3"], ["rcp"])
                    sc.op("dve", lambda e: e.tensor_tensor(out=odf, in0=bank(2), in1=rcp, op=ALU.mult), ["b2", "rcp"], ["odt"])
                    sc.op("dve", lambda e: e.reciprocal(out=rcp, in_=bank(5)), ["b5", "rcp"], ["rcp"])
                    sc.op("dve", lambda e: e.tensor_tensor(out=rcp, in0=bank(4), in1=rcp, op=ALU.mult), ["b4", "rcp"], ["rcp"])
                    sc.op("dve", lambda e: e.scalar_tensor_tensor(out=odf, in0=rcp, scalar=lamt[:, 4:5], in1=odf, op0=ALU.mult, op1=ALU.add), ["rcp", "lamt", "odt"], ["odt"])
                    sc.op("dve", lambda e: e.tensor_tensor(out=sqb, in0=odf, in1=odf, op=ALU.mult), ["odt"], ["sqb"])
                    sc.op("pe", lambda e: e.matmul(bank(7), lhsT=onesb[:], rhs=sqb, start=True, stop=True), ["sqb", "onesb"], ["b7"])
                    sc.op("dve", lambda e: e.tensor_scalar(out=rcp, in0=bank(7), scalar1=1.0 / 128, scalar2=EPS, op0=ALU.mult, op1=ALU.add), ["b7", "rcp"], ["rcp"])
                    sc.op("act", lambda e: e.activation(out=rcp, in_=rcp, func=AF.Ln), ["rcp"], ["rcp"])
                    sc.op("act", lambda e: e.activation(out=rcp, in_=rcp, func=AF.Exp, scale=-0.5), ["rcp"], ["rcp"])
                    sc.op("dve", lambda e, h=h, qc=qc: e.scalar_tensor_tensor(out=OT_diff[:, h, qc * 512:(qc + 1) * 512], in0=odf, scalar=subw[:, 0:1], in1=rcp, op0=ALU.mult, op1=ALU.mult),
                          ["odt", "subw", "rcp"], [("dqT", h, qc)])

            if stop_after in (13, 14, 15, 16):
                break
            for h in range(8):
                half = (h % 2) * 64; pr = h // 2
                for tc in range(4):
                    pb = 6 + tc % 2
                    for kc in range(4):
                        sc.op("pe", lambda e, kc=kc, tc=tc, pb=pb, h=h: e.matmul(bank(pb), lhsT=wuq[:, kc, h * 128:(h + 1) * 128], rhs=qlatT[:, kc, tc * 512:(tc + 1) * 512],
                                                                              start=(kc == 0), stop=(kc == 3)), ["wuq", "qlatT"], ["b%d" % pb], signal=(kc == 3))
                    copy(evac_eng(), qnopeT[:, tc * 512:(tc + 1) * 512], bank(pb), ["b%d" % pb], ["qnopeT"])
                for tc in range(4):
                    pb = 6 + tc % 2
                    for kc in range(2):
                        sc.op("pe", lambda e, kc=kc, tc=tc, pb=pb, h=h: e.matmul(bank(pb), lhsT=wukv[:, kc, h * 128:(h + 1) * 128], rhs=kvlatT[:, kc, tc * 512:(tc + 1) * 512],
                                                                              start=(kc == 0), stop=(kc == 1)), ["wukv", "kvlatT"], ["b%d" % pb], signal=(kc == 1))
                    copy(evac_eng(), knopeT[:, tc * 512:(tc + 1) * 512], bank(pb), ["b%d" % pb], ["knopeT"])
                sc.op("dve", lambda e: e.memset(vaug[:, :, 128:132], 1.0), [], ["vsrc"])
                for tg in range(4):
                    pb = 6 + tg % 2
                    for i in range(4):
                        tt = tg * 4 + i
                        for kc in range(2):
                            sc.op("pe", lambda e, kc=kc, tt=tt, pb=pb, i=i, h=h: e.matmul(
                                bank(pb, 128, i * 128), lhsT=kvlatT[:, kc, tt * 128:(tt + 1) * 128], rhs=wukv[:, kc, 1024 + h * 128:1024 + (h + 1) * 128],
                                start=(kc == 0), stop=(kc == 1)), ["wukv", "kvlatT"], ["b%d" % pb], signal=(kc == 1))
                    copy(evac_eng(), vaug[:, tg * 4:(tg + 1) * 4, 0:128], bank(pb).rearrange("p (t c) -> p t c", t=4), ["b%d" % pb], ["vsrc"])
                qr = qropeT[half:half + 64, pr, :]
                kr = kropeT[half:half + 64, :]
                sel_h = onesb[half:half + 64, :]
                sc.op("act", lambda e: e.activation(out=sqtmp, in_=qnopeT, func=AF.Square), ["qnopeT"], ["sqtmp"])
                sumsq_max([(sqtmp, onesb[:], "sqtmp", 128)], 0)
                sc.op("act", lambda e: e.activation(out=sqtmp, in_=knopeT, func=AF.Square), ["knopeT"], ["sqtmp"])
                sumsq_max([(sqtmp, onesb[:], "sqtmp", 128)], 2)
                sc.op("act", lambda e, qr=qr, half=half: e.activation(out=sqtmp[half:half + 64, :], in_=qr, func=AF.Square), ["qropeT"], ["sqtmp"])
                sumsq_max([(sqtmp[half:half + 64, :], sel_h, "sqtmp", 64)], 1)
                sc.op("act", lambda e, kr=kr, half=half: e.activation(out=sqtmp[half:half + 64, :], in_=kr, func=AF.Square), ["kropeT"], ["sqtmp"])
                sumsq_max([(sqtmp[half:half + 64, :], sel_h, "sqtmp", 64)], 3)
                sc.op("dve", lambda e: e.tensor_tensor(out=st2[:, 4:6], in0=st2[:, 0:2], in1=st2[:, 2:4], op=ALU.add), ["st2"], ["st2"])
                sc.op("dve", lambda e: e.tensor_tensor(out=st2[:, 6:7], in0=st2[:, 4:5], in1=st2[:, 5:6], op=ALU.add), ["st2"], ["st2"])
                sc.op("dve", lambda e: e.tensor_scalar(out=st2[:, 6:7], in0=st2[:, 6:7], scalar1=-0.5 * MLA_SCALE * 1.02, scalar2=None, op0=ALU.mult), ["st2"], ["st2"])
                kparts = [([lambda kt: knopeT[:, kt * 128:(kt + 1) * 128], lambda kt, half=half: kropeT[half:half + 64, kt * 128:(kt + 1) * 128]],
                           [lambda qc: qnopeT[:, qc * 512:(qc + 1) * 512], lambda qc, half=half, pr=pr: qropeT[half:half + 64, pr, qc * 512:(qc + 1) * 512]],
                           [["knopeT", "qnopeT"], ["kropeT", "qropeT"]])]
                vsrc = lambda kt: vaug[:, kt, 0:128]
                for qc, mp in attention_core(1, kparts, None, vsrc, (6,), None, h, False):
                    bo, bs = 2 + 2 * (qc % 2), 3 + 2 * (qc % 2)
                    sc.op("dve", lambda e, bs=bs: e.reciprocal(out=rcp, in_=bank(bs)), ["b%d" % bs], ["rcp"])
                    sc.op("dve", lambda e, bo=bo, h=h, qc=qc: e.tensor_tensor(out=OT_mla[:, h, qc * 512:(qc + 1) * 512], in0=bank(bo), in1=rcp, op=ALU.mult),
                          ["b%d" % bo, "rcp"], [("dkT", h)])
            sc.barrier()
            if stop_after == 2:
                break

            wo = view(OFF_A, BF16, 16 * 2048, "p (k c) -> p k c", k=16)
            wov = w_o.rearrange("p (k c) -> p k c", k=16)
            for mc in range(16):
                dma("pool", wo[:, mc, :], wov[:, mc, :], "wo%d" % (mc % 4), writes=["wo"])
            OFF_W = OFF_P + 65536
            gw = view(OFF_W, F32, 2048)
            gtmp = view(OFF_W + 8192, F32, 2048)
            ast = [view(OFF_W + 16384 + i * 16384, BF16, 16 * 512, "p (k c) -> p k c", k=16) for i in range(2)]
            xt3 = [view(OFF_W + 49152 + i * 8192, F32, 2048) for i in range(2)]
            yt3 = view(OFF_W + 65536, F32, 2048) if OFF_W + 65536 + 8192 <= ARENA else None
            junk = view(OFF_W + 16384, BF16, 2048)
            cbk = view(OFF_W + 65536, BF16, 16 * 128, "p (k c) -> p k c", k=16)

            def build_gw(ch, postw, gw_, gtmp_, ast_, cbk_):
                gw, gtmp, ast, cbk = gw_, gtmp_, ast_, cbk_
                for kc in range(16):
                    sc.op("dve", lambda e, kc=kc: e.tensor_scalar(out=cbk[:, kc, :], in0=onesb[:], scalar1=cact[:, kc:kc + 1],
                                                                 scalar2=None, op0=ALU.mult), ["onesb", "cact"], ["cactbc"])
                dma("sp", gtmp, ada_b[:, ch * D:(ch + 1) * D].partition_broadcast(128), "gtmp", writes=["gtmp"])
                for piece in range(4):
                    st = ast[piece % 2]; sk = "ast%d" % (piece % 2)
                    dma("pool", st, ada_r[ch * 4 + piece].rearrange("p (k c) -> p k c", k=16), sk, writes=[sk])
                    for kc in range(16):
                        sc.op("pe", lambda e, st=st, kc=kc, piece=piece: e.matmul(bank(piece), lhsT=cbk[:, kc, :], rhs=st[:, kc, :], start=(kc == 0), stop=(kc == 15)),
                              [sk, "cactbc"], ["b%d" % piece], signal=(kc == 15))
                    sc.op("dve", lambda e, piece=piece: e.tensor_tensor(out=gw[:, piece * 512:(piece + 1) * 512], in0=bank(piece), in1=gtmp[:, piece * 512:(piece + 1) * 512], op=ALU.add),
                          ["b%d" % piece, "gtmp"], ["gw"])
                dma("sp", gtmp, postw.partition_broadcast(128), "gtmp", reads=["gw"], writes=["gtmp"])
                sc.op("dve", lambda e: e.tensor_tensor(out=gw, in0=gw, in1=gtmp, op=ALU.mult), ["gw", "gtmp"], ["gw"])
                sc.barrier()

            build_gw(2, apost, gw, gtmp, ast, cbk)
            x1v = x1s.rearrange("(t p) d -> t p d", p=128)
            outv = out[lb * S:(lb + 1) * S, :].rearrange("(t p) d -> t p d", p=128)

            def post_norm_residual(tt, ysrc, ykeys, xin, xkey, dst_dram, dkey, bufs):
                gw, gtmp, junk = bufs
                for dch in range(4):
                    sc.op("act", lambda e, dch=dch: e.activation(out=junk[:, dch * 512:(dch + 1) * 512], in_=ysrc[dch], func=AF.Square, accum_out=stat[:, 32 + dch:33 + dch]),
                          [ykeys[dch]], ["junk", "stat_y"])
                sc.op("dve", lambda e: e.tensor_reduce(out=stat[:, 36:37], in_=stat[:, 32:36], axis=AX.X, op=ALU.add), ["stat_y"], ["stat_y2"])
                sc.op("dve", lambda e: e.tensor_scalar(out=stat[:, 36:37], in0=stat[:, 36:37], scalar1=1.0 / D, scalar2=EPS, op0=ALU.mult, op1=ALU.add), ["stat_y2"], ["stat_y2"])
                sc.op("act", lambda e: e.activation(out=stat[:, 36:37], in_=stat[:, 36:37], func=AF.Ln), ["stat_y2"], ["stat_y2"])
                sc.op("act", lambda e: e.activation(out=stat[:, 36:37], in_=stat[:, 36:37], func=AF.Exp, scale=-0.5), ["stat_y2"], ["stat_y2"])
                for dch in range(4):
                    sl = slice(dch * 512, (dch + 1) * 512)
                    sc.op("dve", lambda e, dch=dch, sl=sl: e.scalar_tensor_tensor(out=gtmp[:, sl], in0=ysrc[dch], scalar=stat[:, 36:37], in1=gw[:, sl], op0=ALU.mult, op1=ALU.mult),
                          [ykeys[dch], "stat_y2", "gw"], ["gtmp"])
                    sc.op("dve", lambda e, sl=sl: e.tensor_tensor(out=xin[:, sl], in0=xin[:, sl], in1=gtmp[:, sl], op=ALU.add), ["gtmp", xkey], [xkey])
                dma("sp", dst_dram, xin, "st_" + xkey, reads=[xkey], writes=[dkey])

            for tt in range(NT):
                b_ = tt % 2
                dma("sp", xt3[b_], xv[tt], "xt3%d" % b_, writes=["xt3%d" % b_])
                for dch in range(4):
                    for mc in range(16):
                        src = OT_mla[:, mc, tt * 128:(tt + 1) * 128] if mc < 8 else OT_diff[:, mc - 8, tt * 128:(tt + 1) * 128]
                        sc.op("pe", lambda e, src=src, mc=mc, dch=dch: e.matmul(bank(dch), lhsT=src, rhs=wo[:, mc, dch * 512:(dch + 1) * 512], start=(mc == 0), stop=(mc == 15)),
                              ["wo"], ["b%d" % dch], signal=(mc == 15))
                post_norm_residual(tt, [bank(d_) for d_ in range(4)], ["b%d" % d_ for d_ in range(4)], xt3[b_], "xt3%d" % b_, x1v[tt], ("x1s", tt), (gw, gtmp, junk))
            sc.barrier()
            if stop_after == 3:
                break

            gT = view(0, BF16, NFB * 1024, "p (f t) -> p f t", f=NFB)
            h2T = view(90112, BF16, 16 * 1024, "p (k t) -> p k t", k=16)
            wupb = [view(122880 + i * 8192, BF16, 16 * 256, "p (k c) -> p k c", k=16) for i in range(2)]
            tcv = [view(139264 + i * 4096, F32, 1024) for i in range(2)]
            wdnb = [view(147456 + i * 11264, BF16, NFB * 128, "p (f c) -> p f c", f=NFB) for i in range(2)]
            xt4 = [view(169984 + i * 8192, F32, 2048) for i in range(2)]
            xn4 = view(186368, BF16, 2048)
            gw = view(190464, F32, 2048)
            gtmp = view(139264, F32, 2048)
            ast = [view(0 + i * 16384, BF16, 16 * 512, "p (k c) -> p k c", k=16) for i in range(2)]
            junk = xn4
            assert 202752 <= ARENA
            cbk = view(198656, BF16, 16 * 128, "p (k c) -> p k c", k=16)
            build_gw(5, fpost, gw, gtmp, ast, cbk)
            cwv = cw[:].rearrange("p (b t) -> p b t", t=3)
            sc.op("dve", lambda e: e.memset(halo[:], 0.0), [], ["halo"])

            for H in range(2):
                tiles = list(range(8 * H, 8 * H + 8))
                halo_tiles = ([8 * H - 1] if H == 1 else []) + ([8 * H + 8] if H == 0 else [])
                for tt in halo_tiles + tiles:
                    b_ = tt % 2
                    dma("sp", xt4[b_], x1v[tt], "xt4%d" % b_, reads=[("x1s", tt)], writes=["xt4%d" % b_])
                    sc.op("act", lambda e, b_=b_: e.activation(out=xn4, in_=xt4[b_], func=AF.Square, accum_out=stat[:, 40:41]), ["xt4%d" % b_], ["xn4", "stat_f"])
                    sc.op("dve", lambda e: e.tensor_scalar(out=stat[:, 40:41], in0=stat[:, 40:41], scalar1=D * EPS, scalar2=None, op0=ALU.add), ["stat_f"], ["stat_f"])
                    sc.op("act", lambda e: e.activation(out=stat[:, 40:41], in_=stat[:, 40:41], func=AF.Ln), ["stat_f"], ["stat_f"])
                    sc.op("act", lambda e: e.activation(out=stat[:, 40:41], in_=stat[:, 40:41], func=AF.Exp, scale=-0.5), ["stat_f"], ["stat_f"])
                    sc.op("dve", lambda e, b_=b_: e.tensor_scalar(out=xn4, in0=xt4[b_], scalar1=stat[:, 40:41], scalar2=None, op0=ALU.mult), ["xt4%d" % b_, "stat_f"], ["xn4"])
                    for g in range(2):
                        pb = 6 + g
                        pv = bankbf(pb)
                        for j in range(8):
                            kc = g * 8 + j
                            sc.op("pe", lambda e, pv=pv, j=j, kc=kc: e.transpose(pv[:, j * 128:(j + 1) * 128], xn4[:, kc * 128:(kc + 1) * 128], ident[:]),
                                  ["xn4", "ident"], ["b%d" % pb], signal=(j == 7))
                        for j in range(8):
                            kc = g * 8 + j
                            if tt in tiles:
                                lt = tt - 8 * H
                                dst = h2T[:, kc, lt * 128:(lt + 1) * 128]; src = pv[:, j * 128:(j + 1) * 128]; wk = "h2T"
                            elif tt < 8 * H:
                                dst = halo[:, kc * 2:kc * 2 + 1]; src = pv[:, j * 128 + 127:j * 128 + 128]; wk = "halo"
                            else:
                                dst = halo[:, kc * 2 + 1:kc * 2 + 2]; src = pv[:, j * 128:j * 128 + 1]; wk = "halo"
                            sc.op("dve", lambda e, dst=dst, src=src, kc=kc: e.tensor_scalar(out=dst, in0=src, scalar1=a_ffn[:, kc:kc + 1], scalar2=msc[:, 32 + kc:33 + kc],
                                                                                          op0=ALU.mult, op1=ALU.add), ["b%d" % pb, "a_ffn", "msc"], [wk])
                if H == 1:
                    sc.op("dve", lambda e: e.memset(halo[:].rearrange("p (k c) -> p k c", c=2)[:, :, 1:2], 0.0), [], ["halo"])
                hv = halo[:].rearrange("p (k c) -> p k c", c=2)
                for fb in range(NFB):
                    wb = wupb[fb % 2]; wk = "wup%d" % (fb % 2)
                    dma("pool", wb, w_up[fb].rearrange("p (k c) -> p k c", k=16), wk, writes=[wk])
                    for which in range(2):
                        blk = which * NFB + fb
                        up = psb[which]
                        uks = ["b%d" % (2 * which), "b%d" % (2 * which + 1)]
                        hk = "b%d" % (4 + which)
                        for tc in range(2):
                            for kc in range(16):
                                sc.op("pe", lambda e, wb=wb, kc=kc, tc=tc, up=up, which=which: e.matmul(
                                    up[:, tc * 512:(tc + 1) * 512], lhsT=wb[:, kc, which * 128:(which + 1) * 128], rhs=h2T[:, kc, tc * 512:(tc + 1) * 512],
                                    start=(kc == 0), stop=(kc == 15)), [wk, "h2T"], [uks[tc]], signal=(kc == 15))
                        hp = bank(4 + which, 2)
                        for kc in range(16):
                            sc.op("pe", lambda e, wb=wb, kc=kc, hp=hp, which=which: e.matmul(hp, lhsT=wb[:, kc, which * 128:(which + 1) * 128], rhs=hv[:, kc, :],
                                                                                          start=(kc == 0), stop=(kc == 15)), [wk, "halo"], [hk], signal=(kc == 15))
                        t_ = tcv[which]; tk = "tcv%d" % which
                        sc.op("act", lambda e, t_=t_, up=up, blk=blk: e.activation(out=t_, in_=up[:, :], func=AF.Identity, scale=cwv[:, blk, 1:2], bias=cb[:, blk:blk + 1]),
                              uks + ["cw", "cb"], [tk])
                        sc.op("dve", lambda e, t_=t_, up=up, blk=blk: e.scalar_tensor_tensor(out=t_[:, 1:1024], in0=up[:, 0:1023], scalar=cwv[:, blk, 0:1], in1=t_[:, 1:1024],
                                                                                           op0=ALU.mult, op1=ALU.add), uks + ["cw", tk], [tk])
                        sc.op("dve", lambda e, t_=t_, up=up, blk=blk: e.scalar_tensor_tensor(out=t_[:, 0:1023], in0=up[:, 1:1024], scalar=cwv[:, blk, 2:3], in1=t_[:, 0:1023],
                                                                                           op0=ALU.mult, op1=ALU.add), uks + ["cw", tk], [tk])
                        sc.op("dve", lambda e, t_=t_, hp=hp, blk=blk: e.scalar_tensor_tensor(out=t_[:, 0:1], in0=hp[:, 0:1], scalar=cwv[:, blk, 0:1], in1=t_[:, 0:1],
                                                                                           op0=ALU.mult, op1=ALU.add), [hk, "cw", tk], [tk])
                        sc.op("dve", lambda e, t_=t_, hp=hp, blk=blk: e.scalar_tensor_tensor(out=t_[:, 1023:1024], in0=hp[:, 1:2], scalar=cwv[:, blk, 2:3], in1=t_[:, 1023:1024],
                                                                                           op0=ALU.mult, op1=ALU.add), [hk, "cw", tk], [tk])
                    sc.op("act", lambda e: e.activation(out=tcv[0], in_=tcv[0], func=AF.Gelu_apprx_tanh), ["tcv0"], ["tcv0"])
                    sc.op("dve", lambda e, fb=fb: e.tensor_tensor(out=gT[:, fb, :], in0=tcv[0], in1=tcv[1], op=ALU.mult), ["tcv0", "tcv1"], ["gT"])
                yT = h2T
                for db in range(16):
                    wb = wdnb[db % 2]; wk = "wdn%d" % (db % 2)
                    dma("pool", wb, w_dn[db].rearrange("p (f c) -> p f c", f=NFB), wk, writes=[wk])
                    for tc in range(2):
                        pb = (db * 2 + tc) % 4
                        for fb in range(NFB):
                            sc.op("pe", lambda e, wb=wb, fb=fb, tc=tc, pb=pb: e.matmul(bank(pb), lhsT=wb[:, fb, :], rhs=gT[:, fb, tc * 512:(tc + 1) * 512],
                                                                                    start=(fb == 0), stop=(fb == NFB - 1)), [wk, "gT"], ["b%d" % pb], signal=(fb == NFB - 1))
                        copy(evac_eng(), yT[:, db, tc * 512:(tc + 1) * 512], bank(pb), ["b%d" % pb], ["h2T"])
                for lt in range(8):
                    tt = 8 * H + lt
                    b_ = tt % 2
                    dma("sp", xt4[b_], x1v[tt], "xt4%d" % b_, reads=[("x1s", tt)], writes=["xt4%d" % b_])
                    ysrc = []; ykeys = []
                    for g in range(2):
                        pb = 4 + g * 2
                        pb = 4 + g
                        pv = bankbf(pb)
                        for j in range(8):
                            db = g * 8 + j
                            sc.op("pe", lambda e, pv=pv, j=j, db=db, lt=lt: e.transpose(pv[:, j * 128:(j + 1) * 128], yT[:, db, lt * 128:(lt + 1) * 128], ident[:]),
                                  ["h2T", "ident"], ["b%d" % pb], signal=(j == 7))
                        ysrc += [pv[:, 0:512], pv[:, 512:1024]]
                        ykeys += ["b%d" % pb, "b%d" % pb]
                    post_norm_residual(tt, ysrc, ykeys, xt4[b_], "xt4%d" % b_, outv[tt], ("out", tt), (gw, gtmp, junk))
                sc.barrier()

        sc.finish()
        stuck, val = sc.simulate()
        if stuck:
            raise RuntimeError("semaphore dead-lock in recorded program: %r  sems=%r" % (stuck, {k: v for k, v in val.items() if not k.startswith("d:")}))
        build.sem_counts = {k: v for k, v in val.items()}
        sc.replay(E(nc.Block()))
    return nc


_CACHE = {}


def _prep_shared(inp):
    f = lambda a: np.ascontiguousarray(a, dtype=np.float32)
    w_in = inp["w_in"][0]
    fm_cols = list(range(0, 768)) + list(range(832, 2880))
    w_fm = w_in[:, fm_cols].reshape(16, 128, 22, 128).transpose(2, 1, 0, 3).reshape(22, 128, 16 * 128)
    w_kr = w_in[:, 768:832].reshape(16, 128, 64).transpose(1, 0, 2).reshape(128, 16 * 64)
    w_dv = w_in[:, 2880:3904].reshape(16, 128, 8, 128).transpose(2, 1, 0, 3).reshape(8, 128, 16 * 128)
    wuq = inp["w_uq"][0].reshape(512, 8, 192)
    wuq_r = np.concatenate([wuq[:, :, :128].reshape(512, 1024), wuq[:, :, 128:].reshape(512, 512)], axis=1)
    wuq_r = wuq_r.reshape(4, 128, 1536).transpose(1, 0, 2).reshape(128, 4 * 1536)
    wukv = inp["w_ukv"][0].reshape(256, 8, 256)
    wukv_r = np.concatenate([wukv[:, :, :128].reshape(256, 1024), wukv[:, :, 128:].reshape(256, 1024)], axis=1)
    wukv_r = wukv_r.reshape(2, 128, 2048).transpose(1, 0, 2).reshape(128, 2 * 2048)
    w_o = inp["w_o"][0].reshape(16, 128, 2048).transpose(1, 0, 2).reshape(128, 16 * 2048)
    wup = inp["w_up"][0].reshape(16, 128, 2, NFB, 128).transpose(3, 1, 0, 2, 4).reshape(NFB, 128, 16 * 256)
    cw = inp["conv_w"][0].reshape(3, 88, 128).transpose(2, 1, 0).reshape(128, 88 * 3)
    cb = inp["conv_b"][0].reshape(88, 128).T
    wdn = inp["w_down"][0].reshape(NFB, 128, 16, 128).transpose(2, 1, 0, 3).reshape(16, 128, NFB * 128)
    inv = (1.0 / (np.float32(ROPE_THETA) ** (np.arange(0, 64, 2, dtype=np.float32) / np.float32(64)))).astype(np.float32)
    selh = np.zeros((8, 8, 128), np.float32)
    for h in range(8):
        selh[h, h, :] = 1.0
    d = {
        "relb": f(inp["rel_bias"]), "relbT": f(inp["rel_bias"].T), "ident": np.eye(128, dtype=np.float32),
        "nvals": f(np.broadcast_to(255.0 - np.arange(512, dtype=np.float32), (8, 512))),
        "selh": f(selh.reshape(8, 1024)), "invf": f(np.broadcast_to(inv, (128, 32))),
        "ada_r": f(inp["ada_w"][0].reshape(16, 128, 24, 512).transpose(2, 1, 0, 3).reshape(24, 128, 16 * 512)), "ada_bT": f(inp["ada_b"][0].reshape(96, 128).T), "ada_b": f(inp["ada_b"]),
        "apnT": f(inp["attn_pre_norm"][0].reshape(16, 128).T), "fpnT": f(inp["ffn_pre_norm"][0].reshape(16, 128).T),
        "apost": f(inp["attn_post_norm"]), "fpost": f(inp["ffn_post_norm"]),
        "w_fm": f(w_fm), "w_kr": f(w_kr), "w_dv": f(w_dv),
        "qnT": f(inp["q_norm"][0].reshape(4, 128).T), "kvnT": f(inp["kv_norm"][0].reshape(2, 128).T),
        "w_uq": f(wuq_r), "w_ukv": f(wukv_r),
        "lamv": f(np.concatenate([inp["lambda_q1"][0], inp["lambda_k1"][0], inp["lambda_q2"][0], inp["lambda_k2"][0]])[None, :]),
        "sublnT": f(inp["diff_subln"][0][:, None]),
        "w_o": f(w_o), "w_up": f(wup), "cwT": f(cw), "cbT": f(cb), "w_dn": f(wdn),
    }
    return d


N_CORES = 8
NB = 8 // N_CORES


def _core_inputs(inp, shared, batches):
    m = dict(shared)
    m["x"] = np.ascontiguousarray(np.concatenate([inp["x"][b] for b in batches], axis=0), dtype=np.float32)
    m["cT"] = np.ascontiguousarray(np.concatenate([inp["c"][b].reshape(16, 128).T for b in batches], axis=1), dtype=np.float32)
    m["pos"] = np.ascontiguousarray(np.concatenate([inp["positions"][b].reshape(16, 128).T for b in batches], axis=1), dtype=np.int32)
    return m


def kernel(**inp):
    inp = {k: np.asarray(v) for k, v in inp.items()}
    if "nc" not in _CACHE:
        _CACHE["nc"] = build(nb=NB)
    nc = _CACHE["nc"]
    shared = _prep_shared(inp)
    in_maps = [_core_inputs(inp, shared, list(range(i * NB, (i + 1) * NB))) for i in range(N_CORES)]
    res = run_bass_kernel_spmd(nc, in_maps, core_ids=list(range(N_CORES)))
    outs = [np.asarray(r["out"], dtype=np.float32).reshape(NB, S, D) for r in res.results]
    return np.concatenate(outs, axis=0)
```
